# Optimizing a Trainium2 kernel written in Bass

```python
import jax, jax.numpy as jnp
from jax import lax
import numpy as np

D_MODEL = 1024
BATCH = 16
SEQ = 256
DEPTH = 4
DEC_BATCH = 2
DEC_SEQ = 2048
PAST_LEN = 256

GRID_W = 64
N_MIXERS = 3
N_GLA = (DEPTH + 2) // 3
N_CONF = (DEPTH + 1) // 3
N_SCONV = DEPTH // 3
GLA_HEADS = 4
GLA_KEY_WIDTH = D_MODEL // 2
GLA_DK = GLA_KEY_WIDTH // GLA_HEADS
GLA_DV = D_MODEL // GLA_HEADS
GLA_RANK = 16
GLA_GATE_NORM = 16.0
GLA_CHUNK = 64
CONF_WIDTH = 31
SCONV_WIDTH = 3
D_FF = 4 * D_MODEL
N_MOD = 6
LN_EPS = 1e-5
RMS_EPS = 1e-6
ALPHA = (2 * DEPTH) ** 0.25
BETA = (8 * DEPTH) ** -0.25

kernel_name = 'hybrid_gla_conformer_shortconv_diffusion_step'


def layer_norm(x, g, b):
    xf = x.astype(jnp.float32)
    mu = jnp.mean(xf, axis=-1, keepdims=True)
    var = jnp.mean(jnp.square(xf - mu), axis=-1, keepdims=True)
    return ((xf - mu) * lax.rsqrt(var + LN_EPS)).astype(x.dtype) * g + b


def adaln(cvec, w, b):
    m = jax.nn.silu(cvec) @ w + b
    m = m.reshape(m.shape[:-1] + (1, N_MOD, D_MODEL))
    return [m[..., i, :] for i in range(N_MOD)]


def modulate(x, shift, scale):
    return x * (1 + scale) + shift


def dw_conv(x, w, dilation=1):
    pad = (w.shape[0] // 2) * dilation
    return lax.conv_general_dilated(x, w[:, None, :], window_strides=(1,), padding=[(pad, pad)],
                                    rhs_dilation=(dilation,), dimension_numbers=('NWC', 'WIO', 'NWC'),
                                    feature_group_count=x.shape[-1])


def gla_scan(q, k, v, g, s0):
    B, L, H, DK = q.shape
    n = L // GLA_CHUNK
    rs = lambda t: t.reshape(B, n, GLA_CHUNK, H, t.shape[-1])
    q, k, v, g = rs(q), rs(k), rs(v), rs(g)
    bcum = jnp.cumsum(g, axis=2)
    blast = bcum[:, :, -1:]
    q_dec = q * jnp.exp(bcum)
    k_dec = k * jnp.exp(-bcum)
    k_end = k * jnp.exp(blast - bcum)
    causal_in_chunk = jnp.tril(jnp.ones((GLA_CHUNK, GLA_CHUNK), dtype=bool))
    scores = jnp.einsum('bnihd,bnjhd->bnhij', q_dec, k_dec)
    scores = jnp.where(causal_in_chunk, scores, 0.0)
    o_intra = jnp.einsum('bnhij,bnjhe->bnihe', scores, v)
    chunk_kv = jnp.einsum('bnjhd,bnjhe->bnhde', k_end, v)
    chunk_decay = jnp.exp(blast[:, :, 0])

    def step(s, inp):
        dec, kv = inp
        return dec[..., None] * s + kv, s

    s_final, s_starts = lax.scan(step, s0, (jnp.moveaxis(chunk_decay, 1, 0), jnp.moveaxis(chunk_kv, 1, 0)))
    s_starts = jnp.moveaxis(s_starts, 0, 1)
    o_inter = jnp.einsum('bnihd,bnhde->bnihe', q_dec, s_starts)
    return (o_intra + o_inter).reshape(B, L, H, v.shape[-1]), s_final


def gla_mixer(h, w_in, w_ga, w_gb, b_g, gn_g, w_o, s0):
    B, L, _ = h.shape
    f32 = jnp.float32
    q, k, v, r = jnp.split(h @ w_in, [GLA_KEY_WIDTH, 2 * GLA_KEY_WIDTH, 2 * GLA_KEY_WIDTH + D_MODEL], axis=-1)
    q = q.astype(f32).reshape(B, L, GLA_HEADS, GLA_DK) * (GLA_DK ** -0.5)
    k = k.astype(f32).reshape(B, L, GLA_HEADS, GLA_DK)
    v = v.astype(f32).reshape(B, L, GLA_HEADS, GLA_DV)
    z = jnp.einsum('bld,zdr->zblr', h, w_ga)
    z = jnp.einsum('zblr,zrk->zblk', z, w_gb) + b_g[:, None, None, :]
    g = (jax.nn.log_sigmoid(z.astype(f32)) / GLA_GATE_NORM).reshape(2, B, L, GLA_HEADS, GLA_DK)
    s0 = s0.astype(f32)
    o_f, s_f = gla_scan(q, k, v, g[0], s0[:, 0])
    fl = lambda t: jnp.flip(t, axis=1)
    o_b, s_b = gla_scan(fl(q), fl(k), fl(v), fl(g[1]), s0[:, 1])
    o = o_f + fl(o_b)
    o = o * lax.rsqrt(jnp.mean(jnp.square(o), axis=-1, keepdims=True) + RMS_EPS)
    o = o.reshape(B, L, D_MODEL).astype(h.dtype) * gn_g
    return (o * jax.nn.silu(r)) @ w_o, jnp.stack([s_f, s_b], axis=1).astype(h.dtype)


def conformer_conv(h, w_pw1, b_pw1, w_dw, b_dw, ln_g, ln_b, w_pw2, b_pw2, grid):
    B, L, D = h.shape
    a, gt = jnp.split(h @ w_pw1 + b_pw1, 2, axis=-1)
    u = a * jax.nn.sigmoid(gt)
    if grid:
        rows = L // GRID_W
        u = dw_conv(u.reshape(B * rows, GRID_W, D), w_dw).reshape(B, L, D)
    else:
        u = dw_conv(u, w_dw)
    u = jax.nn.silu(layer_norm(u + b_dw, ln_g, ln_b))
    return u @ w_pw2 + b_pw2


def short_conv(h, w_in, w_conv, w_out, grid):
    bg, cg, u = jnp.split(h @ w_in, 3, axis=-1)
    u = dw_conv(cg * u, w_conv, GRID_W if grid else 1)
    return (bg * u) @ w_out


def sqrelu_mlp(h, w1, w2):
    return jnp.square(jax.nn.relu(h @ w1)) @ w2


def setup_inputs(seed: int = 0) -> dict:
    key = jax.random.key(seed)
    ks = jax.random.split(key, 32)
    nrm = lambda k, shape, s: jax.random.normal(k, shape, jnp.float32) * s
    D = D_MODEL
    gla_in_width = 2 * GLA_KEY_WIDTH + 2 * D
    return {
        'x_prompt': nrm(ks[0], (BATCH, SEQ, D), 1.0),
        'x_sample': nrm(ks[1], (DEC_BATCH, DEC_SEQ, D), 1.0),
        'c': nrm(ks[2], (DEC_BATCH, D), 1.0),
        'state_gla': nrm(ks[3], (DEC_BATCH, N_GLA, 2, GLA_HEADS, GLA_DK, GLA_DV), 1.0),
        'c_ctx': nrm(ks[4], (D,), 1.0),
        'mod_w': nrm(ks[5], (DEPTH, D, N_MOD * D), D ** -0.5),
        'mod_b': nrm(ks[6], (DEPTH, N_MOD * D), 0.02),
        'ln_g': 1.0 + nrm(ks[7], (DEPTH, 2, D), 0.02),
        'ln_b': nrm(ks[8], (DEPTH, 2, D), 0.02),
        'ff_w1': nrm(ks[9], (DEPTH, D, D_FF), D ** -0.5),
        'ff_w2': nrm(ks[10], (DEPTH, D_FF, D), D_FF ** -0.5 * BETA),
        'gla_w_in': nrm(ks[11], (N_GLA, D, gla_in_width), D ** -0.5),
        'gla_w_ga': nrm(ks[12], (N_GLA, 2, D, GLA_RANK), D ** -0.5),
        'gla_w_gb': nrm(ks[13], (N_GLA, 2, GLA_RANK, GLA_KEY_WIDTH), GLA_RANK ** -0.5),
        'gla_b_g': nrm(ks[14], (N_GLA, 2, GLA_KEY_WIDTH), 0.02),
        'gla_gn_g': 1.0 + nrm(ks[15], (N_GLA, D), 0.02),
        'gla_w_o': nrm(ks[16], (N_GLA, D, D), D ** -0.5 * BETA),
        'conf_w_pw1': nrm(ks[17], (N_CONF, D, 2 * D), D ** -0.5),
        'conf_b_pw1': nrm(ks[18], (N_CONF, 2 * D), 0.02),
        'conf_w_dw': nrm(ks[19], (N_CONF, CONF_WIDTH, D), CONF_WIDTH ** -0.5),
        'conf_b_dw': nrm(ks[20], (N_CONF, D), 0.02),
        'conf_ln_g': 1.0 + nrm(ks[21], (N_CONF, D), 0.02),
        'conf_ln_b': nrm(ks[22], (N_CONF, D), 0.02),
        'conf_w_pw2': nrm(ks[23], (N_CONF, D, D), D ** -0.5 * BETA),
        'conf_b_pw2': nrm(ks[24], (N_CONF, D), 0.02),
        'sc_w_in': nrm(ks[25], (N_SCONV, D, 3 * D), D ** -0.5),
        'sc_w_conv': nrm(ks[26], (N_SCONV, SCONV_WIDTH, D), SCONV_WIDTH ** -0.5),
        'sc_w_out': nrm(ks[27], (N_SCONV, D, D), D ** -0.5 * BETA),
    }


def reference(x_prompt, x_sample, c, state_gla, c_ctx, mod_w, mod_b, ln_g, ln_b, ff_w1, ff_w2,
              gla_w_in, gla_w_ga, gla_w_gb, gla_b_g, gla_gn_g, gla_w_o,
              conf_w_pw1, conf_b_pw1, conf_w_dw, conf_b_dw, conf_ln_g, conf_ln_b, conf_w_pw2, conf_b_pw2,
              sc_w_in, sc_w_conv, sc_w_out):
    xp, xs = x_prompt, x_sample
    new_states = []
    for l in range(DEPTH):
        kind, j = l % N_MIXERS, l // N_MIXERS
        mp = adaln(c_ctx, mod_w[l], mod_b[l])
        ms = adaln(c, mod_w[l], mod_b[l])
        hp = modulate(xp, mp[0], mp[1])
        hs = modulate(xs, ms[0], ms[1])
        if kind == 0:
            s_zero = jnp.zeros((xp.shape[0], 2, GLA_HEADS, GLA_DK, GLA_DV), xp.dtype)
            op, st = gla_mixer(hp, gla_w_in[j], gla_w_ga[j], gla_w_gb[j], gla_b_g[j], gla_gn_g[j], gla_w_o[j], s_zero)
            os_, _ = gla_mixer(hs, gla_w_in[j], gla_w_ga[j], gla_w_gb[j], gla_b_g[j], gla_gn_g[j], gla_w_o[j], state_gla[:, j])
            new_states.append(st)
        elif kind == 1:
            op = conformer_conv(hp, conf_w_pw1[j], conf_b_pw1[j], conf_w_dw[j], conf_b_dw[j], conf_ln_g[j], conf_ln_b[j], conf_w_pw2[j], conf_b_pw2[j], False)
            os_ = conformer_conv(hs, conf_w_pw1[j], conf_b_pw1[j], conf_w_dw[j], conf_b_dw[j], conf_ln_g[j], conf_ln_b[j], conf_w_pw2[j], conf_b_pw2[j], True)
        else:
            op = short_conv(hp, sc_w_in[j], sc_w_conv[j], sc_w_out[j], False)
            os_ = short_conv(hs, sc_w_in[j], sc_w_conv[j], sc_w_out[j], True)
        xp = layer_norm(ALPHA * xp + mp[2] * op, ln_g[l, 0], ln_b[l, 0])
        xs = layer_norm(ALPHA * xs + ms[2] * os_, ln_g[l, 0], ln_b[l, 0])
        fp = sqrelu_mlp(modulate(xp, mp[3], mp[4]), ff_w1[l], ff_w2[l])
        fs = sqrelu_mlp(modulate(xs, ms[3], ms[4]), ff_w1[l], ff_w2[l])
        xp = layer_norm(ALPHA * xp + mp[5] * fp, ln_g[l, 1], ln_b[l, 1])
        xs = layer_norm(ALPHA * xs + ms[5] * fs, ln_g[l, 1], ln_b[l, 1])
    new_state_gla = jnp.stack(new_states, axis=1)
    return (xp, xs, new_state_gla)
```

```python
import contextlib
import numpy as np
import ml_dtypes
import concourse.bass as bass
import concourse.mybir as mybir
from concourse.bass_utils import run_bass_kernel_spmd

F32 = mybir.dt.float32
BF16 = mybir.dt.bfloat16
ALU = mybir.AluOpType
AF = mybir.ActivationFunctionType

D = 1024
NT = 8
DEPTH = 4
ALPHA = (2 * DEPTH) ** 0.25
LN_EPS = 1e-5
RMS_EPS = 1e-6
DK = 128
DV = 256
NH = 4
CONF_W = 31
SAME_ENGINE_SYNC = True


class Op:
    __slots__ = ("eng", "fn", "deps", "is_dma", "chan", "signal", "ev", "idx", "inc", "epoch")

    def __init__(self, eng, fn, is_dma=False, chan=None, inc=16):
        self.inc = inc
        self.eng = eng
        self.fn = fn
        self.deps = []
        self.is_dma = is_dma
        self.chan = chan
        self.signal = False
        self.ev = None
        self.idx = None


class Sched:
    ENGS = ("pe", "act", "dve", "pool", "sp")

    def __init__(self):
        self.ops = []
        self.last_writer = {}
        self.readers = {}
        self.chan_count = {}
        self.epoch = 0

    def op(self, eng, fn, reads=(), writes=(), is_dma=False, chan=None, inc=16):
        o = Op(eng, fn, is_dma, chan, inc)
        o.idx = len(self.ops)
        o.epoch = self.epoch
        deps = {}
        for k in reads:
            w = self.last_writer.get(k)
            if w is not None:
                deps[w.idx] = w
        for k in writes:
            w = self.last_writer.get(k)
            if w is not None:
                deps[w.idx] = w
            for r in self.readers.get(k, ()):
                deps[r.idx] = r
        deps.pop(o.idx, None)
        o.deps = list(deps.values())
        for k in writes:
            self.last_writer[k] = o
            self.readers[k] = []
        for k in reads:
            self.readers.setdefault(k, []).append(o)
        if is_dma:
            c = self.chan_count.get(chan, 0) + inc
            self.chan_count[chan] = c
            o.ev = (("dma", chan), c)
        self.ops.append(o)
        return o

    def barrier(self, main_fn):
        last = {}
        skip = ()
        for o in self.ops:
            if o.is_dma and ((isinstance(o.chan, tuple) and o.chan[0] in skip) or (isinstance(o.chan, str) and o.chan.startswith("misc"))):
                continue
            last[("dma", o.chan) if o.is_dma else ("eng", o.eng)] = o
        B = self.op("dve", main_fn, [], [("barrier",)])
        have = {d.idx for d in B.deps}
        for o in last.values():
            if o.idx not in have and o is not B:
                B.deps.append(o)
        for eng in ("pe", "act", "pool", "sp"):
            self.op(eng, lambda e: e.nop(nofuse=True), [("barrier",)], [])

    def finalize(self):
        for o in self.ops:
            for d in o.deps:
                if d.is_dma:
                    continue
                if d.eng == o.eng and not o.is_dma:
                    if d.eng == "pe" or not SAME_ENGINE_SYNC:
                        continue
                d.signal = True
        cnt = {}
        for o in self.ops:
            if o.is_dma:
                continue
            if o.signal:
                k = ("eng", o.eng, o.epoch)
                cnt[k] = cnt.get(k, 0) + 1
                o.ev = (k, cnt[k])
        return cnt

    def emit(self, eng_name, eng, sems):
        known = {}
        for o in self.ops:
            if o.eng != eng_name:
                continue
            for d in o.deps:
                if d.ev is None:
                    continue
                if (not d.is_dma) and d.eng == o.eng and not o.is_dma:
                    if d.eng == "pe" or not SAME_ENGINE_SYNC:
                        continue
                key, val = d.ev
                if known.get(key, 0) >= val:
                    continue
                eng.wait_ge(sems[key], val)
                known[key] = val
            ins = o.fn(eng)
            if o.is_dma:
                ins.then_inc(sems[o.ev[0]], o.inc)
            elif o.signal:
                ins.then_inc(sems[o.ev[0]], 1)

    def final_waits(self, eng, sems):
        for chan, c in self.chan_count.items():
            eng.wait_ge(sems[("dma", chan)], c)


class Builder:
    def __init__(self, n_sub=8, skip_mixers=False, debug=False, skip_kinds=()):
        self.debug = debug
        self.skip_kinds = set(skip_kinds)
        self.n_sub = n_sub
        self.skip_mixers = skip_mixers
        self.S = Sched()
        self.nc = bass.Bass("TRN2", target_bir_lowering=False)
        self.bank_rr = 0
        self.pair_rr = 0
        self.ring_next_load = 0
        self.ring_next_use = 0
        self.ring_released = set()
        self.ring_pending = []
        self.tmp_rr = {}

    def declare(self):
        nc = self.nc
        di = lambda name, shape, dt=F32: nc.dram_tensor(name, list(shape), dt, kind="ExternalInput").ap()
        self.x_in = di("x_in", [1024, D])
        self.cvecT = di("cvecT", [128, 2, 8])
        self.cmask = di("cmask", [128, 16])
        self.consts = di("consts", [128, 9, 128])
        self.state0 = di("state0", [2, 2, NH, DK, DV])
        self.pvec = di("pvec", [128, 320])
        self.w = {}
        for name, shape in WEIGHT_SHAPES.items():
            self.w[name] = di(name, shape)
        self.y_out = nc.dram_tensor("y_out", [1024, D], F32, kind="ExternalOutput").ap()
        self.ns_out = nc.dram_tensor("ns_out", [2, 2, 2, NH, DK, DV], F32, kind="ExternalOutput").ap()
        self.dbg = {}
        if self.debug:
            self.dbg["hT"] = nc.dram_tensor("dbg_hT", [128, 8, 1024], BF16, kind="ExternalOutput").ap()
            self.dbg["rows"] = nc.dram_tensor("dbg_rows", [128, 4, 1024], F32, kind="ExternalOutput").ap()
            self.dbg["acc"] = nc.dram_tensor("dbg_acc", [128, 8, 1024], F32, kind="ExternalOutput").ap()
            self.dbg["gSb"] = nc.dram_tensor("dbg_gSb", [128, 8, 4, 256], BF16, kind="ExternalOutput").ap()
            self.dbg["gyT"] = nc.dram_tensor("dbg_gyT", [128, 8, 1024], BF16, kind="ExternalOutput").ap()
            self.dbg["ccv"] = nc.dram_tensor("dbg_ccv", [128, 8, 1024], F32, kind="ExternalOutput").ap()
            self.dbg["chT"] = nc.dram_tensor("dbg_chT", [128, 8, 1024], BF16, kind="ExternalOutput").ap()
            self.dbg["chin"] = nc.dram_tensor("dbg_chin", [128, 8, 1024], BF16, kind="ExternalOutput").ap()
        self.agg_src = [[nc.dram_tensor(f"agg_src{j}_{z}", [128, 1032], F32) for z in range(2)] for j in range(2)]
        self.agg_dst = [[nc.dram_tensor(f"agg_dst{j}_{z}", [4 * 128, 1032], F32) for z in range(2)] for j in range(2)]
        self.halo_src = nc.dram_tensor("halo_src", [128, 2 * 8 * 64], F32)
        self.halo_dst = nc.dram_tensor("halo_dst", [4 * 128, 2 * 8 * 64], F32)

    def view(self, off_bytes, shape, dt):
        esz = 4 if dt == F32 else 2
        n = int(np.prod(shape[1:]))
        assert off_bytes % 4 == 0
        assert off_bytes + n * esz <= self.SCR_BYTES, (off_bytes, n * esz, self.SCR_BYTES)
        a = self.scr[:, off_bytes // 4: off_bytes // 4 + (n * esz + 3) // 4]
        if dt != F32:
            a = a.bitcast(dt)
        if len(shape) == 2:
            return a
        names = " ".join(f"d{i}" for i in range(1, len(shape)))
        kw = {f"d{i}": shape[i] for i in range(1, len(shape))}
        return a.rearrange(f"p ({names}) -> p {names}", **kw)

    single_pool = list(range(8))
    pair_pool = [0, 2, 4, 6]

    def bank(self):
        self.bank_rr = (self.bank_rr + 1) % len(self.single_pool)
        return self.single_pool[self.bank_rr]

    def pair(self):
        self.pair_rr = (self.pair_rr + 1) % len(self.pair_pool)
        return self.pair_pool[self.pair_rr]

    def psf(self, b, n=1):
        return self.ps[:, b * 512:(b + n) * 512]

    def psb(self, b):
        return self.ps[:, b * 512:(b + 1) * 512].bitcast(BF16)

    def pkeys(self, b, n=1):
        return [("ps", b + i) for i in range(n)]

    def mm(self, out, lhsT, rhs, start, stop, reads, writes):
        return self.S.op("pe", lambda e: e.matmul(out, lhsT, rhs, start=start, stop=stop), reads, writes)

    def tr(self, out, in_, ident, reads, writes):
        return self.S.op("pe", lambda e: e.transpose(out, in_, ident), reads, writes)

    def act(self, out, in_, func, reads, writes, bias=None, scale=None):
        kw = {}
        if bias is not None:
            kw["bias"] = bias
        if scale is not None:
            kw["scale"] = scale
        return self.S.op("act", lambda e: e.activation(out, in_, func, **kw), reads, writes)

    def tt(self, eng, out, in0, in1, op, reads, writes):
        return self.S.op(eng, lambda e: e.tensor_tensor(out, in0, in1, op), reads, writes)

    def ts(self, eng, out, in0, s1, s2, op0, op1, reads, writes):
        if op1 is None:
            return self.S.op(eng, lambda e: e.tensor_scalar(out, in0, s1, None, op0), reads, writes)
        return self.S.op(eng, lambda e: e.tensor_scalar(out, in0, s1, s2, op0, op1), reads, writes)

    def stt(self, eng, out, in0, scalar, in1, op0, op1, reads, writes):
        return self.S.op(eng, lambda e: e.scalar_tensor_tensor(out, in0, scalar, in1, op0, op1), reads, writes)

    def cp(self, eng, out, in_, reads, writes):
        if eng == "act":
            return self.S.op("act", lambda e: e.copy(out, in_), reads, writes)
        return self.S.op(eng, lambda e: e.tensor_copy(out, in_), reads, writes)

    def dma(self, q, out, in_, reads, writes, chan):
        return self.S.op(q, lambda e: e.dma_start(out=out, in_=in_), reads, writes, is_dma=True, chan=chan)

    NSLOT = 3

    def build_stream(self):
        w = self.w
        st = []
        sub = 0
        for l in range(DEPTH):
            for s in range(2):
                if sub >= self.n_sub:
                    break
                sub += 1
                if s == 0 and (self.skip_mixers or (l % 3) in self.skip_kinds):
                    continue
                mod = lambda j: (("mod", l, j), w["mod_w"][l, :, j * 1024:(j + 1) * 1024])
                st += [mod(3 * s), mod(3 * s + 1)]
                core = []
                if s == 0:
                    kind, j = l % 3, l // 3
                    if kind == 0:
                        core += [(("gin", j, i), w["gla_w_in"][j, :, i * 1024:(i + 1) * 1024]) for i in range(3)]
                        core += [(("go", j), w["gla_w_o"][j])]
                    elif kind == 1:
                        core += [(("cpw1", i), w["conf_w_pw1"][0, :, i * 1024:(i + 1) * 1024]) for i in range(2)]
                        core += [(("cpw2",), w["conf_w_pw2"][0])]
                    else:
                        core += [(("sin", i), w["sc_w_in"][0, :, i * 1024:(i + 1) * 1024]) for i in (1, 2, 0)]
                        core += [(("sout",), w["sc_w_out"][0])]
                else:
                    for b in range(4):
                        core += [(("w1", l, b), w["ff_w1"][l, :, b * 1024:(b + 1) * 1024]),
                                 (("w2", l, b), w["ff_w2"][l, b * 1024:(b + 1) * 1024, :])]
                if s == 0 and l % 3 == 0:
                    st += [mod(3 * s + 2)] + core
                else:
                    st += [core[0], mod(3 * s + 2)] + core[1:]
        self.ring_tags = [t for t, _ in st]
        self.ring_pending = [a for _, a in st]

    def ring_take(self, tag):
        idx = self.ring_next_use
        assert self.ring_tags[idx] == tag, (self.ring_tags[idx], tag)
        self.ring_next_use += 1
        if idx == 0:
            self.ring_issue_upto(self.NSLOT - 1)
        assert idx < self.ring_next_load
        slot = idx % self.NSLOT
        return self.ring[:, slot], ("ring", slot), idx

    def ring_release(self, idx):
        self.ring_released.add(idx)
        while (self.ring_next_load < len(self.ring_pending)
               and (self.ring_next_load - self.NSLOT) in self.ring_released):
            self.ring_issue_upto(self.ring_next_load)

    def ring_issue_upto(self, idx):
        while self.ring_next_load <= idx and self.ring_next_load < len(self.ring_pending):
            i = self.ring_next_load
            slot = i % self.NSLOT
            src = self.ring_pending[i].rearrange("(kc p) n -> p kc n", p=128)
            for q in range(4):
                self.dma("pool", self.ring[:, slot, 2 * q:2 * q + 2, :], src[:, 2 * q:2 * q + 2, :],
                         reads=[], writes=[("ring", slot)], chan=("ring", slot))
            self.ring_next_load += 1

    def run(self):
        nc = self.nc
        self.declare()
        with contextlib.ExitStack() as st:
            sb = lambda name, shape, dt: st.enter_context(nc.sbuf_tensor(name, list(shape), dt))
            self.x = sb("x", [128, NT, D], F32)
            self.hT = sb("hT", [128, 8, 1024], BF16)
            self.ring = sb("ring", [128, self.NSLOT, 8, 1024], BF16)
            self.rows = sb("rows", [128, 4, D], F32)
            self.brow = sb("brow", [128, 1, D], F32)
            self.gsm = sb("gsm", [128, 256], F32)
            self.SCR_BYTES = 64 * 1024
            self.scr = sb("scr", [128, self.SCR_BYTES // 4], F32)
            self.cst = sb("cst", [128, 9, 128], F32)
            self.cstb = sb("cstb", [128, 4, 128], BF16)
            self.cT = sb("cT", [128, 2, 8], F32)
            self.sT = sb("sT", [128, 2, 8], F32)
            self.sTrep = sb("sTrep", [128, 2, 8, 128], BF16)
            self.cm = sb("cm", [128, 16], F32)
            self.pv = sb("pv", [128, 320], F32)
            self.hb = sb("hb", [128, 2, D], BF16)
            self.tmpf = sb("tmpf", [128, 2, D], F32)
            self.stat = sb("stat", [128, 2, 16], F32)
            self.bst = sb("bst", [128, 2, 12], F32)
            self.ps = st.enter_context(nc.psum_tensor("ps", [128, 8 * 512], F32))

            self.program()

            cnt = self.S.finalize()
            import os
            if os.environ.get("KDEBUG"):
                print("SIGNAL COUNTS", max(cnt.values()), len(cnt), "n_ops", len(self.S.ops), "chan max", max(self.S.chan_count.values()), "n_chan", len(self.S.chan_count))
            sems = {}
            for k in cnt:
                sems[k] = st.enter_context(nc.semaphore(f"s_{k[1]}_{k[2]}"))
            for i, chan in enumerate(self.S.chan_count):
                sems[("dma", chan)] = st.enter_context(nc.semaphore(f"d_{i}"))
            block = st.enter_context(nc.Block())
            S = self.S

            @block.tensor
            def _(e):
                S.emit("pe", e, sems)

            @block.scalar
            def _(e):
                S.emit("act", e, sems)

            @block.vector
            def _(e):
                S.emit("dve", e, sems)

            @block.gpsimd
            def _(e):
                S.emit("pool", e, sems)
                for (kind, name), ap in []:
                    pass

            @block.sync
            def _(e):
                S.emit("sp", e, sems)
                S.final_waits(e, sems)
                for k, v in cnt.items():
                    e.wait_ge(sems[k], v)
        return nc

    def program(self):
        import os
        if os.environ.get("DMA_PROBE"):
            w = self.w
            nblk = int(os.environ["DMA_PROBE"])
            self.ring_tags = [("p", i) for i in range(nblk)]
            self.ring_pending = [w["ff_w1"][i % 4, :, (i // 4 % 4) * 1024:(i // 4 % 4 + 1) * 1024] for i in range(nblk)]
            for i in range(nblk):
                wv, wk, wi = self.ring_take(("p", i))
                b = self.bank()
                self.mm(self.psf(b)[:, 0:128], wv[:, 7, 0:128], wv[:, 7, 896:1024], True, True, [wk], self.pkeys(b))
                self.ring_release(wi)
            self.cp("act", self.x[:, 0, 0:128], self.psf(b)[:, 0:128], self.pkeys(b), [("x", 0)])
            self.epilogue()
            return
        self.build_stream()
        self.prologue()
        sub = 0
        for l in range(DEPTH):
            for s in range(2):
                if sub >= self.n_sub:
                    break
                if s == 0 and (self.skip_mixers or (l % 3) in self.skip_kinds):
                    sub += 1
                    continue
                self.S.epoch = sub
                self.sublayer(l, s)
                sub += 1
        self.epilogue()

    def prologue(self):
        for t in range(NT):
            self.dma("sp", self.x[:, t, :], self.x_in[t * 128:(t + 1) * 128, :], [], [("x", t)], chan=("x", t))
        self.dma("sp", self.cT[:], self.cvecT, [], [("cT",)], chan="misc0")
        self.dma("sp", self.cm[:], self.cmask, [], [("cm",)], chan="misc1")
        self.dma("sp", self.cst[:], self.consts, [], [("cst",)], chan="misc2")
        self.dma("sp", self.pv[:], self.pvec, [], [("pv",)], chan="misc3")
        self.cp("dve", self.cstb[:, 0:3, :], self.cst[:, 0:3, :], [("cst",)], [("cstb",)])
        self.act(self.sT[:], self.cT[:], AF.Silu, [("cT",)], [("sT",)])
        self.cp("dve", self.sTrep[:].rearrange("p s k m -> p (s k) m"),
                self.sT[:].rearrange("p s k -> p (s k)").unsqueeze(2).to_broadcast([128, 16, 128]),
                [("sT",)], [("sTrep",)])

    def epilogue(self):
        for t in range(NT):
            self.dma("sp", self.y_out[t * 128:(t + 1) * 128, :], self.x[:, t, :], [("x", t)], [("yout", t)],
                     chan=("yo", t % 2))

    def mod_rows(self, l, j, dsts, plus_one):
        wv, wkey, widx = self.ring_take(("mod", l, j))
        bslot = self.tmp_next("brow", 1)
        self.dma("sp", self.brow[:, bslot, :], self.w["mod_b"][l, j * 1024:(j + 1) * 1024].partition_broadcast(128),
                 [], [("brow", bslot)], chan=("brow", bslot))
        for s in range(2):
            b = self.pair()
            for half in range(2):
                for kc in range(8):
                    self.mm(self.psf(b + half), self.sTrep[:, s, kc, :], wv[:, kc, half * 512:(half + 1) * 512],
                            kc == 0, kc == 7, [("sTrep",), wkey], self.pkeys(b + half))
            if plus_one:
                self.stt("dve", self.rows[:, dsts[s], :], self.psf(b, 2), 1.0, self.brow[:, bslot, :], ALU.add, ALU.add,
                         self.pkeys(b, 2) + [("brow", bslot)], [("rows", dsts[s])])
            else:
                self.tt("dve", self.rows[:, dsts[s], :], self.psf(b, 2), self.brow[:, bslot, :], ALU.add,
                        self.pkeys(b, 2) + [("brow", bslot)], [("rows", dsts[s])])
        self.ring_release(widx)

    def tmp_next(self, name, n):
        v = self.tmp_rr.get(name, 0)
        self.tmp_rr[name] = (v + 1) % n
        return v

    def row_load(self, dst_idx, src_row_ap):
        self.dma("sp", self.rows[:, dst_idx, :], src_row_ap.partition_broadcast(128), [], [("rows", dst_idx)],
                 chan=("rows", dst_idx))

    def sublayer(self, l, s):
        self.mod_rows(l, 3 * s + 0, (0, 2), False)
        self.mod_rows(l, 3 * s + 1, (1, 3), True)
        for t in range(NT):
            st_ = 0 if t < 4 else 1
            tb = self.tmp_next("tmpf", 2)
            hbk = self.tmp_next("hb", 2)
            self.tt("dve", self.tmpf[:, tb, :], self.x[:, t, :], self.rows[:, 2 * st_ + 1, :], ALU.mult,
                    [("x", t), ("rows", 2 * st_ + 1)], [("tmpf", tb)])
            self.tt("dve", self.hb[:, hbk, :], self.tmpf[:, tb, :], self.rows[:, 2 * st_, :], ALU.add,
                    [("tmpf", tb), ("rows", 2 * st_)], [("hb", hbk)])
            self.transpose_tile(self.hb[:, hbk, :], ("hb", hbk), self.hT, "hT", t)
        if self.debug and "hT" in self.dbg:
            self.dma("sp", self.dbg.pop("hT"), self.hT[:], [("hT", t) for t in range(NT)], [("dbg", 0)], chan="dbg0")
            self.dma("sp", self.dbg.pop("rows"), self.rows[:], [("rows", i) for i in range(4)], [("dbg", 1)], chan="dbg1")
        self.gate_rows = lambda: self.mod_rows(l, 3 * s + 2, (0, 1), False)
        self.row_load(2, self.w["ln_g"][l, s, :])
        self.row_load(3, self.w["ln_b"][l, s, :])
        self.S.barrier(lambda e: e.memset(self.gsm[:, 255:256], 0.0))
        if s == 0:
            kind = l % 3
            if kind == 0:
                src = self.gla_core(l // 3)
            elif kind == 1:
                src = self.conf_core()
            else:
                src = self.sconv_core()
            early = getattr(self, "early_post", set())
            for t in range(NT):
                if t in early:
                    continue
                ap, keys = src(t)
                self.post_tile(t, ap, keys)
                if t % 4 == 3:
                    self.post_group(range(t - 3, t + 1))
            self.early_post = set()
        else:
            self.mlp_core(l)

    def transpose_tile(self, src, src_key, dstT, dst_name, t):
        b = self.bank()
        pv = self.psb(b)
        for kc in range(8):
            self.tr(pv[:, kc * 128:(kc + 1) * 128], src[:, kc * 128:(kc + 1) * 128], self.cstb[:, 0, :],
                    [src_key, ("cstb",)], self.pkeys(b))
        self.cp("act", dstT[:, :, t * 128:(t + 1) * 128], pv.rearrange("p (k m) -> p k m", k=8),
                self.pkeys(b), [(dst_name, t)])

    def post_tile(self, t, ap, keys):
        mv = self.stat[:, 0, :].rearrange("p (t c) -> p t c", c=2)
        st_ = 0 if t < 4 else 1
        tb = self.tmp_next("tmpf", 2)
        xt = self.x[:, t, :]
        self.tt("dve", self.tmpf[:, tb, :], ap, self.rows[:, st_, :], ALU.mult, keys + [("rows", st_)], [("tmpf", tb)])
        self.stt("dve", xt, xt, float(ALPHA), self.tmpf[:, tb, :], ALU.mult, ALU.add, [("x", t), ("tmpf", tb)], [("x", t)])
        sums = self.bst[:].rearrange("p a b -> p (a b)")[:, 0:16].rearrange("p (t c) -> p t c", c=2)
        junk = self.tmpf[:, tb, :]
        self.S.op("act", lambda e, o=junk, i=xt, a=sums[:, t, 0:1]: e.activation(o, i, AF.Identity, accum_out=a),
                  [("x", t)], [("tmpf", tb), ("sums", t, 0)])
        self.S.op("act", lambda e, o=junk, i=xt, a=sums[:, t, 1:2]: e.activation(o, i, AF.Square, accum_out=a),
                  [("x", t)], [("tmpf", tb), ("sums", t, 1)])

    def post_group(self, tiles):
        tiles = list(tiles)
        t0, t1 = tiles[0], tiles[-1] + 1
        g = t0 // 4
        mv = self.stat[:, 0, :].rearrange("p (t c) -> p t c", c=2)
        aux = self.stat[:, 1, :].rearrange("p (c t) -> p c t", c=2)
        allmv = [("mv", t) for t in tiles]
        sums = self.bst[:].rearrange("p a b -> p (a b)")[:, 0:16].rearrange("p (t c) -> p t c", c=2)
        msq = self.bst[:].rearrange("p a b -> p (a b)")[:, 16:24]
        allsums = [("sums", t, c) for t in tiles for c in range(2)]
        self.ts("dve", mv[:, t0:t1, 0], sums[:, t0:t1, 0], 1.0 / D, None, ALU.mult, None, allsums, allmv)
        self.tt("dve", msq[:, t0:t1], mv[:, t0:t1, 0], mv[:, t0:t1, 0], ALU.mult, allmv, [("msq", g)])
        self.stt("dve", mv[:, t0:t1, 1], sums[:, t0:t1, 1], 1.0 / D, msq[:, t0:t1], ALU.mult, ALU.subtract,
                 allsums + [("msq", g)], allmv)
        self.act(aux[:, 0, t0:t1], mv[:, t0:t1, 1], AF.Sqrt, allmv, [("aux", 0, g)], bias=float(LN_EPS))
        self.S.op("dve", lambda e: e.reciprocal(aux[:, 0, t0:t1], aux[:, 0, t0:t1]), [("aux", 0, g)], [("aux", 0, g)])
        self.stt("dve", aux[:, 1, t0:t1], mv[:, t0:t1, 0], -1.0, aux[:, 0, t0:t1], ALU.mult, ALU.mult, allmv + [("aux", 0, g)], [("aux", 1, g)])
        for t in tiles:
            xt = self.x[:, t, :]
            self.act(xt, xt, AF.Identity, [("x", t), ("aux", 0, g), ("aux", 1, g)], [("x", t)],
                     bias=aux[:, 1, t:t + 1], scale=aux[:, 0, t:t + 1])
            self.tt("dve", xt, xt, self.rows[:, 2, :], ALU.mult, [("x", t), ("rows", 2)], [("x", t)])
            self.tt("dve", xt, xt, self.rows[:, 3, :], ALU.add, [("x", t), ("rows", 3)], [("x", t)])

    def mlp_core(self, l):
        acc = self.view(0, [128, NT, D], F32)
        uT = self.view(32 * 1024, [128, 2, 8, 512], BF16)
        rt = self.view(48 * 1024, [128, 4, 512], F32)
        W1, W2 = {}, {}

        def F(b, half):
            w1, k1, i1 = W1[b]
            for sub in range(8):
                pb = self.bank()
                for kc in range(8):
                    self.mm(self.psf(pb), w1[:, kc, sub * 128:(sub + 1) * 128], self.hT[:, kc, half * 512:(half + 1) * 512],
                            kc == 0, kc == 7, [k1] + [("hT", half * 4 + i) for i in range(4)], self.pkeys(pb))
                ri = self.tmp_next("rt", 4)
                self.act(rt[:, ri, :], self.psf(pb), AF.Relu, self.pkeys(pb), [("rt", ri)])
                if sub % 2 == 0:
                    self.tt("dve", uT[:, half, sub, :], rt[:, ri, :], rt[:, ri, :], ALU.mult, [("rt", ri)], [("uT", half, sub)])
                else:
                    self.act(uT[:, half, sub, :], rt[:, ri, :], AF.Square, [("rt", ri)], [("uT", half, sub)])
            if half == 1:
                self.ring_release(i1)

        def S_(b, half):
            w2, k2, i2 = W2[b]
            for tt_ in range(4):
                t = half * 4 + tt_
                for nh in range(2):
                    pb = self.bank()
                    for sub in range(8):
                        self.mm(self.psf(pb), uT[:, half, sub, tt_ * 128:(tt_ + 1) * 128], w2[:, sub, nh * 512:(nh + 1) * 512],
                                sub == 0, sub == 7, [k2, ("uT", half, sub)], self.pkeys(pb))
                    dst = acc[:, t, nh * 512:(nh + 1) * 512]
                    if b == 0:
                        self.cp("act", dst, self.psf(pb), self.pkeys(pb), [("acc", t, nh)])
                    else:
                        self.tt("dve", dst, dst, self.psf(pb), ALU.add, self.pkeys(pb) + [("acc", t, nh)], [("acc", t, nh)])
                if b == 3:
                    self.post_tile(t, acc[:, t, :], [("acc", t, 0), ("acc", t, 1)])
            if half == 1:
                self.ring_release(i2)
            if b == 3:
                self.post_group(range(half * 4, half * 4 + 4))

        W1[0] = self.ring_take(("w1", l, 0))
        self.gate_rows()
        F(0, 0)
        F(0, 1)
        for b in range(4):
            W2[b] = self.ring_take(("w2", l, b))
            S_(b, 0)
            if b < 3:
                W1[b + 1] = self.ring_take(("w1", l, b + 1))
                F(b + 1, 0)
            S_(b, 1)
            if b < 3:
                F(b + 1, 1)

    def gla_core(self, j):
        K1 = 1024
        qdec = self.view(0, [128, 4, 2, 4, 128], BF16)
        sm = self.view(8 * K1, [128, 4, 2, 4, 128], BF16)
        vtok = self.view(16 * K1, [128, 4, 1024], BF16)
        kend = self.view(24 * K1, [128, 4, 2, 512], BF16)
        Sbst = self.view(32 * K1, [128, 8, 4, 256], BF16)
        qk_tok = self.view(48 * K1, [128, 1024], F32)
        S_f = self.view(48 * K1, [128, 4, 256], F32)
        Lt = self.view(52 * K1, [128, 1024], F32)
        S_b = self.view(52 * K1, [128, 4, 256], F32)
        kdec = self.view(56 * K1, [128, 2, 4, 128], BF16)
        Sfb = self.view(56 * K1, [128, 2, 4, 256], BF16)
        zrT = self.view(60 * K1, [128, 1024], BF16)
        wga = self.view(62 * K1, [128, 8, 64], BF16)
        wgb = self.view(63 * K1, [128, 512], BF16)
        gsm = self.gsm
        dec = gsm[:, 0:64].rearrange("p (t c) -> p t c", t=4)
        Ptot = gsm[:, 64:72]
        Prc = gsm[:, 72:88].rearrange("p (s c) -> p s c", s=2)
        Dm = gsm[:, 88:92]
        ssq = gsm[:, 96:104].rearrange("p (s c) -> p s c", s=2)
        gA, gB, gC = ("gA",), ("gB",), ("gC",)
        self.single_pool, self.pair_pool = [6, 7], [0, 2, 4]
        self.single_pool, self.pair_pool = list(range(8)), [0, 2, 4, 6]
        self.gate_rows()
        self.single_pool, self.pair_pool = [6, 7], [0, 2, 4]
        w0, k0, i0 = self.ring_take(("gin", j, 0))
        w1, k1, i1 = self.ring_take(("gin", j, 1))
        w2, k2, i2 = self.ring_take(("gin", j, 2))
        wsrc = self.w
        self.memset("pool", wga, 0.0, [("wga",)])
        self.memset("pool", zrT[0:64, :], 1.0, [("zrT",)])
        for z in range(2):
            self.dma("pool", wga[:, :, z * 32:z * 32 + 16], wsrc["gla_w_ga"][j, z].rearrange("(kc p) r -> p kc r", p=128),
                     [], [("wga",)], chan="wga")
            self.dma("pool", wgb[z * 32:z * 32 + 16, :], wsrc["gla_w_gb"][j, z], [], [("wgb",)], chan="wgb")
            self.dma("pool", wgb[z * 32 + 16:z * 32 + 17, :], wsrc["gla_b_g"][j, z:z + 1, :], [], [("wgb",)], chan="wgb")
        self.dma("sp", self.brow[:, 0, :], wsrc["gla_gn_g"][j, :].partition_broadcast(128), [], [("brow", 0)], chan=("brow", 0))
        for half in range(2):
            b = self.bank()
            for kc in range(8):
                self.mm(self.psf(b)[0:64, :], wga[:, kc, :], self.hT[:, kc, half * 512:(half + 1) * 512], kc == 0, kc == 7,
                        [("wga",)] + [("hT", half * 4 + i) for i in range(4)], self.pkeys(b))
            for z in range(2):
                self.cp("act", zrT[z * 32:z * 32 + 16, half * 512:(half + 1) * 512], self.psf(b)[z * 32:z * 32 + 16, :],
                        self.pkeys(b), [("zrT",)])

        def proj(t, w, wk, nh):
            pass

        def phase1(tt, t):
            tok = slice(t * 128, (t + 1) * 128)
            pq = self.pair()
            for nh in range(2):
                for kc in range(8):
                    self.mm(self.psf(pq + nh), self.hT[:, kc, tok], w0[:, kc, nh * 512:(nh + 1) * 512], kc == 0, kc == 7,
                            [k0, ("hT", t)], self.pkeys(pq + nh))
            self.cp("act", qk_tok, self.psf(pq, 2), self.pkeys(pq, 2), [gA])
            pv_ = self.pair()
            for nh in range(2):
                for kc in range(8):
                    self.mm(self.psf(pv_ + nh), self.hT[:, kc, tok], w1[:, kc, nh * 512:(nh + 1) * 512], kc == 0, kc == 7,
                            [k1, ("hT", t)], self.pkeys(pv_ + nh))
            self.cp("dve", vtok[:, tt, :], self.psf(pv_, 2), self.pkeys(pv_, 2), [("vtok", tt)])
            pz = self.pair()
            for z in range(2):
                self.mm(self.psf(pz + z), zrT[z * 32:z * 32 + 17, tok], wgb[z * 32:z * 32 + 17, :], True, True,
                        [("zrT",), ("wgb",)], self.pkeys(pz + z))
            self.act(Lt, self.psf(pz, 2), AF.Exp, self.pkeys(pz, 2), [gB], scale=-1.0)
            self.act(Lt, Lt, AF.Ln, [gB], [gB], bias=1.0)
            pc = self.pair()
            for z in range(2):
                for h in range(4):
                    c0 = (z * 4 + h) * 128
                    self.mm(self.psf(pc, 2)[:, c0:c0 + 128], Lt[:, z * 512 + h * 128:z * 512 + (h + 1) * 128], self.cst[:, 3 + z, :],
                            True, True, [gB, ("cst",)], self.pkeys(pc + z))
            pss = self.pair()
            for z in range(2):
                self.mm(self.psf(pss + z), self.cst[:, 5 + z, :], Lt[:, z * 512:(z + 1) * 512], True, True,
                        [gB, ("cst",)], self.pkeys(pss + z))
            ptot = self.bank()
            for z in range(2):
                for h in range(4):
                    c0 = (z * 4 + h) * 2
                    self.mm(self.psf(ptot)[:, c0:c0 + 2], Lt[:, z * 512 + h * 128:z * 512 + (h + 1) * 128], self.cst[:, 7, 0:2],
                            True, True, [gB, ("cst",)], self.pkeys(ptot))
            self.act(dec[:, tt, :], self.psf(ptot)[:, 0:16], AF.Exp, self.pkeys(ptot), [("dec", tt)])
            pt = self.pair()
            for idx in range(8):
                self.tr(self.psf(pt, 2)[:, idx * 128:(idx + 1) * 128], qk_tok[:, idx * 128:(idx + 1) * 128], self.cst[:, 0, :],
                        [gA, ("cst",)], self.pkeys(pt + idx // 4))
            e1 = self.tmp_next("tmpf", 2)
            E1 = self.tmpf[:, e1, :]
            self.act(E1, self.psf(pc, 2), AF.Exp, self.pkeys(pc, 2), [("tmpf", e1)])
            self.stt("dve", qdec[:, tt].rearrange("p d h i -> p d (h i)"), E1.rearrange("p (d x) -> p d x", d=2), float(DK ** -0.5),
                     self.psf(pt)[:, 0:512].unsqueeze(1).to_broadcast([128, 2, 512]), ALU.mult, ALU.mult,
                     [("tmpf", e1)] + self.pkeys(pt), [("qdec", tt)])
            e2 = self.tmp_next("tmpf", 2)
            E2 = self.tmpf[:, e2, :]
            self.S.op("dve", lambda e, o=E2, i=E1: e.reciprocal(o, i), [("tmpf", e1)], [("tmpf", e2)])
            self.tt("dve", kdec.rearrange("p d h i -> p d (h i)"), E2.rearrange("p (d x) -> p d x", d=2),
                    self.psf(pt + 1)[:, 0:512].unsqueeze(1).to_broadcast([128, 2, 512]), ALU.mult,
                    [("tmpf", e2)] + self.pkeys(pt + 1), [gC])
            e3 = self.tmp_next("tmpf", 2)
            E3 = self.tmpf[:, e3, :]
            self.act(E3, self.psf(pss, 2), AF.Exp, self.pkeys(pss, 2), [("tmpf", e3)])
            self.tt("dve", kend[:, tt], E3.rearrange("p (d x) -> p d x", d=2),
                    qk_tok[:, 512:1024].unsqueeze(1).to_broadcast([128, 2, 512]), ALU.mult,
                    [("tmpf", e3), gA], [("kend", tt)])
            psc = self.pair()
            for z in range(2):
                for h in range(4):
                    c0 = (z * 4 + h) * 128
                    self.mm(self.psf(psc, 2)[:, c0:c0 + 128], kdec[:, z, h, :], qdec[:, tt, z, h, :], True, True,
                            [gC, ("qdec", tt)], self.pkeys(psc + z))
            self.tt("dve", sm[:, tt], self.psf(psc, 2).rearrange("p (d h i) -> p d h i", d=2, h=4),
                    self.cst[:, 1:3, :].unsqueeze(2).to_broadcast([128, 2, 4, 128]), ALU.mult,
                    self.pkeys(psc, 2) + [("cst",)], [("sm", tt)])

        def kv(tt, cc, z):
            pk = self.pair()
            rows = slice(cc * 64, (cc + 1) * 64)
            for h in range(4):
                self.mm(self.psf(pk, 2)[:, h * 256:(h + 1) * 256], kend[rows, tt, z, h * 128:(h + 1) * 128],
                        vtok[rows, tt, h * 256:(h + 1) * 256], True, True, [("kend", tt), ("vtok", tt)], self.pkeys(pk + h // 2))
            return pk

        def step(S, skey, tt, cc, z):
            pk = kv(tt, cc, z)
            for h in range(4):
                ci = (z * 4 + h) * 2 + cc
                self.stt("dve", S[:, h, :], S[:, h, :], dec[:, tt, ci:ci + 1], self.psf(pk, 2)[:, h * 256:(h + 1) * 256],
                         ALU.mult, ALU.add, [skey, ("dec", tt)] + self.pkeys(pk + h // 2), [skey])

        def dec4(tt, cc, z):
            return dec[:, tt, :].rearrange("p (d h c) -> p d h c", d=2, h=4)[:, z, :, cc]

        def phase2(tiles, seq):
            T = len(tiles)
            C = 2 * T
            sample = seq is None
            Sf2 = S_f.rearrange("p h e -> p (h e)")
            Sb2 = S_b.rearrange("p h e -> p (h e)")
            if sample:
                self.memset("dve", Sf2, 0.0, [gA])
                self.memset("dve", Sb2, 0.0, [gB])
                self.memset("dve", Ptot, 1.0, [("Ptot",)])
                for c in range(C):
                    step(S_f, gA, c // 2, c % 2, 0)
                    self.tt("dve", Ptot[:, 0:4], Ptot[:, 0:4], dec4(c // 2, c % 2, 0), ALU.mult, [("Ptot",), ("dec", c // 2)], [("Ptot",)])
                for c in reversed(range(C)):
                    step(S_b, gB, c // 2, c % 2, 1)
                    self.tt("dve", Ptot[:, 4:8], Ptot[:, 4:8], dec4(c // 2, c % 2, 1), ALU.mult, [("Ptot",), ("dec", c // 2)], [("Ptot",)])
                dsts_ = []
                for z, S2z, skz in ((0, Sf2, gA), (1, Sb2, gB)):
                    src_ = self.agg_src[j][z].ap()
                    self.dma("sp", src_[:, 0:1024], S2z, [skz], [("aggs", z, 0)], chan=("ag0", z, 0))
                    self.dma("sp", src_[:, 1024:1032], Ptot, [("Ptot",)], [("aggs", z, 1)], chan=("ag0", z, 1))
                    self.S.op("pool", lambda e, z=z: e.collective_compute("AllGather", ALU.bypass, replica_groups=[[0, 1, 2, 3], [4, 5, 6, 7]],
                                                                          ins=[self.agg_src[j][z].ap()], outs=[self.agg_dst[j][z].ap()]),
                              [("aggs", z, 0), ("aggs", z, 1)], [("aggd", z)], is_dma=True, chan=("gcc", j, z), inc=1)
                    dsts_.append(self.agg_dst[j][z].ap().rearrange("(r p) c -> r p c", p=128))
                mid_exchange()
                st0 = self.state0
                self.dma("sp", S_f, st0[j, 0].rearrange("h d e -> d h e"), [], [gA], chan=("st0", 0))
                self.dma("sp", S_b, st0[j, 1].rearrange("h d e -> d h e"), [], [gB], chan=("st0", 1))
                for z, S2, S3, skey, order in ((0, Sf2, S_f, gA, range(4)), (1, Sb2, S_b, gB, reversed(range(4)))):
                    for i in order:
                        mcol = self.cm[:, z * 4 + i:z * 4 + i + 1]
                        ab = self.tmp_next("tmpf", 2)
                        ps_ = self.tmp_next("Prc", 2)
                        self.dma("sp", self.tmpf[:, ab, :], dsts_[z][i, :, 0:1024], [("aggd", z)], [("tmpf", ab)], chan=("agl", ab))
                        self.dma("sp", Prc[:, ps_, :], dsts_[z][i, :, 1024:1032], [("aggd", z)], [("Prc", ps_)], chan=("prl", ps_))
                        self.ts("dve", Dm, Prc[:, ps_, z * 4:(z + 1) * 4], 1.0, mcol, ALU.subtract, ALU.mult, [("Prc", ps_), ("cm",)], [("Dm",)])
                        self.ts("dve", Dm, Dm, 1.0, None, ALU.add, None, [("Dm",)], [("Dm",)])
                        for h in range(4):
                            self.ts("dve", S3[:, h, :], S3[:, h, :], Dm[:, h:h + 1], None, ALU.mult, None, [skey, ("Dm",)], [skey])
                        self.stt("dve", S2, self.tmpf[:, ab, :], mcol, S2, ALU.mult, ALU.add, [("tmpf", ab), ("cm",), skey], [skey])
            else:
                self.memset("dve", Sf2, 0.0, [gA])
                self.memset("dve", Sb2, 0.0, [gB])
            for c in reversed(range(C)):
                self.cp("act", Sbst[:, c].rearrange("p h e -> p (h e)"), Sb2, [gB], [("Sbst", c)])
                step(S_b, gB, c // 2, c % 2, 1)
            for tt, t in enumerate(tiles):
                tok = slice(t * 128, (t + 1) * 128)
                for cc in range(2):
                    self.cp("act", Sfb[:, cc].rearrange("p h e -> p (h e)"), Sf2, [gA], [gC])
                    step(S_f, gA, tt, cc, 0)
                po = self.pair()
                for h in range(4):
                    ov = self.psf(po, 2)[:, h * 256:(h + 1) * 256]
                    vh = vtok[:, tt, h * 256:(h + 1) * 256]
                    wk_ = self.pkeys(po + h // 2)
                    for cc in range(2):
                        rows = slice(cc * 64, (cc + 1) * 64)
                        self.mm(ov[rows, :], sm[:, tt, 0, h, rows], vh, True, False, [("sm", tt), ("vtok", tt)], wk_)
                        self.mm(ov[rows, :], sm[:, tt, 1, h, rows], vh, False, False, [("sm", tt), ("vtok", tt)], wk_)
                        self.mm(ov[rows, :], qdec[:, tt, 0, h, rows], Sfb[:, cc, h, :], False, False, [("qdec", tt), gC], wk_)
                        self.mm(ov[rows, :], qdec[:, tt, 1, h, rows], Sbst[:, 2 * tt + cc, h, :], False, True,
                                [("qdec", tt), ("Sbst", 2 * tt + cc)], wk_)
                pr = self.pair()
                for nh in range(2):
                    for kc in range(8):
                        self.mm(self.psf(pr + nh), self.hT[:, kc, tok], w2[:, kc, nh * 512:(nh + 1) * 512], kc == 0, kc == 7,
                                [k2, ("hT", t)], self.pkeys(pr + nh))
                rb = self.tmp_next("tmpf", 2)
                G2 = self.tmpf[:, rb, :]
                self.act(G2, self.psf(pr, 2), AF.Silu, self.pkeys(pr, 2), [("tmpf", rb)])
                self.tt("dve", G2, G2, self.brow[:, 0, :], ALU.mult, [("tmpf", rb), ("brow", 0)], [("tmpf", rb)])
                si = self.tmp_next("ssq", 2)
                jb = self.tmp_next("tmpf", 2)
                for h in range(4):
                    self.S.op("act", lambda e, o=self.tmpf[:, jb, h * 256:(h + 1) * 256], i=self.psf(po, 2)[:, h * 256:(h + 1) * 256],
                              a=ssq[:, si, h:h + 1]: e.activation(o, i, AF.Square, accum_out=a),
                              self.pkeys(po + h // 2), [("tmpf", jb), ("ssq", si)])
                self.act(ssq[:, si, :], ssq[:, si, :], AF.Sqrt, [("ssq", si)], [("ssq", si)], bias=float(RMS_EPS), scale=1.0 / DV)
                self.S.op("dve", lambda e, o=ssq[:, si, :]: e.reciprocal(o, o), [("ssq", si)], [("ssq", si)])
                hbk = self.tmp_next("hb", 2)
                for h in range(4):
                    self.stt("dve", self.hb[:, hbk, h * 256:(h + 1) * 256], self.psf(po, 2)[:, h * 256:(h + 1) * 256], ssq[:, si, h:h + 1],
                             G2[:, h * 256:(h + 1) * 256], ALU.mult, ALU.mult,
                             self.pkeys(po + h // 2) + [("ssq", si), ("tmpf", rb)], [("hb", hbk)])
                self.transpose_tile(self.hb[:, hbk, :], ("hb", hbk), self.hT, "hT", t)
            if sample and self.debug and "gSb" in self.dbg:
                self.dma("sp", self.dbg.pop("gSb"), Sbst, [("Sbst", c) for c in range(8)], [("dbg", 9)], chan="dbg9")
            if not sample:
                self.dma("sp", self.ns_out[seq, j, 0].rearrange("h d e -> d h e"), S_f, [gA], [("nso", seq, 0)], chan=("nso", 0))
                self.dma("sp", self.ns_out[seq, j, 1].rearrange("h d e -> d h e"), S_b, [gB], [("nso", seq, 1)], chan=("nso", 1))

        segs = [([0, 1], 0), ([2, 3], 1), ([4, 5, 6, 7], None)]
        go_blk = []
        self.early_post = set()

        def wo_tile(t):
            wo, ko, io = go_blk[0]
            b = self.pair()
            for nh in range(2):
                for kc in range(8):
                    self.mm(self.psf(b + nh), self.hT[:, kc, t * 128:(t + 1) * 128], wo[:, kc, nh * 512:(nh + 1) * 512],
                            kc == 0, kc == 7, [ko, ("hT", t)], self.pkeys(b + nh))
            if t == NT - 1:
                self.ring_release(io)
            return self.psf(b, 2), self.pkeys(b, 2)

        def mid_exchange():
            if not go_blk:
                return
            for t in range(4):
                ap, keys = wo_tile(t)
                self.post_tile(t, ap, keys)
                self.early_post.add(t)
            self.post_group(range(4))
        import os
        lvl = int(os.environ.get("GLA_STOP", "9"))
        if lvl == 1:
            segs = []
        elif lvl == 2:
            phase1(0, 0)
            segs = []
        elif lvl == 3:
            segs = segs[:1]
        elif lvl == 4:
            segs = segs[:2]
        for si_, (tiles, seq) in enumerate(segs):
            for tt, t in enumerate(tiles):
                phase1(tt, t)
            if si_ == len(segs) - 1:
                self.ring_release(i0)
                self.ring_release(i1)
                if len(segs) == 3:
                    go_blk.append(self.ring_take(("go", j)))
            phase2(tiles, seq)
        if not segs:
            self.ring_release(i0)
            self.ring_release(i1)
        self.ring_release(i2)
        if self.debug and "gyT" in self.dbg:
            self.dma("sp", self.dbg.pop("gyT"), self.hT[:], [("hT", t) for t in range(NT)], [("dbg", 8)], chan="dbg8")
        if not go_blk:
            go_blk.append(self.ring_take(("go", j)))
        self.single_pool, self.pair_pool = list(range(8)), [0, 2, 4, 6]
        return wo_tile

    def memset(self, eng, ap, val, writes):
        return self.S.op(eng, lambda e: e.memset(ap, val), [], writes)

    def conf_core(self):
        pv = self.pv
        cv = self.view(0, [128, 8, 1024], F32)
        upad = self.view(32 * 1024, [128, 2, 1324], BF16)
        dg = self.view(38 * 1024, [128, 8, 128], BF16)
        sig = self.view(44 * 1024, [128, 2, 512], F32)
        sq = self.view(48 * 1024, [128, 2, 512], F32)
        mrow = self.view(52 * 1024, [128, 2, 512], F32)
        rrow = self.view(56 * 1024, [128, 2, 512], F32)
        vtmp = self.view(60 * 1024, [128, 512], F32)
        wa, ka, ia = self.ring_take(("cpw1", 0))
        self.gate_rows()
        wg, kg, ig = self.ring_take(("cpw1", 1))
        for ub in range(2):
            self.memset("pool", upad[:, ub, :], 0.0, [("upad", ub)])
        hkeys = lambda half: [("hT", half * 4 + i) for i in range(4)]

        def uview(ub, half, off, L):
            if half == 0:
                return upad[:, ub, 0:572].rearrange("p (s w) -> p s w", s=2)[:, :, off:off + L]
            return upad[:, ub, 572:1324].rearrange("p (s w) -> p s w", s=8)[:, :, off:off + L]

        def glu(j):
            ub = j % 2
            for half in range(2):
                pa = self.bank()
                for kc in range(8):
                    self.mm(self.psf(pa), wa[:, kc, j * 128:(j + 1) * 128], self.hT[:, kc, half * 512:(half + 1) * 512],
                            kc == 0, kc == 7, [ka] + hkeys(half), self.pkeys(pa))
                pg = self.bank()
                for kc in range(8):
                    self.mm(self.psf(pg), wg[:, kc, j * 128:(j + 1) * 128], self.hT[:, kc, half * 512:(half + 1) * 512],
                            kc == 0, kc == 7, [kg] + hkeys(half), self.pkeys(pg))
                si = self.tmp_next("sig", 2)
                self.act(sig[:, si, :], self.psf(pg), AF.Sigmoid, self.pkeys(pg) + [("pv",)], [("sig", si)], bias=pv[:, 8 + j:9 + j])
                s_ = 2 if half == 0 else 8
                self.stt("dve", uview(ub, half, 15, 512 // s_), self.psf(pa).rearrange("p (s w) -> p s w", s=s_), pv[:, j:j + 1],
                         sig[:, si, :].rearrange("p (s w) -> p s w", s=s_), ALU.add, ALU.mult,
                         self.pkeys(pa) + [("sig", si), ("pv",), ("upad", ub)], [("upad", ub, half)])

        def conv(j):
            ub = j % 2
            pcs = [self.bank(), self.bank()]
            for k in range(CONF_W):
                di = self.tmp_next("dg", 8)
                wcol = pv[:, 16 + k * 8 + j:17 + k * 8 + j]
                if k % 2 == 0:
                    self.ts("dve", dg[:, di, :], self.cstb[:, 0, :], wcol, None, ALU.mult, None, [("cstb",), ("pv",)], [("dg", di)])
                else:
                    self.act(dg[:, di, :], self.cstb[:, 0, :], AF.Identity, [("cstb",), ("pv",)], [("dg", di)], scale=wcol)
                for half in range(2):
                    self.mm(self.psf(pcs[half]), dg[:, di, :], uview(ub, half, k, 256 if half == 0 else 64), k == 0, k == CONF_W - 1,
                            [("dg", di), ("upad", ub, half), ("upad", ub)], self.pkeys(pcs[half]))
            for half in range(2):
                self.act(cv[:, j, half * 512:(half + 1) * 512], self.psf(pcs[half]), AF.Identity, self.pkeys(pcs[half]) + [("pv",)],
                         [("cv", j, half)], bias=pv[:, 264 + j:265 + j])

        for j in range(8):
            glu(j)
            if j > 0:
                conv(j - 1)
        self.ring_release(ia)
        self.ring_release(ig)
        conv(7)
        if self.debug:
            self.dma("sp", self.dbg["ccv"], cv, [("cv", j, h) for j in range(8) for h in range(2)], [("dbg", 5)], chan="dbg5")
            self.dma("sp", self.dbg["chin"], self.hT[:], [("hT", t) for t in range(NT)], [("dbg", 7)], chan="dbg7")
        ones = self.cst[:, 8, :]
        for half in range(2):
            b1 = self.bank()
            for j in range(8):
                self.mm(self.psf(b1), ones, cv[:, j, half * 512:(half + 1) * 512], j == 0, j == 7,
                        [("cst",), ("cv", j, half)], self.pkeys(b1))
            b2 = self.bank()
            for j in range(8):
                qi = self.tmp_next("sq", 2)
                self.act(sq[:, qi, :], cv[:, j, half * 512:(half + 1) * 512], AF.Square, [("cv", j, half)], [("sq", qi)])
                self.mm(self.psf(b2), ones, sq[:, qi, :], j == 0, j == 7, [("cst",), ("sq", qi)], self.pkeys(b2))
            mr, rr = mrow[:, half, :], rrow[:, half, :]
            self.act(mr, self.psf(b1), AF.Identity, self.pkeys(b1), [("mrow", half)], scale=1.0 / D)
            self.tt("dve", vtmp, mr, mr, ALU.mult, [("mrow", half)], [("vtmp",)])
            self.stt("dve", vtmp, self.psf(b2), 1.0 / D, vtmp, ALU.mult, ALU.subtract, self.pkeys(b2) + [("vtmp",)], [("vtmp",)])
            self.act(rr, vtmp, AF.Sqrt, [("vtmp",)], [("rrow", half)], bias=float(LN_EPS))
            self.S.op("dve", lambda e, o=rr: e.reciprocal(o, o), [("rrow", half)], [("rrow", half)])
            self.stt("dve", mr, mr, -1.0, rr, ALU.mult, ALU.mult, [("mrow", half), ("rrow", half)], [("mrow", half)])
            for j in range(8):
                c_ = cv[:, j, half * 512:(half + 1) * 512]
                self.tt("dve", c_, c_, rr, ALU.mult, [("cv", j, half), ("rrow", half)], [("cv", j, half)])
                self.tt("dve", c_, c_, mr, ALU.add, [("cv", j, half), ("mrow", half)], [("cv", j, half)])
                self.act(self.hT[:, j, half * 512:(half + 1) * 512], c_, AF.Silu,
                         [("cv", j, half), ("pv",)], [("hT", half * 4 + i) for i in range(4)],
                         bias=pv[:, 280 + j:281 + j], scale=pv[:, 272 + j:273 + j])
        if self.debug:
            self.dma("sp", self.dbg["chT"], self.hT[:], [("hT", t) for t in range(NT)], [("dbg", 6)], chan="dbg6")
        w2, k2, i2 = self.ring_take(("cpw2",))
        brow2 = self.view(0, [128, D], F32)
        self.dma("sp", brow2, self.w["conf_b_pw2"][0, :].partition_broadcast(128),
                 [], [("cv", j, h) for j in range(8) for h in range(2)], chan="cb2")

        def src(t):
            b = self.pair()
            for nh in range(2):
                for kc in range(8):
                    self.mm(self.psf(b + nh), self.hT[:, kc, t * 128:(t + 1) * 128], w2[:, kc, nh * 512:(nh + 1) * 512],
                            kc == 0, kc == 7, [k2, ("hT", t)], self.pkeys(b + nh))
            if t == NT - 1:
                self.ring_release(i2)
            tb = self.tmp_next("tmpf", 2)
            self.tt("dve", self.tmpf[:, tb, :], self.psf(b, 2), brow2, ALU.add,
                    self.pkeys(b, 2) + [("cv", 0, 0)], [("tmpf", tb)])
            return self.tmpf[:, tb, :], [("tmpf", tb)]
        return src

    def sconv_core(self):
        pv = self.pv
        Pp = self.view(0, [128, 8, 2, 258], F32)
        Ps = self.view(16512, [128, 8, 640], F32)
        vT = self.view(36992, [128, 8, 1024], BF16)
        hst = self.view(53376, [128, 2, 8, 64], F32)
        hrc = self.view(57472, [128, 2, 8, 64], F32)
        ctmp = self.view(61568, [128, 512], F32)
        utmp = self.view(63616, [128, 480], F32)
        utmp = self.tmpf
        hkeys = lambda half: [("hT", half * 4 + i) for i in range(4)]
        wc, kc_, ic = self.ring_take(("sin", 1))
        self.gate_rows()
        wu, ku, iu = self.ring_take(("sin", 2))
        self.memset("pool", Pp, 0.0, [("Pp", j) for j in range(8)])
        self.memset("pool", Ps[:, :, 0:64], 0.0, [("halo", 0)])
        self.memset("pool", Ps[:, :, 576:640], 0.0, [("halo", 1)])
        for j in range(8):
            for half in range(2):
                pc = self.bank()
                for kc in range(8):
                    self.mm(self.psf(pc), wc[:, kc, j * 128:(j + 1) * 128], self.hT[:, kc, half * 512:(half + 1) * 512],
                            kc == 0, kc == 7, [kc_] + hkeys(half), self.pkeys(pc))
                pu = self.bank()
                for kc in range(8):
                    self.mm(self.psf(pu), wu[:, kc, j * 128:(j + 1) * 128], self.hT[:, kc, half * 512:(half + 1) * 512],
                            kc == 0, kc == 7, [ku] + hkeys(half), self.pkeys(pu))
                tb = self.tmp_next("tmpf", 2)
                self.cp("act", utmp[:, tb, 0:512], self.psf(pu), self.pkeys(pu), [("tmpf", tb)])
                if half == 0:
                    self.tt("dve", Pp[:, j, :, 1:257], self.psf(pc).rearrange("p (s w) -> p s w", s=2),
                            utmp[:, tb, 0:512].rearrange("p (s w) -> p s w", s=2), ALU.mult,
                            self.pkeys(pc) + [("tmpf", tb)], [("Pp", j)])
                else:
                    self.tt("dve", Ps[:, j, 64:576], self.psf(pc), utmp[:, tb, 0:512], ALU.mult,
                            self.pkeys(pc) + [("tmpf", tb)], [("Ps", j)])
        self.ring_release(ic)
        self.ring_release(iu)
        wb, kb, ib = self.ring_take(("sin", 0))
        allPs = [("Ps", j) for j in range(8)]
        self.cp("pool", hst[:, 0], Ps[:, :, 64:128], allPs, [("hst",)])
        self.cp("pool", hst[:, 1], Ps[:, :, 512:576], allPs, [("hst",)])
        self.dma("sp", self.halo_src.ap(), hst.rearrange("p a j w -> p (a j w)"), [("hst",)], [("halo_src",)], chan="hs")
        self.S.op("pool", lambda e: e.collective_compute("AllGather", ALU.bypass, replica_groups=[[0, 1, 2, 3], [4, 5, 6, 7]],
                                                         ins=[self.halo_src.ap()], outs=[self.halo_dst.ap()]),
                  [("halo_src",)], [("halo_dst",)], is_dma=True, chan="hcc", inc=1)
        def recv_halo():
            hd = self.halo_dst.ap().rearrange("(r p) (a j w) -> r p a j w", p=128, a=2, j=8)
            for i in range(4):
                for side, a_idx, mcol, dst in ((0, 1, 8 + i, Ps[:, :, 0:64]), (1, 0, 12 + i, Ps[:, :, 576:640])):
                    ri = self.tmp_next("hrc", 2)
                    self.dma("sp", hrc[:, ri], hd[i, :, a_idx], [("halo_dst",)], [("hrc", ri)], chan=("hrc", ri))
                    self.stt("dve", dst, hrc[:, ri], self.cm[:, mcol:mcol + 1], dst, ALU.mult, ALU.add,
                             [("hrc", ri), ("cm",), ("halo", side)], [("halo", side)])
        for half in range(2):
            if half == 1:
                recv_halo()
            for j in range(8):
                pb = self.bank()
                for kc in range(8):
                    self.mm(self.psf(pb), wb[:, kc, j * 128:(j + 1) * 128], self.hT[:, kc, half * 512:(half + 1) * 512],
                            kc == 0, kc == 7, [kb] + hkeys(half), self.pkeys(pb))
                w_ = lambda k: pv[:, 288 + k * 8 + j:289 + k * 8 + j]
                if half == 0:
                    cview = ctmp.rearrange("p (s w) -> p s w", s=2)
                    srcs = [Pp[:, j, :, k:k + 256] for k in range(3)]
                    rk = [("Pp", j), ("pv",)]
                    pbv = self.psf(pb).rearrange("p (s w) -> p s w", s=2)
                    outv = vT[:, j, 0:512].rearrange("p (s w) -> p s w", s=2)
                else:
                    cview = ctmp
                    srcs = [Ps[:, j, 64 * k:64 * k + 512] for k in range(3)]
                    rk = [("Ps", j), ("halo", 0), ("halo", 1), ("pv",)]
                    pbv = self.psf(pb)
                    outv = vT[:, j, 512:1024]
                self.ts("dve", cview, srcs[0], w_(0), None, ALU.mult, None, rk, [("ctmp",)])
                self.stt("dve", cview, srcs[1], w_(1), cview, ALU.mult, ALU.add, rk + [("ctmp",)], [("ctmp",)])
                self.stt("dve", cview, srcs[2], w_(2), cview, ALU.mult, ALU.add, rk + [("ctmp",)], [("ctmp",)])
                self.tt("dve", outv, pbv, cview, ALU.mult, self.pkeys(pb) + [("ctmp",)], [("vT", j, half)])
        self.ring_release(ib)
        wo, ko, io = self.ring_take(("sout",))

        def src(t):
            b = self.pair()
            for nh in range(2):
                for kc in range(8):
                    self.mm(self.psf(b + nh), vT[:, kc, t * 128:(t + 1) * 128], wo[:, kc, nh * 512:(nh + 1) * 512],
                            kc == 0, kc == 7, [ko, ("vT", kc, t // 4)], self.pkeys(b + nh))
            if t == NT - 1:
                self.ring_release(io)
            return self.psf(b, 2), self.pkeys(b, 2)
        return src


WEIGHT_SHAPES = {
    "mod_w": (4, 1024, 6144), "mod_b": (4, 6144), "ln_g": (4, 2, 1024), "ln_b": (4, 2, 1024),
    "ff_w1": (4, 1024, 4096), "ff_w2": (4, 4096, 1024),
    "gla_w_in": (2, 1024, 3072), "gla_w_ga": (2, 2, 1024, 16), "gla_w_gb": (2, 2, 16, 512),
    "gla_b_g": (2, 2, 512), "gla_gn_g": (2, 1024), "gla_w_o": (2, 1024, 1024),
    "conf_w_pw1": (1, 1024, 2048), "conf_w_pw2": (1, 1024, 1024), "conf_b_pw2": (1, 1024),
    "sc_w_in": (1, 1024, 3072), "sc_w_out": (1, 1024, 1024),
}


def make_consts():
    j = np.arange(128)[:, None]
    i = np.arange(128)[None, :]
    same = (j // 64) == (i // 64)
    c = np.zeros((128, 9, 128), np.float32)
    c[:, 8, :] = 1.0
    c[:, 0, :] = np.eye(128, dtype=np.float32)
    c[:, 1, :] = (same & (j <= i)).astype(np.float32)
    c[:, 2, :] = (same & (j >= i)).astype(np.float32)
    c[:, 3, :] = (same & (j <= i)).astype(np.float32) * (-1.0 / 16.0)
    c[:, 4, :] = (same & (j >= i)).astype(np.float32) * (-1.0 / 16.0)
    c[:, 5, :] = (same & (j > i)).astype(np.float32) * (-1.0 / 16.0)
    c[:, 6, :] = (same & (j < i)).astype(np.float32) * (-1.0 / 16.0)
    c[:, 7, 0] = (np.arange(128) < 64) * (-1.0 / 16.0)
    c[:, 7, 1] = (np.arange(128) >= 64) * (-1.0 / 16.0)
    return c


def make_in_maps(inp):
    f = lambda a: np.ascontiguousarray(np.asarray(a, dtype=np.float32))
    xp = f(inp["x_prompt"])
    xs = f(inp["x_sample"])
    c = f(inp["c"])
    cctx = f(inp["c_ctx"])
    st = f(inp["state_gla"])
    consts = make_consts()
    fm = lambda v: np.ascontiguousarray(v.reshape(-1, 128).T)
    pvec = np.zeros((128, 320), np.float32)
    pvec[:, 0:16] = fm(f(inp["conf_b_pw1"])[0])
    wdw = f(inp["conf_w_dw"])[0]
    pvec[:, 16:16 + 248] = np.concatenate([fm(wdw[k]) for k in range(CONF_W)], axis=1)
    pvec[:, 264:272] = fm(f(inp["conf_b_dw"])[0])
    pvec[:, 272:280] = fm(f(inp["conf_ln_g"])[0])
    pvec[:, 280:288] = fm(f(inp["conf_ln_b"])[0])
    wsc = f(inp["sc_w_conv"])[0]
    pvec[:, 288:312] = np.concatenate([fm(wsc[k]) for k in range(3)], axis=1)
    shared = {name: f(inp[name]) for name in WEIGHT_SHAPES}
    maps = []
    for r in range(8):
        b, p = r // 4, r % 4
        x_in = np.concatenate([xp[2 * r], xp[2 * r + 1], xs[b, 512 * p:512 * (p + 1)]], axis=0)
        cv = np.stack([cctx, c[b]], axis=0)
        cvecT = np.ascontiguousarray(cv.reshape(2, 8, 128).transpose(2, 0, 1))
        cm = np.zeros((128, 16), np.float32)
        for i in range(4):
            cm[:, i] = 1.0 if i < p else 0.0
            cm[:, 4 + i] = 1.0 if i > p else 0.0
            cm[:, 8 + i] = 1.0 if i == p - 1 else 0.0
            cm[:, 12 + i] = 1.0 if i == p + 1 else 0.0
        m = dict(shared)
        m.update({"x_in": np.ascontiguousarray(x_in), "cvecT": cvecT, "cmask": cm, "consts": consts,
                  "state0": np.ascontiguousarray(st[b]), "pvec": pvec})
        maps.append(m)
    return maps


_NC_CACHE = {}


def run(inp, n_sub=8, skip_mixers=False, trace=False, debug=False, skip_kinds=()):
    key = (n_sub, skip_mixers, debug, tuple(skip_kinds))
    if key not in _NC_CACHE:
        _NC_CACHE[key] = Builder(n_sub, skip_mixers, debug, skip_kinds).run()
    nc = _NC_CACHE[key]
    maps = make_in_maps(inp)
    res = run_bass_kernel_spmd(nc, maps, core_ids=list(range(8)), **({"trace": True} if trace else {}))
    yp = np.zeros((16, 256, D), np.float32)
    ys = np.zeros((2, 2048, D), np.float32)
    ns = np.zeros((16, 2, 2, NH, DK, DV), np.float32)
    for r in range(8):
        b, p = r // 4, r % 4
        y = res.results[r]["y_out"]
        yp[2 * r] = y[0:256]
        yp[2 * r + 1] = y[256:512]
        ys[b, 512 * p:512 * (p + 1)] = y[512:1024]
        ns[2 * r:2 * r + 2] = res.results[r]["ns_out"]
    return (yp, ys, ns), res


def kernel(**inputs):
    outs, _ = run(inputs)
    return outs
```

```python
import contextlib
import numpy as np
import ml_dtypes
import concourse.bass as bass
import concourse.mybir as mybir
from concourse.bass_utils import run_bass_kernel_spmd

F32 = mybir.dt.float32
BF16 = mybir.dt.bfloat16
ALU = mybir.AluOpType
AF = mybir.ActivationFunctionType

D = 1024
NT = 8
DEPTH = 4
ALPHA = (2 * DEPTH) ** 0.25
LN_EPS = 1e-5
RMS_EPS = 1e-6
DK = 128
DV = 256
NH = 4
CONF_W = 31
SAME_ENGINE_SYNC = True


class Op:
    __slots__ = ("eng", "fn", "deps", "is_dma", "chan", "signal", "ev", "idx", "inc", "epoch")

    def __init__(self, eng, fn, is_dma=False, chan=None, inc=16):
        self.inc = inc
        self.eng = eng
        self.fn = fn
        self.deps = []
        self.is_dma = is_dma
        self.chan = chan
        self.signal = False
        self.ev = None
        self.idx = None


class Sched:
    ENGS = ("pe", "act", "dve", "pool", "sp")

    def __init__(self):
        self.ops = []
        self.last_writer = {}
        self.readers = {}
        self.chan_count = {}
        self.epoch = 0

    def op(self, eng, fn, reads=(), writes=(), is_dma=False, chan=None, inc=16):
        o = Op(eng, fn, is_dma, chan, inc)
        o.idx = len(self.ops)
        o.epoch = self.epoch
        deps = {}
        for k in reads:
            w = self.last_writer.get(k)
            if w is not None:
                deps[w.idx] = w
        for k in writes:
            w = self.last_writer.get(k)
            if w is not None:
                deps[w.idx] = w
            for r in self.readers.get(k, ()):
                deps[r.idx] = r
        deps.pop(o.idx, None)
        o.deps = list(deps.values())
        for k in writes:
            self.last_writer[k] = o
            self.readers[k] = []
        for k in reads:
            self.readers.setdefault(k, []).append(o)
        if is_dma:
            c = self.chan_count.get(chan, 0) + inc
            self.chan_count[chan] = c
            o.ev = (("dma", chan), c)
        self.ops.append(o)
        return o

    def barrier(self, main_fn):
        last = {}
        skip = ()
        for o in self.ops:
            if o.is_dma and ((isinstance(o.chan, tuple) and o.chan[0] in skip) or (isinstance(o.chan, str) and o.chan.startswith("misc"))):
                continue
            last[("dma", o.chan) if o.is_dma else ("eng", o.eng)] = o
        B = self.op("dve", main_fn, [], [("barrier",)])
        have = {d.idx for d in B.deps}
        for o in last.values():
            if o.idx not in have and o is not B:
                B.deps.append(o)
        for eng in ("pe", "act", "pool", "sp"):
            self.op(eng, lambda e: e.nop(nofuse=True), [("barrier",)], [])

    def finalize(self):
        for o in self.ops:
            for d in o.deps:
                if d.is_dma:
                    continue
                if d.eng == o.eng and not o.is_dma:
                    if d.eng == "pe" or not SAME_ENGINE_SYNC:
                        continue
                d.signal = True
        cnt = {}
        for o in self.ops:
            if o.is_dma:
                continue
            if o.signal:
                k = ("eng", o.eng, o.epoch)
                cnt[k] = cnt.get(k, 0) + 1
                o.ev = (k, cnt[k])
        return cnt

    def emit(self, eng_name, eng, sems):
        known = {}
        for o in self.ops:
            if o.eng != eng_name:
                continue
            for d in o.deps:
                if d.ev is None:
                    continue
                if (not d.is_dma) and d.eng == o.eng and not o.is_dma:
                    if d.eng == "pe" or not SAME_ENGINE_SYNC:
                        continue
                key, val = d.ev
                if known.get(key, 0) >= val:
                    continue
                eng.wait_ge(sems[key], val)
                known[key] = val
            ins = o.fn(eng)
            if o.is_dma:
                ins.then_inc(sems[o.ev[0]], o.inc)
            elif o.signal:
                ins.then_inc(sems[o.ev[0]], 1)

    def final_waits(self, eng, sems):
        for chan, c in self.chan_count.items():
            eng.wait_ge(sems[("dma", chan)], c)


class Builder:
    def __init__(self, n_sub=8, skip_mixers=False, debug=False, skip_kinds=()):
        self.debug = debug
        self.skip_kinds = set(skip_kinds)
        self.n_sub = n_sub
        self.skip_mixers = skip_mixers
        self.S = Sched()
        self.nc = bass.Bass("TRN2", target_bir_lowering=False)
        self.bank_rr = 0
        self.pair_rr = 0
        self.ring_next_load = 0
        self.ring_next_use = 0
        self.ring_released = set()
        self.ring_pending = []
        self.tmp_rr = {}

    def declare(self):
        nc = self.nc
        di = lambda name, shape, dt=F32: nc.dram_tensor(name, list(shape), dt, kind="ExternalInput").ap()
        self.x_in = di("x_in", [1024, D])
        self.cvecT = di("cvecT", [128, 2, 8])
        self.cmask = di("cmask", [128, 16])
        self.consts = di("consts", [128, 9, 128])
        self.state0 = di("state0", [2, 2, NH, DK, DV])
        self.pvec = di("pvec", [128, 320])
        self.w = {}
        for name, shape in WEIGHT_SHAPES.items():
            self.w[name] = di(name, shape)
        self.y_out = nc.dram_tensor("y_out", [1024, D], F32, kind="ExternalOutput").ap()
        self.ns_out = nc.dram_tensor("ns_out", [2, 2, 2, NH, DK, DV], F32, kind="ExternalOutput").ap()
        self.dbg = {}
        if self.debug:
            self.dbg["hT"] = nc.dram_tensor("dbg_hT", [128, 8, 1024], BF16, kind="ExternalOutput").ap()
            self.dbg["rows"] = nc.dram_tensor("dbg_rows", [128, 4, 1024], F32, kind="ExternalOutput").ap()
            self.dbg["acc"] = nc.dram_tensor("dbg_acc", [128, 8, 1024], F32, kind="ExternalOutput").ap()
            self.dbg["gSb"] = nc.dram_tensor("dbg_gSb", [128, 8, 4, 256], BF16, kind="ExternalOutput").ap()
            self.dbg["gyT"] = nc.dram_tensor("dbg_gyT", [128, 8, 1024], BF16, kind="ExternalOutput").ap()
            self.dbg["ccv"] = nc.dram_tensor("dbg_ccv", [128, 8, 1024], F32, kind="ExternalOutput").ap()
            self.dbg["chT"] = nc.dram_tensor("dbg_chT", [128, 8, 1024], BF16, kind="ExternalOutput").ap()
            self.dbg["chin"] = nc.dram_tensor("dbg_chin", [128, 8, 1024], BF16, kind="ExternalOutput").ap()
        self.agg_src = [[nc.dram_tensor(f"agg_src{j}_{z}", [128, 1032], F32) for z in range(2)] for j in range(2)]
        self.agg_dst = [[nc.dram_tensor(f"agg_dst{j}_{z}", [4 * 128, 1032], F32) for z in range(2)] for j in range(2)]
        self.halo_src = nc.dram_tensor("halo_src", [128, 2 * 8 * 64], F32)
        self.halo_dst = nc.dram_tensor("halo_dst", [4 * 128, 2 * 8 * 64], F32)

    def view(self, off_bytes, shape, dt):
        esz = 4 if dt == F32 else 2
        n = int(np.prod(shape[1:]))
        assert off_bytes % 4 == 0
        assert off_bytes + n * esz <= self.SCR_BYTES, (off_bytes, n * esz, self.SCR_BYTES)
        a = self.scr[:, off_bytes // 4: off_bytes // 4 + (n * esz + 3) // 4]
        if dt != F32:
            a = a.bitcast(dt)
        if len(shape) == 2:
            return a
        names = " ".join(f"d{i}" for i in range(1, len(shape)))
        kw = {f"d{i}": shape[i] for i in range(1, len(shape))}
        return a.rearrange(f"p ({names}) -> p {names}", **kw)

    single_pool = list(range(8))
    pair_pool = [0, 2, 4, 6]

    def bank(self):
        self.bank_rr = (self.bank_rr + 1) % len(self.single_pool)
        return self.single_pool[self.bank_rr]

    def pair(self):
        self.pair_rr = (self.pair_rr + 1) % len(self.pair_pool)
        return self.pair_pool[self.pair_rr]

    def psf(self, b, n=1):
        return self.ps[:, b * 512:(b + n) * 512]

    def psb(self, b):
        return self.ps[:, b * 512:(b + 1) * 512].bitcast(BF16)

    def pkeys(self, b, n=1):
        return [("ps", b + i) for i in range(n)]

    def mm(self, out, lhsT, rhs, start, stop, reads, writes):
        return self.S.op("pe", lambda e: e.matmul(out, lhsT, rhs, start=start, stop=stop), reads, writes)

    def tr(self, out, in_, ident, reads, writes):
        return self.S.op("pe", lambda e: e.transpose(out, in_, ident), reads, writes)

    def act(self, out, in_, func, reads, writes, bias=None, scale=None):
        kw = {}
        if bias is not None:
            kw["bias"] = bias
        if scale is not None:
            kw["scale"] = scale
        return self.S.op("act", lambda e: e.activation(out, in_, func, **kw), reads, writes)

    def tt(self, eng, out, in0, in1, op, reads, writes):
        return self.S.op(eng, lambda e: e.tensor_tensor(out, in0, in1, op), reads, writes)

    def ts(self, eng, out, in0, s1, s2, op0, op1, reads, writes):
        if op1 is None:
            return self.S.op(eng, lambda e: e.tensor_scalar(out, in0, s1, None, op0), reads, writes)
        return self.S.op(eng, lambda e: e.tensor_scalar(out, in0, s1, s2, op0, op1), reads, writes)

    def stt(self, eng, out, in0, scalar, in1, op0, op1, reads, writes):
        return self.S.op(eng, lambda e: e.scalar_tensor_tensor(out, in0, scalar, in1, op0, op1), reads, writes)

    def cp(self, eng, out, in_, reads, writes):
        if eng == "act":
            return self.S.op("act", lambda e: e.copy(out, in_), reads, writes)
        return self.S.op(eng, lambda e: e.tensor_copy(out, in_), reads, writes)

    def dma(self, q, out, in_, reads, writes, chan):
        return self.S.op(q, lambda e: e.dma_start(out=out, in_=in_), reads, writes, is_dma=True, chan=chan)

    NSLOT = 3

    def build_stream(self):
        w = self.w
        st = []
        sub = 0
        for l in range(DEPTH):
            for s in range(2):
                if sub >= self.n_sub:
                    break
                sub += 1
                if s == 0 and (self.skip_mixers or (l % 3) in self.skip_kinds):
                    continue
                mod = lambda j: (("mod", l, j), w["mod_w"][l, :, j * 1024:(j + 1) * 1024])
                st += [mod(3 * s), mod(3 * s + 1)]
                core = []
                if s == 0:
                    kind, j = l % 3, l // 3
                    if kind == 0:
                        core += [(("gin", j, i), w["gla_w_in"][j, :, i * 1024:(i + 1) * 1024]) for i in range(3)]
                        core += [(("go", j), w["gla_w_o"][j])]
                    elif kind == 1:
                        core += [(("cpw1", i), w["conf_w_pw1"][0, :, i * 1024:(i + 1) * 1024]) for i in range(2)]
                        core += [(("cpw2",), w["conf_w_pw2"][0])]
                    else:
                        core += [(("sin", i), w["sc_w_in"][0, :, i * 1024:(i + 1) * 1024]) for i in (1, 2, 0)]
                        core += [(("sout",), w["sc_w_out"][0])]
                else:
                    for b in range(4):
                        core += [(("w1", l, b), w["ff_w1"][l, :, b * 1024:(b + 1) * 1024]),
                                 (("w2", l, b), w["ff_w2"][l, b * 1024:(b + 1) * 1024, :])]
                if s == 0 and l % 3 == 0:
                    st += [mod(3 * s + 2)] + core
                else:
                    st += [core[0], mod(3 * s + 2)] + core[1:]
        self.ring_tags = [t for t, _ in st]
        self.ring_pending = [a for _, a in st]

    def ring_take(self, tag):
        idx = self.ring_next_use
        assert self.ring_tags[idx] == tag, (self.ring_tags[idx], tag)
        self.ring_next_use += 1
        if idx == 0:
            self.ring_issue_upto(self.NSLOT - 1)
        assert idx < self.ring_next_load
        slot = idx % self.NSLOT
        return self.ring[:, slot], ("ring", slot), idx

    def ring_release(self, idx):
        self.ring_released.add(idx)
        while (self.ring_next_load < len(self.ring_pending)
               and (self.ring_next_load - self.NSLOT) in self.ring_released):
            self.ring_issue_upto(self.ring_next_load)

    def ring_issue_upto(self, idx):
        while self.ring_next_load <= idx and self.ring_next_load < len(self.ring_pending):
            i = self.ring_next_load
            slot = i % self.NSLOT
            src = self.ring_pending[i].rearrange("(kc p) n -> p kc n", p=128)
            for q in range(4):
                self.dma("pool", self.ring[:, slot, 2 * q:2 * q + 2, :], src[:, 2 * q:2 * q + 2, :],
                         reads=[], writes=[("ring", slot)], chan=("ring", slot))
            self.ring_next_load += 1

    def run(self):
        nc = self.nc
        self.declare()
        with contextlib.ExitStack() as st:
            sb = lambda name, shape, dt: st.enter_context(nc.sbuf_tensor(name, list(shape), dt))
            self.x = sb("x", [128, NT, D], F32)
            self.hT = sb("hT", [128, 8, 1024], BF16)
            self.ring = sb("ring", [128, self.NSLOT, 8, 1024], BF16)
            self.rows = sb("rows", [128, 4, D], F32)
            self.brow = sb("brow", [128, 1, D], F32)
            self.gsm = sb("gsm", [128, 256], F32)
            self.SCR_BYTES = 64 * 1024
            self.scr = sb("scr", [128, self.SCR_BYTES // 4], F32)
            self.cst = sb("cst", [128, 9, 128], F32)
            self.cstb = sb("cstb", [128, 4, 128], BF16)
            self.cT = sb("cT", [128, 2, 8], F32)
            self.sT = sb("sT", [128, 2, 8], F32)
            self.sTrep = sb("sTrep", [128, 2, 8, 128], BF16)
            self.cm = sb("cm", [128, 16], F32)
            self.pv = sb("pv", [128, 320], F32)
            self.hb = sb("hb", [128, 2, D], BF16)
            self.tmpf = sb("tmpf", [128, 2, D], F32)
            self.stat = sb("stat", [128, 2, 16], F32)
            self.bst = sb("bst", [128, 2, 12], F32)
            self.ps = st.enter_context(nc.psum_tensor("ps", [128, 8 * 512], F32))

            self.program()

            cnt = self.S.finalize()
            import os
            if os.environ.get("KDEBUG"):
                print("SIGNAL COUNTS", max(cnt.values()), len(cnt), "n_ops", len(self.S.ops), "chan max", max(self.S.chan_count.values()), "n_chan", len(self.S.chan_count))
            sems = {}
            for k in cnt:
                sems[k] = st.enter_context(nc.semaphore(f"s_{k[1]}_{k[2]}"))
            for i, chan in enumerate(self.S.chan_count):
                sems[("dma", chan)] = st.enter_context(nc.semaphore(f"d_{i}"))
            block = st.enter_context(nc.Block())
            S = self.S

            @block.tensor
            def _(e):
                S.emit("pe", e, sems)

            @block.scalar
            def _(e):
                S.emit("act", e, sems)

            @block.vector
            def _(e):
                S.emit("dve", e, sems)

            @block.gpsimd
            def _(e):
                S.emit("pool", e, sems)
                for (kind, name), ap in []:
                    pass

            @block.sync
            def _(e):
                S.emit("sp", e, sems)
                S.final_waits(e, sems)
                for k, v in cnt.items():
                    e.wait_ge(sems[k], v)
        return nc

    def program(self):
        import os
        if os.environ.get("DMA_PROBE"):
            w = self.w
            nblk = int(os.environ["DMA_PROBE"])
            self.ring_tags = [("p", i) for i in range(nblk)]
            self.ring_pending = [w["ff_w1"][i % 4, :, (i // 4 % 4) * 1024:(i // 4 % 4 + 1) * 1024] for i in range(nblk)]
            for i in range(nblk):
                wv, wk, wi = self.ring_take(("p", i))
                b = self.bank()
                self.mm(self.psf(b)[:, 0:128], wv[:, 7, 0:128], wv[:, 7, 896:1024], True, True, [wk], self.pkeys(b))
                self.ring_release(wi)
            self.cp("act", self.x[:, 0, 0:128], self.psf(b)[:, 0:128], self.pkeys(b), [("x", 0)])
            self.epilogue()
            return
        self.build_stream()
        self.prologue()
        sub = 0
        for l in range(DEPTH):
            for s in range(2):
                if sub >= self.n_sub:
                    break
                if s == 0 and (self.skip_mixers or (l % 3) in self.skip_kinds):
                    sub += 1
                    continue
                self.S.epoch = sub
                self.sublayer(l, s)
                sub += 1
        self.epilogue()

    def prologue(self):
        for t in range(NT):
            self.dma("sp", self.x[:, t, :], self.x_in[t * 128:(t + 1) * 128, :], [], [("x", t)], chan=("x", t))
        self.dma("sp", self.cT[:], self.cvecT, [], [("cT",)], chan="misc0")
        self.dma("sp", self.cm[:], self.cmask, [], [("cm",)], chan="misc1")
        self.dma("sp", self.cst[:], self.consts, [], [("cst",)], chan="misc2")
        self.dma("sp", self.pv[:], self.pvec, [], [("pv",)], chan="misc3")
        self.cp("dve", self.cstb[:, 0:3, :], self.cst[:, 0:3, :], [("cst",)], [("cstb",)])
        self.act(self.sT[:], self.cT[:], AF.Silu, [("cT",)], [("sT",)])
        self.cp("dve", self.sTrep[:].rearrange("p s k m -> p (s k) m"),
                self.sT[:].rearrange("p s k -> p (s k)").unsqueeze(2).to_broadcast([128, 16, 128]),
                [("sT",)], [("sTrep",)])

    def epilogue(self):
        for t in range(NT):
            self.dma("sp", self.y_out[t * 128:(t + 1) * 128, :], self.x[:, t, :], [("x", t)], [("yout", t)],
                     chan=("yo", t % 2))

    def mod_rows(self, l, j, dsts, plus_one):
        wv, wkey, widx = self.ring_take(("mod", l, j))
        bslot = self.tmp_next("brow", 1)
        self.dma("sp", self.brow[:, bslot, :], self.w["mod_b"][l, j * 1024:(j + 1) * 1024].partition_broadcast(128),
                 [], [("brow", bslot)], chan=("brow", bslot))
        for s in range(2):
            b = self.pair()
            for half in range(2):
                for kc in range(8):
                    self.mm(self.psf(b + half), self.sTrep[:, s, kc, :], wv[:, kc, half * 512:(half + 1) * 512],
                            kc == 0, kc == 7, [("sTrep",), wkey], self.pkeys(b + half))
            if plus_one:
                self.stt("dve", self.rows[:, dsts[s], :], self.psf(b, 2), 1.0, self.brow[:, bslot, :], ALU.add, ALU.add,
                         self.pkeys(b, 2) + [("brow", bslot)], [("rows", dsts[s])])
            else:
                self.tt("dve", self.rows[:, dsts[s], :], self.psf(b, 2), self.brow[:, bslot, :], ALU.add,
                        self.pkeys(b, 2) + [("brow", bslot)], [("rows", dsts[s])])
        self.ring_release(widx)

    def tmp_next(self, name, n):
        v = self.tmp_rr.get(name, 0)
        self.tmp_rr[name] = (v + 1) % n
        return v

    def row_load(self, dst_idx, src_row_ap):
        self.dma("sp", self.rows[:, dst_idx, :], src_row_ap.partition_broadcast(128), [], [("rows", dst_idx)],
                 chan=("rows", dst_idx))

    def sublayer(self, l, s):
        self.mod_rows(l, 3 * s + 0, (0, 2), False)
        self.mod_rows(l, 3 * s + 1, (1, 3), True)
        for t in range(NT):
            st_ = 0 if t < 4 else 1
            tb = self.tmp_next("tmpf", 2)
            hbk = self.tmp_next("hb", 2)
            self.tt("dve", self.tmpf[:, tb, :], self.x[:, t, :], self.rows[:, 2 * st_ + 1, :], ALU.mult,
                    [("x", t), ("rows", 2 * st_ + 1)], [("tmpf", tb)])
            self.tt("dve", self.hb[:, hbk, :], self.tmpf[:, tb, :], self.rows[:, 2 * st_, :], ALU.add,
                    [("tmpf", tb), ("rows", 2 * st_)], [("hb", hbk)])
            self.transpose_tile(self.hb[:, hbk, :], ("hb", hbk), self.hT, "hT", t)
        if self.debug and "hT" in self.dbg:
            self.dma("sp", self.dbg.pop("hT"), self.hT[:], [("hT", t) for t in range(NT)], [("dbg", 0)], chan="dbg0")
            self.dma("sp", self.dbg.pop("rows"), self.rows[:], [("rows", i) for i in range(4)], [("dbg", 1)], chan="dbg1")
        self.gate_rows = lambda: self.mod_rows(l, 3 * s + 2, (0, 1), False)
        self.row_load(2, self.w["ln_g"][l, s, :])
        self.row_load(3, self.w["ln_b"][l, s, :])
        self.S.barrier(lambda e: e.memset(self.gsm[:, 255:256], 0.0))
        if s == 0:
            kind = l % 3
            if kind == 0:
                src = self.gla_core(l // 3)
            elif kind == 1:
                src = self.conf_core()
            else:
                src = self.sconv_core()
            early = getattr(self, "early_post", set())
            for t in range(NT):
                if t in early:
                    continue
                ap, keys = src(t)
                self.post_tile(t, ap, keys)
                if t % 4 == 3:
                    self.post_group(range(t - 3, t + 1))
            self.early_post = set()
        else:
            self.mlp_core(l)

    def transpose_tile(self, src, src_key, dstT, dst_name, t):
        b = self.bank()
        pv = self.psb(b)
        for kc in range(8):
            self.tr(pv[:, kc * 128:(kc + 1) * 128], src[:, kc * 128:(kc + 1) * 128], self.cstb[:, 0, :],
                    [src_key, ("cstb",)], self.pkeys(b))
        self.cp("act", dstT[:, :, t * 128:(t + 1) * 128], pv.rearrange("p (k m) -> p k m", k=8),
                self.pkeys(b), [(dst_name, t)])

    def post_tile(self, t, ap, keys):
        mv = self.stat[:, 0, :].rearrange("p (t c) -> p t c", c=2)
        st_ = 0 if t < 4 else 1
        tb = self.tmp_next("tmpf", 2)
        xt = self.x[:, t, :]
        self.tt("dve", self.tmpf[:, tb, :], ap, self.rows[:, st_, :], ALU.mult, keys + [("rows", st_)], [("tmpf", tb)])
        self.stt("dve", xt, xt, float(ALPHA), self.tmpf[:, tb, :], ALU.mult, ALU.add, [("x", t), ("tmpf", tb)], [("x", t)])
        sums = self.bst[:].rearrange("p a b -> p (a b)")[:, 0:16].rearrange("p (t c) -> p t c", c=2)
        junk = self.tmpf[:, tb, :]
        self.S.op("act", lambda e, o=junk, i=xt, a=sums[:, t, 0:1]: e.activation(o, i, AF.Identity, accum_out=a),
                  [("x", t)], [("tmpf", tb), ("sums", t, 0)])
        self.S.op("act", lambda e, o=junk, i=xt, a=sums[:, t, 1:2]: e.activation(o, i, AF.Square, accum_out=a),
                  [("x", t)], [("tmpf", tb), ("sums", t, 1)])

    def post_group(self, tiles):
        tiles = list(tiles)
        t0, t1 = tiles[0], tiles[-1] + 1
        g = t0 // 4
        mv = self.stat[:, 0, :].rearrange("p (t c) -> p t c", c=2)
        aux = self.stat[:, 1, :].rearrange("p (c t) -> p c t", c=2)
        allmv = [("mv", t) for t in tiles]
        sums = self.bst[:].rearrange("p a b -> p (a b)")[:, 0:16].rearrange("p (t c) -> p t c", c=2)
        msq = self.bst[:].rearrange("p a b -> p (a b)")[:, 16:24]
        allsums = [("sums", t, c) for t in tiles for c in range(2)]
        self.ts("dve", mv[:, t0:t1, 0], sums[:, t0:t1, 0], 1.0 / D, None, ALU.mult, None, allsums, allmv)
        self.tt("dve", msq[:, t0:t1], mv[:, t0:t1, 0], mv[:, t0:t1, 0], ALU.mult, allmv, [("msq", g)])
        self.stt("dve", mv[:, t0:t1, 1], sums[:, t0:t1, 1], 1.0 / D, msq[:, t0:t1], ALU.mult, ALU.subtract,
                 allsums + [("msq", g)], allmv)
        self.act(aux[:, 0, t0:t1], mv[:, t0:t1, 1], AF.Sqrt, allmv, [("aux", 0, g)], bias=float(LN_EPS))
        self.S.op("dve", lambda e: e.reciprocal(aux[:, 0, t0:t1], aux[:, 0, t0:t1]), [("aux", 0, g)], [("aux", 0, g)])
        self.stt("dve", aux[:, 1, t0:t1], mv[:, t0:t1, 0], -1.0, aux[:, 0, t0:t1], ALU.mult, ALU.mult, allmv + [("aux", 0, g)], [("aux", 1, g)])
        for t in tiles:
            xt = self.x[:, t, :]
            self.act(xt, xt, AF.Identity, [("x", t), ("aux", 0, g), ("aux", 1, g)], [("x", t)],
                     bias=aux[:, 1, t:t + 1], scale=aux[:, 0, t:t + 1])
            self.tt("dve", xt, xt, self.rows[:, 2, :], ALU.mult, [("x", t), ("rows", 2)], [("x", t)])
            self.tt("dve", xt, xt, self.rows[:, 3, :], ALU.add, [("x", t), ("rows", 3)], [("x", t)])

    def mlp_core(self, l):
        acc = self.view(0, [128, NT, D], F32)
        uT = self.view(32 * 1024, [128, 2, 8, 512], BF16)
        rt = self.view(48 * 1024, [128, 4, 512], F32)
        W1, W2 = {}, {}

        def F(b, half):
            w1, k1, i1 = W1[b]
            for sub in range(8):
                pb = self.bank()
                for kc in range(8):
                    self.mm(self.psf(pb), w1[:, kc, sub * 128:(sub + 1) * 128], self.hT[:, kc, half * 512:(half + 1) * 512],
                            kc == 0, kc == 7, [k1] + [("hT", half * 4 + i) for i in range(4)], self.pkeys(pb))
                ri = self.tmp_next("rt", 4)
                self.act(rt[:, ri, :], self.psf(pb), AF.Relu, self.pkeys(pb), [("rt", ri)])
                if sub % 2 == 0:
                    self.tt("dve", uT[:, half, sub, :], rt[:, ri, :], rt[:, ri, :], ALU.mult, [("rt", ri)], [("uT", half, sub)])
                else:
                    self.act(uT[:, half, sub, :], rt[:, ri, :], AF.Square, [("rt", ri)], [("uT", half, sub)])
            if half == 1:
                self.ring_release(i1)

        def S_(b, half):
            w2, k2, i2 = W2[b]
            for tt_ in range(4):
                t = half * 4 + tt_
                for nh in range(2):
                    pb = self.bank()
                    for sub in range(8):
                        self.mm(self.psf(pb), uT[:, half, sub, tt_ * 128:(tt_ + 1) * 128], w2[:, sub, nh * 512:(nh + 1) * 512],
                                sub == 0, sub == 7, [k2, ("uT", half, sub)], self.pkeys(pb))
                    dst = acc[:, t, nh * 512:(nh + 1) * 512]
                    if b == 0:
                        self.cp("act", dst, self.psf(pb), self.pkeys(pb), [("acc", t, nh)])
                    else:
                        self.tt("dve", dst, dst, self.psf(pb), ALU.add, self.pkeys(pb) + [("acc", t, nh)], [("acc", t, nh)])
                if b == 3:
                    self.post_tile(t, acc[:, t, :], [("acc", t, 0), ("acc", t, 1)])
            if half == 1:
                self.ring_release(i2)
            if b == 3:
                self.post_group(range(half * 4, half * 4 + 4))

        W1[0] = self.ring_take(("w1", l, 0))
        self.gate_rows()
        F(0, 0)
        F(0, 1)
        for b in range(4):
            W2[b] = self.ring_take(("w2", l, b))
            S_(b, 0)
            if b < 3:
                W1[b + 1] = self.ring_take(("w1", l, b + 1))
                F(b + 1, 0)
            S_(b, 1)
            if b < 3:
                F(b + 1, 1)

    def gla_core(self, j):
        K1 = 1024
        qdec = self.view(0, [128, 4, 2, 4, 128], BF16)
        sm = self.view(8 * K1, [128, 4, 2, 4, 128], BF16)
        vtok = self.view(16 * K1, [128, 4, 1024], BF16)
        kend = self.view(24 * K1, [128, 4, 2, 512], BF16)
        Sbst = self.view(32 * K1, [128, 8, 4, 256], BF16)
        qk_tok = self.view(48 * K1, [128, 1024], F32)
        S_f = self.view(48 * K1, [128, 4, 256], F32)
        Lt = self.view(52 * K1, [128, 1024], F32)
        S_b = self.view(52 * K1, [128, 4, 256], F32)
        kdec = self.view(56 * K1, [128, 2, 4, 128], BF16)
        Sfb = self.view(56 * K1, [128, 2, 4, 256], BF16)
        zrT = self.view(60 * K1, [128, 1024], BF16)
        wga = self.view(62 * K1, [128, 8, 64], BF16)
        wgb = self.view(63 * K1, [128, 512], BF16)
        gsm = self.gsm
        dec = gsm[:, 0:64].rearrange("p (t c) -> p t c", t=4)
        Ptot = gsm[:, 64:72]
        Prc = gsm[:, 72:88].rearrange("p (s c) -> p s c", s=2)
        Dm = gsm[:, 88:92]
        ssq = gsm[:, 96:104].rearrange("p (s c) -> p s c", s=2)
        gA, gB, gC = ("gA",), ("gB",), ("gC",)
        self.single_pool, self.pair_pool = [6, 7], [0, 2, 4]
        self.single_pool, self.pair_pool = list(range(8)), [0, 2, 4, 6]
        self.gate_rows()
        self.single_pool, self.pair_pool = [6, 7], [0, 2, 4]
        w0, k0, i0 = self.ring_take(("gin", j, 0))
        w1, k1, i1 = self.ring_take(("gin", j, 1))
        w2, k2, i2 = self.ring_take(("gin", j, 2))
        wsrc = self.w
        self.memset("pool", wga, 0.0, [("wga",)])
        self.memset("pool", zrT[0:64, :], 1.0, [("zrT",)])
        for z in range(2):
            self.dma("pool", wga[:, :, z * 32:z * 32 + 16], wsrc["gla_w_ga"][j, z].rearrange("(kc p) r -> p kc r", p=128),
                     [], [("wga",)], chan="wga")
            self.dma("pool", wgb[z * 32:z * 32 + 16, :], wsrc["gla_w_gb"][j, z], [], [("wgb",)], chan="wgb")
            self.dma("pool", wgb[z * 32 + 16:z * 32 + 17, :], wsrc["gla_b_g"][j, z:z + 1, :], [], [("wgb",)], chan="wgb")
        self.dma("sp", self.brow[:, 0, :], wsrc["gla_gn_g"][j, :].partition_broadcast(128), [], [("brow", 0)], chan=("brow", 0))
        for half in range(2):
            b = self.bank()
            for kc in range(8):
                self.mm(self.psf(b)[0:64, :], wga[:, kc, :], self.hT[:, kc, half * 512:(half + 1) * 512], kc == 0, kc == 7,
                        [("wga",)] + [("hT", half * 4 + i) for i in range(4)], self.pkeys(b))
            for z in range(2):
                self.cp("act", zrT[z * 32:z * 32 + 16, half * 512:(half + 1) * 512], self.psf(b)[z * 32:z * 32 + 16, :],
                        self.pkeys(b), [("zrT",)])

        def proj(t, w, wk, nh):
            pass

        def phase1(tt, t):
            tok = slice(t * 128, (t + 1) * 128)
            pq = self.pair()
            for nh in range(2):
                for kc in range(8):
                    self.mm(self.psf(pq + nh), self.hT[:, kc, tok], w0[:, kc, nh * 512:(nh + 1) * 512], kc == 0, kc == 7,
                            [k0, ("hT", t)], self.pkeys(pq + nh))
            self.cp("act", qk_tok, self.psf(pq, 2), self.pkeys(pq, 2), [gA])
            pv_ = self.pair()
            for nh in range(2):
                for kc in range(8):
                    self.mm(self.psf(pv_ + nh), self.hT[:, kc, tok], w1[:, kc, nh * 512:(nh + 1) * 512], kc == 0, kc == 7,
                            [k1, ("hT", t)], self.pkeys(pv_ + nh))
            self.cp("act", vtok[:, tt, :], self.psf(pv_, 2), self.pkeys(pv_, 2), [("vtok", tt)])
            pz = self.pair()
            for z in range(2):
                self.mm(self.psf(pz + z), zrT[z * 32:z * 32 + 17, tok], wgb[z * 32:z * 32 + 17, :], True, True,
                        [("zrT",), ("wgb",)], self.pkeys(pz + z))
            self.act(Lt, self.psf(pz, 2), AF.Exp, self.pkeys(pz, 2), [gB], scale=-1.0)
            self.act(Lt, Lt, AF.Ln, [gB], [gB], bias=1.0)
            pc = self.pair()
            for z in range(2):
                for h in range(4):
                    c0 = (z * 4 + h) * 128
                    self.mm(self.psf(pc, 2)[:, c0:c0 + 128], Lt[:, z * 512 + h * 128:z * 512 + (h + 1) * 128], self.cst[:, 3 + z, :],
                            True, True, [gB, ("cst",)], self.pkeys(pc + z))
            pss = self.pair()
            for z in range(2):
                self.mm(self.psf(pss + z), self.cst[:, 5 + z, :], Lt[:, z * 512:(z + 1) * 512], True, True,
                        [gB, ("cst",)], self.pkeys(pss + z))
            ptot = self.bank()
            for z in range(2):
                for h in range(4):
                    c0 = (z * 4 + h) * 2
                    self.mm(self.psf(ptot)[:, c0:c0 + 2], Lt[:, z * 512 + h * 128:z * 512 + (h + 1) * 128], self.cst[:, 7, 0:2],
                            True, True, [gB, ("cst",)], self.pkeys(ptot))
            self.act(dec[:, tt, :], self.psf(ptot)[:, 0:16], AF.Exp, self.pkeys(ptot), [("dec", tt)])
            pt = self.pair()
            for idx in range(8):
                self.tr(self.psf(pt, 2)[:, idx * 128:(idx + 1) * 128], qk_tok[:, idx * 128:(idx + 1) * 128], self.cst[:, 0, :],
                        [gA, ("cst",)], self.pkeys(pt + idx // 4))
            e1 = self.tmp_next("tmpf", 2)
            E1 = self.tmpf[:, e1, :]
            self.act(E1, self.psf(pc, 2), AF.Exp, self.pkeys(pc, 2), [("tmpf", e1)])
            self.stt("dve", qdec[:, tt].rearrange("p d h i -> p d (h i)"), E1.rearrange("p (d x) -> p d x", d=2), float(DK ** -0.5),
                     self.psf(pt)[:, 0:512].unsqueeze(1).to_broadcast([128, 2, 512]), ALU.mult, ALU.mult,
                     [("tmpf", e1)] + self.pkeys(pt), [("qdec", tt)])
            e2 = self.tmp_next("tmpf", 2)
            E2 = self.tmpf[:, e2, :]
            self.act(E2, self.psf(pc, 2), AF.Exp, self.pkeys(pc, 2), [("tmpf", e2)], scale=-1.0)
            self.tt("dve", kdec.rearrange("p d h i -> p d (h i)"), E2.rearrange("p (d x) -> p d x", d=2),
                    self.psf(pt + 1)[:, 0:512].unsqueeze(1).to_broadcast([128, 2, 512]), ALU.mult,
                    [("tmpf", e2)] + self.pkeys(pt + 1), [gC])
            e3 = self.tmp_next("tmpf", 2)
            E3 = self.tmpf[:, e3, :]
            self.act(E3, self.psf(pss, 2), AF.Exp, self.pkeys(pss, 2), [("tmpf", e3)])
            self.tt("dve", kend[:, tt], E3.rearrange("p (d x) -> p d x", d=2),
                    qk_tok[:, 512:1024].unsqueeze(1).to_broadcast([128, 2, 512]), ALU.mult,
                    [("tmpf", e3), gA], [("kend", tt)])
            psc = self.pair()
            for z in range(2):
                for h in range(4):
                    c0 = (z * 4 + h) * 128
                    self.mm(self.psf(psc, 2)[:, c0:c0 + 128], kdec[:, z, h, :], qdec[:, tt, z, h, :], True, True,
                            [gC, ("qdec", tt)], self.pkeys(psc + z))
            self.tt("dve", sm[:, tt], self.psf(psc, 2).rearrange("p (d h i) -> p d h i", d=2, h=4),
                    self.cst[:, 1:3, :].unsqueeze(2).to_broadcast([128, 2, 4, 128]), ALU.mult,
                    self.pkeys(psc, 2) + [("cst",)], [("sm", tt)])

        def kv(tt, cc, z):
            pk = self.pair()
            rows = slice(cc * 64, (cc + 1) * 64)
            for h in range(4):
                self.mm(self.psf(pk, 2)[:, h * 256:(h + 1) * 256], kend[rows, tt, z, h * 128:(h + 1) * 128],
                        vtok[rows, tt, h * 256:(h + 1) * 256], True, True, [("kend", tt), ("vtok", tt)], self.pkeys(pk + h // 2))
            return pk

        def step(S, skey, tt, cc, z):
            pk = kv(tt, cc, z)
            for h in range(4):
                ci = (z * 4 + h) * 2 + cc
                self.stt("dve", S[:, h, :], S[:, h, :], dec[:, tt, ci:ci + 1], self.psf(pk, 2)[:, h * 256:(h + 1) * 256],
                         ALU.mult, ALU.add, [skey, ("dec", tt)] + self.pkeys(pk + h // 2), [skey])

        def dec4(tt, cc, z):
            return dec[:, tt, :].rearrange("p (d h c) -> p d h c", d=2, h=4)[:, z, :, cc]

        def phase2(tiles, seq):
            T = len(tiles)
            C = 2 * T
            sample = seq is None
            Sf2 = S_f.rearrange("p h e -> p (h e)")
            Sb2 = S_b.rearrange("p h e -> p (h e)")
            if sample:
                self.memset("dve", Sf2, 0.0, [gA])
                self.memset("dve", Sb2, 0.0, [gB])
                self.memset("dve", Ptot, 1.0, [("Ptot",)])
                for c in range(C):
                    step(S_f, gA, c // 2, c % 2, 0)
                    self.tt("dve", Ptot[:, 0:4], Ptot[:, 0:4], dec4(c // 2, c % 2, 0), ALU.mult, [("Ptot",), ("dec", c // 2)], [("Ptot",)])
                for c in reversed(range(C)):
                    step(S_b, gB, c // 2, c % 2, 1)
                    self.tt("dve", Ptot[:, 4:8], Ptot[:, 4:8], dec4(c // 2, c % 2, 1), ALU.mult, [("Ptot",), ("dec", c // 2)], [("Ptot",)])
                dsts_ = []
                for z, S2z, skz in ((0, Sf2, gA), (1, Sb2, gB)):
                    src_ = self.agg_src[j][z].ap()
                    self.dma("sp", src_[:, 0:1024], S2z, [skz], [("aggs", z, 0)], chan=("ag0", z, 0))
                    self.dma("sp", src_[:, 1024:1032], Ptot, [("Ptot",)], [("aggs", z, 1)], chan=("ag0", z, 1))
                    self.S.op("pool", lambda e, z=z: e.collective_compute("AllGather", ALU.bypass, replica_groups=[[0, 1, 2, 3], [4, 5, 6, 7]],
                                                                          ins=[self.agg_src[j][z].ap()], outs=[self.agg_dst[j][z].ap()]),
                              [("aggs", z, 0), ("aggs", z, 1)], [("aggd", z)], is_dma=True, chan=("gcc", j, z), inc=1)
                    dsts_.append(self.agg_dst[j][z].ap().rearrange("(r p) c -> r p c", p=128))
                mid_exchange()
                st0 = self.state0
                self.dma("sp", S_f, st0[j, 0].rearrange("h d e -> d h e"), [], [gA], chan=("st0", 0))
                self.dma("sp", S_b, st0[j, 1].rearrange("h d e -> d h e"), [], [gB], chan=("st0", 1))
                for z, S2, S3, skey, order in ((0, Sf2, S_f, gA, range(4)), (1, Sb2, S_b, gB, reversed(range(4)))):
                    for i in order:
                        mcol = self.cm[:, z * 4 + i:z * 4 + i + 1]
                        ab = self.tmp_next("tmpf", 2)
                        ps_ = self.tmp_next("Prc", 2)
                        self.dma("sp", self.tmpf[:, ab, :], dsts_[z][i, :, 0:1024], [("aggd", z)], [("tmpf", ab)], chan=("agl", ab))
                        self.dma("sp", Prc[:, ps_, :], dsts_[z][i, :, 1024:1032], [("aggd", z)], [("Prc", ps_)], chan=("prl", ps_))
                        self.ts("dve", Dm, Prc[:, ps_, z * 4:(z + 1) * 4], 1.0, mcol, ALU.subtract, ALU.mult, [("Prc", ps_), ("cm",)], [("Dm",)])
                        self.ts("dve", Dm, Dm, 1.0, None, ALU.add, None, [("Dm",)], [("Dm",)])
                        for h in range(4):
                            self.ts("dve", S3[:, h, :], S3[:, h, :], Dm[:, h:h + 1], None, ALU.mult, None, [skey, ("Dm",)], [skey])
                        self.stt("dve", S2, self.tmpf[:, ab, :], mcol, S2, ALU.mult, ALU.add, [("tmpf", ab), ("cm",), skey], [skey])
            else:
                self.memset("dve", Sf2, 0.0, [gA])
                self.memset("dve", Sb2, 0.0, [gB])
            for c in reversed(range(C)):
                self.cp("act", Sbst[:, c].rearrange("p h e -> p (h e)"), Sb2, [gB], [("Sbst", c)])
                step(S_b, gB, c // 2, c % 2, 1)
            for tt, t in enumerate(tiles):
                tok = slice(t * 128, (t + 1) * 128)
                for cc in range(2):
                    self.cp("act", Sfb[:, cc].rearrange("p h e -> p (h e)"), Sf2, [gA], [gC])
                    step(S_f, gA, tt, cc, 0)
                po = self.pair()
                for h in range(4):
                    ov = self.psf(po, 2)[:, h * 256:(h + 1) * 256]
                    vh = vtok[:, tt, h * 256:(h + 1) * 256]
                    wk_ = self.pkeys(po + h // 2)
                    for cc in range(2):
                        rows = slice(cc * 64, (cc + 1) * 64)
                        self.mm(ov[rows, :], sm[:, tt, 0, h, rows], vh, True, False, [("sm", tt), ("vtok", tt)], wk_)
                        self.mm(ov[rows, :], sm[:, tt, 1, h, rows], vh, False, False, [("sm", tt), ("vtok", tt)], wk_)
                        self.mm(ov[rows, :], qdec[:, tt, 0, h, rows], Sfb[:, cc, h, :], False, False, [("qdec", tt), gC], wk_)
                        self.mm(ov[rows, :], qdec[:, tt, 1, h, rows], Sbst[:, 2 * tt + cc, h, :], False, True,
                                [("qdec", tt), ("Sbst", 2 * tt + cc)], wk_)
                pr = self.pair()
                for nh in range(2):
                    for kc in range(8):
                        self.mm(self.psf(pr + nh), self.hT[:, kc, tok], w2[:, kc, nh * 512:(nh + 1) * 512], kc == 0, kc == 7,
                                [k2, ("hT", t)], self.pkeys(pr + nh))
                rb = self.tmp_next("tmpf", 2)
                G2 = self.tmpf[:, rb, :]
                self.act(G2, self.psf(pr, 2), AF.Silu, self.pkeys(pr, 2), [("tmpf", rb)])
                self.tt("dve", G2, G2, self.brow[:, 0, :], ALU.mult, [("tmpf", rb), ("brow", 0)], [("tmpf", rb)])
                si = self.tmp_next("ssq", 2)
                jb = self.tmp_next("tmpf", 2)
                for h in range(4):
                    self.S.op("act", lambda e, o=self.tmpf[:, jb, h * 256:(h + 1) * 256], i=self.psf(po, 2)[:, h * 256:(h + 1) * 256],
                              a=ssq[:, si, h:h + 1]: e.activation(o, i, AF.Square, accum_out=a),
                              self.pkeys(po + h // 2), [("tmpf", jb), ("ssq", si)])
                self.act(ssq[:, si, :], ssq[:, si, :], AF.Sqrt, [("ssq", si)], [("ssq", si)], bias=float(RMS_EPS), scale=1.0 / DV)
                self.S.op("dve", lambda e, o=ssq[:, si, :]: e.reciprocal(o, o), [("ssq", si)], [("ssq", si)])
                hbk = self.tmp_next("hb", 2)
                for h in range(4):
                    self.stt("dve", self.hb[:, hbk, h * 256:(h + 1) * 256], self.psf(po, 2)[:, h * 256:(h + 1) * 256], ssq[:, si, h:h + 1],
                             G2[:, h * 256:(h + 1) * 256], ALU.mult, ALU.mult,
                             self.pkeys(po + h // 2) + [("ssq", si), ("tmpf", rb)], [("hb", hbk)])
                self.transpose_tile(self.hb[:, hbk, :], ("hb", hbk), self.hT, "hT", t)
            if sample and self.debug and "gSb" in self.dbg:
                self.dma("sp", self.dbg.pop("gSb"), Sbst, [("Sbst", c) for c in range(8)], [("dbg", 9)], chan="dbg9")
            if not sample:
                self.dma("sp", self.ns_out[seq, j, 0].rearrange("h d e -> d h e"), S_f, [gA], [("nso", seq, 0)], chan=("nso", 0))
                self.dma("sp", self.ns_out[seq, j, 1].rearrange("h d e -> d h e"), S_b, [gB], [("nso", seq, 1)], chan=("nso", 1))

        segs = [([0, 1], 0), ([2, 3], 1), ([4, 5, 6, 7], None)]
        go_blk = []
        self.early_post = set()

        def wo_tile(t):
            wo, ko, io = go_blk[0]
            b = self.pair()
            for nh in range(2):
                for kc in range(8):
                    self.mm(self.psf(b + nh), self.hT[:, kc, t * 128:(t + 1) * 128], wo[:, kc, nh * 512:(nh + 1) * 512],
                            kc == 0, kc == 7, [ko, ("hT", t)], self.pkeys(b + nh))
            if t == NT - 1:
                self.ring_release(io)
            return self.psf(b, 2), self.pkeys(b, 2)

        def mid_exchange():
            if not go_blk:
                return
            for t in range(4):
                ap, keys = wo_tile(t)
                self.post_tile(t, ap, keys)
                self.early_post.add(t)
            self.post_group(range(4))
        import os
        lvl = int(os.environ.get("GLA_STOP", "9"))
        if lvl == 1:
            segs = []
        elif lvl == 2:
            phase1(0, 0)
            segs = []
        elif lvl == 3:
            segs = segs[:1]
        elif lvl == 4:
            segs = segs[:2]
        for si_, (tiles, seq) in enumerate(segs):
            for tt, t in enumerate(tiles):
                phase1(tt, t)
            if si_ == len(segs) - 1:
                self.ring_release(i0)
                self.ring_release(i1)
                if len(segs) == 3:
                    go_blk.append(self.ring_take(("go", j)))
            phase2(tiles, seq)
        if not segs:
            self.ring_release(i0)
            self.ring_release(i1)
        self.ring_release(i2)
        if self.debug and "gyT" in self.dbg:
            self.dma("sp", self.dbg.pop("gyT"), self.hT[:], [("hT", t) for t in range(NT)], [("dbg", 8)], chan="dbg8")
        if not go_blk:
            go_blk.append(self.ring_take(("go", j)))
        self.single_pool, self.pair_pool = list(range(8)), [0, 2, 4, 6]
        return wo_tile

    def memset(self, eng, ap, val, writes):
        return self.S.op(eng, lambda e: e.memset(ap, val), [], writes)

    def conf_core(self):
        pv = self.pv
        cv = self.view(0, [128, 8, 1024], F32)
        upad = self.view(32 * 1024, [128, 2, 1324], BF16)
        dg = self.view(38 * 1024, [128, 8, 128], BF16)
        sig = self.view(44 * 1024, [128, 2, 512], F32)
        sq = self.view(48 * 1024, [128, 2, 512], F32)
        mrow = self.view(52 * 1024, [128, 2, 512], F32)
        rrow = self.view(56 * 1024, [128, 2, 512], F32)
        vtmp = self.view(60 * 1024, [128, 512], F32)
        wa, ka, ia = self.ring_take(("cpw1", 0))
        self.gate_rows()
        wg, kg, ig = self.ring_take(("cpw1", 1))
        for ub in range(2):
            self.memset("pool", upad[:, ub, :], 0.0, [("upad", ub)])
        hkeys = lambda half: [("hT", half * 4 + i) for i in range(4)]

        def uview(ub, half, off, L):
            if half == 0:
                return upad[:, ub, 0:572].rearrange("p (s w) -> p s w", s=2)[:, :, off:off + L]
            return upad[:, ub, 572:1324].rearrange("p (s w) -> p s w", s=8)[:, :, off:off + L]

        def glu(j):
            ub = j % 2
            for half in range(2):
                pa = self.bank()
                for kc in range(8):
                    self.mm(self.psf(pa), wa[:, kc, j * 128:(j + 1) * 128], self.hT[:, kc, half * 512:(half + 1) * 512],
                            kc == 0, kc == 7, [ka] + hkeys(half), self.pkeys(pa))
                pg = self.bank()
                for kc in range(8):
                    self.mm(self.psf(pg), wg[:, kc, j * 128:(j + 1) * 128], self.hT[:, kc, half * 512:(half + 1) * 512],
                            kc == 0, kc == 7, [kg] + hkeys(half), self.pkeys(pg))
                si = self.tmp_next("sig", 2)
                self.act(sig[:, si, :], self.psf(pg), AF.Sigmoid, self.pkeys(pg) + [("pv",)], [("sig", si)], bias=pv[:, 8 + j:9 + j])
                s_ = 2 if half == 0 else 8
                self.stt("dve", uview(ub, half, 15, 512 // s_), self.psf(pa).rearrange("p (s w) -> p s w", s=s_), pv[:, j:j + 1],
                         sig[:, si, :].rearrange("p (s w) -> p s w", s=s_), ALU.add, ALU.mult,
                         self.pkeys(pa) + [("sig", si), ("pv",), ("upad", ub)], [("upad", ub, half)])

        def conv(j):
            ub = j % 2
            pcs = [self.bank(), self.bank()]
            for k in range(CONF_W):
                di = self.tmp_next("dg", 8)
                wcol = pv[:, 16 + k * 8 + j:17 + k * 8 + j]
                if k % 2 == 0:
                    self.ts("dve", dg[:, di, :], self.cstb[:, 0, :], wcol, None, ALU.mult, None, [("cstb",), ("pv",)], [("dg", di)])
                else:
                    self.act(dg[:, di, :], self.cstb[:, 0, :], AF.Identity, [("cstb",), ("pv",)], [("dg", di)], scale=wcol)
                for half in range(2):
                    self.mm(self.psf(pcs[half]), dg[:, di, :], uview(ub, half, k, 256 if half == 0 else 64), k == 0, k == CONF_W - 1,
                            [("dg", di), ("upad", ub, half), ("upad", ub)], self.pkeys(pcs[half]))
            for half in range(2):
                self.act(cv[:, j, half * 512:(half + 1) * 512], self.psf(pcs[half]), AF.Identity, self.pkeys(pcs[half]) + [("pv",)],
                         [("cv", j, half)], bias=pv[:, 264 + j:265 + j])

        for j in range(8):
            glu(j)
            if j > 0:
                conv(j - 1)
        self.ring_release(ia)
        self.ring_release(ig)
        conv(7)
        if self.debug:
            self.dma("sp", self.dbg["ccv"], cv, [("cv", j, h) for j in range(8) for h in range(2)], [("dbg", 5)], chan="dbg5")
            self.dma("sp", self.dbg["chin"], self.hT[:], [("hT", t) for t in range(NT)], [("dbg", 7)], chan="dbg7")
        ones = self.cst[:, 8, :]
        for half in range(2):
            b1 = self.bank()
            for j in range(8):
                self.mm(self.psf(b1), ones, cv[:, j, half * 512:(half + 1) * 512], j == 0, j == 7,
                        [("cst",), ("cv", j, half)], self.pkeys(b1))
            b2 = self.bank()
            for j in range(8):
                qi = self.tmp_next("sq", 2)
                self.act(sq[:, qi, :], cv[:, j, half * 512:(half + 1) * 512], AF.Square, [("cv", j, half)], [("sq", qi)])
                self.mm(self.psf(b2), ones, sq[:, qi, :], j == 0, j == 7, [("cst",), ("sq", qi)], self.pkeys(b2))
            mr, rr = mrow[:, half, :], rrow[:, half, :]
            self.act(mr, self.psf(b1), AF.Identity, self.pkeys(b1), [("mrow", half)], scale=1.0 / D)
            self.tt("dve", vtmp, mr, mr, ALU.mult, [("mrow", half)], [("vtmp",)])
            self.stt("dve", vtmp, self.psf(b2), 1.0 / D, vtmp, ALU.mult, ALU.subtract, self.pkeys(b2) + [("vtmp",)], [("vtmp",)])
            self.act(rr, vtmp, AF.Sqrt, [("vtmp",)], [("rrow", half)], bias=float(LN_EPS))
            self.S.op("dve", lambda e, o=rr: e.reciprocal(o, o), [("rrow", half)], [("rrow", half)])
            self.stt("dve", mr, mr, -1.0, rr, ALU.mult, ALU.mult, [("mrow", half), ("rrow", half)], [("mrow", half)])
            for j in range(8):
                c_ = cv[:, j, half * 512:(half + 1) * 512]
                self.tt("dve", c_, c_, rr, ALU.mult, [("cv", j, half), ("rrow", half)], [("cv", j, half)])
                self.tt("dve", c_, c_, mr, ALU.add, [("cv", j, half), ("mrow", half)], [("cv", j, half)])
                self.act(self.hT[:, j, half * 512:(half + 1) * 512], c_, AF.Silu,
                         [("cv", j, half), ("pv",)], [("hT", half * 4 + i) for i in range(4)],
                         bias=pv[:, 280 + j:281 + j], scale=pv[:, 272 + j:273 + j])
        if self.debug:
            self.dma("sp", self.dbg["chT"], self.hT[:], [("hT", t) for t in range(NT)], [("dbg", 6)], chan="dbg6")
        w2, k2, i2 = self.ring_take(("cpw2",))
        brow2 = self.view(0, [128, D], F32)
        self.dma("sp", brow2, self.w["conf_b_pw2"][0, :].partition_broadcast(128),
                 [], [("cv", j, h) for j in range(8) for h in range(2)], chan="cb2")

        def src(t):
            b = self.pair()
            for nh in range(2):
                for kc in range(8):
                    self.mm(self.psf(b + nh), self.hT[:, kc, t * 128:(t + 1) * 128], w2[:, kc, nh * 512:(nh + 1) * 512],
                            kc == 0, kc == 7, [k2, ("hT", t)], self.pkeys(b + nh))
            if t == NT - 1:
                self.ring_release(i2)
            tb = self.tmp_next("tmpf", 2)
            self.tt("dve", self.tmpf[:, tb, :], self.psf(b, 2), brow2, ALU.add,
                    self.pkeys(b, 2) + [("cv", 0, 0)], [("tmpf", tb)])
            return self.tmpf[:, tb, :], [("tmpf", tb)]
        return src

    def sconv_core(self):
        pv = self.pv
        Pp = self.view(0, [128, 8, 2, 258], F32)
        Ps = self.view(16512, [128, 8, 640], F32)
        vT = self.view(36992, [128, 8, 1024], BF16)
        hst = self.view(53376, [128, 2, 8, 64], F32)
        hrc = self.view(57472, [128, 2, 8, 64], F32)
        ctmp = self.view(61568, [128, 512], F32)
        utmp = self.view(63616, [128, 480], F32)
        utmp = self.tmpf
        hkeys = lambda half: [("hT", half * 4 + i) for i in range(4)]
        wc, kc_, ic = self.ring_take(("sin", 1))
        self.gate_rows()
        wu, ku, iu = self.ring_take(("sin", 2))
        self.memset("pool", Pp, 0.0, [("Pp", j) for j in range(8)])
        self.memset("pool", Ps[:, :, 0:64], 0.0, [("halo", 0)])
        self.memset("pool", Ps[:, :, 576:640], 0.0, [("halo", 1)])
        for j in range(8):
            for half in range(2):
                pc = self.bank()
                for kc in range(8):
                    self.mm(self.psf(pc), wc[:, kc, j * 128:(j + 1) * 128], self.hT[:, kc, half * 512:(half + 1) * 512],
                            kc == 0, kc == 7, [kc_] + hkeys(half), self.pkeys(pc))
                pu = self.bank()
                for kc in range(8):
                    self.mm(self.psf(pu), wu[:, kc, j * 128:(j + 1) * 128], self.hT[:, kc, half * 512:(half + 1) * 512],
                            kc == 0, kc == 7, [ku] + hkeys(half), self.pkeys(pu))
                tb = self.tmp_next("tmpf", 2)
                self.cp("act", utmp[:, tb, 0:512], self.psf(pu), self.pkeys(pu), [("tmpf", tb)])
                if half == 0:
                    self.tt("dve", Pp[:, j, :, 1:257], self.psf(pc).rearrange("p (s w) -> p s w", s=2),
                            utmp[:, tb, 0:512].rearrange("p (s w) -> p s w", s=2), ALU.mult,
                            self.pkeys(pc) + [("tmpf", tb)], [("Pp", j)])
                else:
                    self.tt("dve", Ps[:, j, 64:576], self.psf(pc), utmp[:, tb, 0:512], ALU.mult,
                            self.pkeys(pc) + [("tmpf", tb)], [("Ps", j)])
        self.ring_release(ic)
        self.ring_release(iu)
        wb, kb, ib = self.ring_take(("sin", 0))
        allPs = [("Ps", j) for j in range(8)]
        self.cp("pool", hst[:, 0], Ps[:, :, 64:128], allPs, [("hst",)])
        self.cp("pool", hst[:, 1], Ps[:, :, 512:576], allPs, [("hst",)])
        self.dma("sp", self.halo_src.ap(), hst.rearrange("p a j w -> p (a j w)"), [("hst",)], [("halo_src",)], chan="hs")
        self.S.op("pool", lambda e: e.collective_compute("AllGather", ALU.bypass, replica_groups=[[0, 1, 2, 3], [4, 5, 6, 7]],
                                                         ins=[self.halo_src.ap()], outs=[self.halo_dst.ap()]),
                  [("halo_src",)], [("halo_dst",)], is_dma=True, chan="hcc", inc=1)
        def recv_halo():
            hd = self.halo_dst.ap().rearrange("(r p) (a j w) -> r p a j w", p=128, a=2, j=8)
            for i in range(4):
                for side, a_idx, mcol, dst in ((0, 1, 8 + i, Ps[:, :, 0:64]), (1, 0, 12 + i, Ps[:, :, 576:640])):
                    ri = self.tmp_next("hrc", 2)
                    self.dma("sp", hrc[:, ri], hd[i, :, a_idx], [("halo_dst",)], [("hrc", ri)], chan=("hrc", ri))
                    self.stt("dve", dst, hrc[:, ri], self.cm[:, mcol:mcol + 1], dst, ALU.mult, ALU.add,
                             [("hrc", ri), ("cm",), ("halo", side)], [("halo", side)])
        for half in range(2):
            if half == 1:
                recv_halo()
            for j in range(8):
                pb = self.bank()
                for kc in range(8):
                    self.mm(self.psf(pb), wb[:, kc, j * 128:(j + 1) * 128], self.hT[:, kc, half * 512:(half + 1) * 512],
                            kc == 0, kc == 7, [kb] + hkeys(half), self.pkeys(pb))
                w_ = lambda k: pv[:, 288 + k * 8 + j:289 + k * 8 + j]
                if half == 0:
                    cview = ctmp.rearrange("p (s w) -> p s w", s=2)
                    srcs = [Pp[:, j, :, k:k + 256] for k in range(3)]
                    rk = [("Pp", j), ("pv",)]
                    pbv = self.psf(pb).rearrange("p (s w) -> p s w", s=2)
                    outv = vT[:, j, 0:512].rearrange("p (s w) -> p s w", s=2)
                else:
                    cview = ctmp
                    srcs = [Ps[:, j, 64 * k:64 * k + 512] for k in range(3)]
                    rk = [("Ps", j), ("halo", 0), ("halo", 1), ("pv",)]
                    pbv = self.psf(pb)
                    outv = vT[:, j, 512:1024]
                self.ts("dve", cview, srcs[0], w_(0), None, ALU.mult, None, rk, [("ctmp",)])
                self.stt("dve", cview, srcs[1], w_(1), cview, ALU.mult, ALU.add, rk + [("ctmp",)], [("ctmp",)])
                self.stt("dve", cview, srcs[2], w_(2), cview, ALU.mult, ALU.add, rk + [("ctmp",)], [("ctmp",)])
                self.tt("dve", outv, pbv, cview, ALU.mult, self.pkeys(pb) + [("ctmp",)], [("vT", j, half)])
        self.ring_release(ib)
        wo, ko, io = self.ring_take(("sout",))

        def src(t):
            b = self.pair()
            for nh in range(2):
                for kc in range(8):
                    self.mm(self.psf(b + nh), vT[:, kc, t * 128:(t + 1) * 128], wo[:, kc, nh * 512:(nh + 1) * 512],
                            kc == 0, kc == 7, [ko, ("vT", kc, t // 4)], self.pkeys(b + nh))
            if t == NT - 1:
                self.ring_release(io)
            return self.psf(b, 2), self.pkeys(b, 2)
        return src


WEIGHT_SHAPES = {
    "mod_w": (4, 1024, 6144), "mod_b": (4, 6144), "ln_g": (4, 2, 1024), "ln_b": (4, 2, 1024),
    "ff_w1": (4, 1024, 4096), "ff_w2": (4, 4096, 1024),
    "gla_w_in": (2, 1024, 3072), "gla_w_ga": (2, 2, 1024, 16), "gla_w_gb": (2, 2, 16, 512),
    "gla_b_g": (2, 2, 512), "gla_gn_g": (2, 1024), "gla_w_o": (2, 1024, 1024),
    "conf_w_pw1": (1, 1024, 2048), "conf_w_pw2": (1, 1024, 1024), "conf_b_pw2": (1, 1024),
    "sc_w_in": (1, 1024, 3072), "sc_w_out": (1, 1024, 1024),
}


def make_consts():
    j = np.arange(128)[:, None]
    i = np.arange(128)[None, :]
    same = (j // 64) == (i // 64)
    c = np.zeros((128, 9, 128), np.float32)
    c[:, 8, :] = 1.0
    c[:, 0, :] = np.eye(128, dtype=np.float32)
    c[:, 1, :] = (same & (j <= i)).astype(np.float32)
    c[:, 2, :] = (same & (j >= i)).astype(np.float32)
    c[:, 3, :] = (same & (j <= i)).astype(np.float32) * (-1.0 / 16.0)
    c[:, 4, :] = (same & (j >= i)).astype(np.float32) * (-1.0 / 16.0)
    c[:, 5, :] = (same & (j > i)).astype(np.float32) * (-1.0 / 16.0)
    c[:, 6, :] = (same & (j < i)).astype(np.float32) * (-1.0 / 16.0)
    c[:, 7, 0] = (np.arange(128) < 64) * (-1.0 / 16.0)
    c[:, 7, 1] = (np.arange(128) >= 64) * (-1.0 / 16.0)
    return c


def make_in_maps(inp):
    f = lambda a: np.ascontiguousarray(np.asarray(a, dtype=np.float32))
    xp = f(inp["x_prompt"])
    xs = f(inp["x_sample"])
    c = f(inp["c"])
    cctx = f(inp["c_ctx"])
    st = f(inp["state_gla"])
    consts = make_consts()
    fm = lambda v: np.ascontiguousarray(v.reshape(-1, 128).T)
    pvec = np.zeros((128, 320), np.float32)
    pvec[:, 0:16] = fm(f(inp["conf_b_pw1"])[0])
    wdw = f(inp["conf_w_dw"])[0]
    pvec[:, 16:16 + 248] = np.concatenate([fm(wdw[k]) for k in range(CONF_W)], axis=1)
    pvec[:, 264:272] = fm(f(inp["conf_b_dw"])[0])
    pvec[:, 272:280] = fm(f(inp["conf_ln_g"])[0])
    pvec[:, 280:288] = fm(f(inp["conf_ln_b"])[0])
    wsc = f(inp["sc_w_conv"])[0]
    pvec[:, 288:312] = np.concatenate([fm(wsc[k]) for k in range(3)], axis=1)
    shared = {name: f(inp[name]) for name in WEIGHT_SHAPES}
    maps = []
    for r in range(8):
        b, p = r // 4, r % 4
        x_in = np.concatenate([xp[2 * r], xp[2 * r + 1], xs[b, 512 * p:512 * (p + 1)]], axis=0)
        cv = np.stack([cctx, c[b]], axis=0)
        cvecT = np.ascontiguousarray(cv.reshape(2, 8, 128).transpose(2, 0, 1))
        cm = np.zeros((128, 16), np.float32)
        for i in range(4):
            cm[:, i] = 1.0 if i < p else 0.0
            cm[:, 4 + i] = 1.0 if i > p else 0.0
            cm[:, 8 + i] = 1.0 if i == p - 1 else 0.0
            cm[:, 12 + i] = 1.0 if i == p + 1 else 0.0
        m = dict(shared)
        m.update({"x_in": np.ascontiguousarray(x_in), "cvecT": cvecT, "cmask": cm, "consts": consts,
                  "state0": np.ascontiguousarray(st[b]), "pvec": pvec})
        maps.append(m)
    return maps


_NC_CACHE = {}


def run(inp, n_sub=8, skip_mixers=False, trace=False, debug=False, skip_kinds=()):
    key = (n_sub, skip_mixers, debug, tuple(skip_kinds))
    if key not in _NC_CACHE:
        _NC_CACHE[key] = Builder(n_sub, skip_mixers, debug, skip_kinds).run()
    nc = _NC_CACHE[key]
    maps = make_in_maps(inp)
    res = run_bass_kernel_spmd(nc, maps, core_ids=list(range(8)), **({"trace": True} if trace else {}))
    yp = np.zeros((16, 256, D), np.float32)
    ys = np.zeros((2, 2048, D), np.float32)
    ns = np.zeros((16, 2, 2, NH, DK, DV), np.float32)
    for r in range(8):
        b, p = r // 4, r % 4
        y = res.results[r]["y_out"]
        yp[2 * r] = y[0:256]
        yp[2 * r + 1] = y[256:512]
        ys[b, 512 * p:512 * (p + 1)] = y[512:1024]
        ns[2 * r:2 * r + 2] = res.results[r]["ns_out"]
    return (yp, ys, ns), res


def kernel(**inputs):
    outs, _ = run(inputs)
    return outs
```

```python
import contextlib
import numpy as np
import ml_dtypes
import concourse.bass as bass
import concourse.mybir as mybir
from concourse.bass_utils import run_bass_kernel_spmd

F32 = mybir.dt.float32
BF16 = mybir.dt.bfloat16
ALU = mybir.AluOpType
AF = mybir.ActivationFunctionType

D = 1024
NT = 8
DEPTH = 4
ALPHA = (2 * DEPTH) ** 0.25
LN_EPS = 1e-5
RMS_EPS = 1e-6
DK = 128
DV = 256
NH = 4
CONF_W = 31
SAME_ENGINE_SYNC = True


class Op:
    __slots__ = ("eng", "fn", "deps", "is_dma", "chan", "signal", "ev", "idx", "inc", "epoch")

    def __init__(self, eng, fn, is_dma=False, chan=None, inc=16):
        self.inc = inc
        self.eng = eng
        self.fn = fn
        self.deps = []
        self.is_dma = is_dma
        self.chan = chan
        self.signal = False
        self.ev = None
        self.idx = None


class Sched:
    ENGS = ("pe", "act", "dve", "pool", "sp")

    def __init__(self):
        self.ops = []
        self.last_writer = {}
        self.readers = {}
        self.chan_count = {}
        self.epoch = 0

    def op(self, eng, fn, reads=(), writes=(), is_dma=False, chan=None, inc=16):
        o = Op(eng, fn, is_dma, chan, inc)
        o.idx = len(self.ops)
        o.epoch = self.epoch
        deps = {}
        for k in reads:
            w = self.last_writer.get(k)
            if w is not None:
                deps[w.idx] = w
        for k in writes:
            w = self.last_writer.get(k)
            if w is not None:
                deps[w.idx] = w
            for r in self.readers.get(k, ()):
                deps[r.idx] = r
        deps.pop(o.idx, None)
        o.deps = list(deps.values())
        for k in writes:
            self.last_writer[k] = o
            self.readers[k] = []
        for k in reads:
            self.readers.setdefault(k, []).append(o)
        if is_dma:
            c = self.chan_count.get(chan, 0) + inc
            self.chan_count[chan] = c
            o.ev = (("dma", chan), c)
        self.ops.append(o)
        return o

    def barrier(self, main_fn):
        last = {}
        skip = ()
        for o in self.ops:
            if o.is_dma and ((isinstance(o.chan, tuple) and o.chan[0] in skip) or (isinstance(o.chan, str) and o.chan.startswith("misc"))):
                continue
            last[("dma", o.chan) if o.is_dma else ("eng", o.eng)] = o
        B = self.op("dve", main_fn, [], [("barrier",)])
        have = {d.idx for d in B.deps}
        for o in last.values():
            if o.idx not in have and o is not B:
                B.deps.append(o)
        for eng in ("pe", "act", "pool", "sp"):
            self.op(eng, lambda e: e.nop(nofuse=True), [("barrier",)], [])

    def finalize(self):
        for o in self.ops:
            for d in o.deps:
                if d.is_dma:
                    continue
                if d.eng == o.eng and not o.is_dma:
                    if d.eng == "pe" or not SAME_ENGINE_SYNC:
                        continue
                d.signal = True
        cnt = {}
        for o in self.ops:
            if o.is_dma:
                continue
            if o.signal:
                k = ("eng", o.eng, o.epoch)
                cnt[k] = cnt.get(k, 0) + 1
                o.ev = (k, cnt[k])
        return cnt

    def emit(self, eng_name, eng, sems):
        known = {}
        for o in self.ops:
            if o.eng != eng_name:
                continue
            for d in o.deps:
                if d.ev is None:
                    continue
                if (not d.is_dma) and d.eng == o.eng and not o.is_dma:
                    if d.eng == "pe" or not SAME_ENGINE_SYNC:
                        continue
                key, val = d.ev
                if known.get(key, 0) >= val:
                    continue
                eng.wait_ge(sems[key], val)
                known[key] = val
            ins = o.fn(eng)
            if o.is_dma:
                ins.then_inc(sems[o.ev[0]], o.inc)
            elif o.signal:
                ins.then_inc(sems[o.ev[0]], 1)

    def final_waits(self, eng, sems):
        for chan, c in self.chan_count.items():
            eng.wait_ge(sems[("dma", chan)], c)


class Builder:
    def __init__(self, n_sub=8, skip_mixers=False, debug=False, skip_kinds=()):
        self.debug = debug
        self.skip_kinds = set(skip_kinds)
        self.n_sub = n_sub
        self.skip_mixers = skip_mixers
        self.S = Sched()
        self.nc = bass.Bass("TRN2", target_bir_lowering=False)
        self.bank_rr = 0
        self.pair_rr = 0
        self.ring_next_load = 0
        self.ring_next_use = 0
        self.ring_released = set()
        self.ring_pending = []
        self.tmp_rr = {}

    def declare(self):
        nc = self.nc
        di = lambda name, shape, dt=F32: nc.dram_tensor(name, list(shape), dt, kind="ExternalInput").ap()
        self.x_in = di("x_in", [1024, D])
        self.cvecT = di("cvecT", [128, 2, 8])
        self.cmask = di("cmask", [128, 16])
        self.consts = di("consts", [128, 9, 128])
        self.state0 = di("state0", [2, 2, NH, DK, DV])
        self.pvec = di("pvec", [128, 320])
        self.w = {}
        for name, shape in WEIGHT_SHAPES.items():
            self.w[name] = di(name, shape)
        self.y_out = nc.dram_tensor("y_out", [1024, D], F32, kind="ExternalOutput").ap()
        self.ns_out = nc.dram_tensor("ns_out", [2, 2, 2, NH, DK, DV], F32, kind="ExternalOutput").ap()
        self.dbg = {}
        if self.debug:
            self.dbg["hT"] = nc.dram_tensor("dbg_hT", [128, 8, 1024], BF16, kind="ExternalOutput").ap()
            self.dbg["rows"] = nc.dram_tensor("dbg_rows", [128, 4, 1024], F32, kind="ExternalOutput").ap()
            self.dbg["acc"] = nc.dram_tensor("dbg_acc", [128, 8, 1024], F32, kind="ExternalOutput").ap()
            self.dbg["gSb"] = nc.dram_tensor("dbg_gSb", [128, 8, 4, 256], BF16, kind="ExternalOutput").ap()
            self.dbg["gyT"] = nc.dram_tensor("dbg_gyT", [128, 8, 1024], BF16, kind="ExternalOutput").ap()
            self.dbg["ccv"] = nc.dram_tensor("dbg_ccv", [128, 8, 1024], F32, kind="ExternalOutput").ap()
            self.dbg["chT"] = nc.dram_tensor("dbg_chT", [128, 8, 1024], BF16, kind="ExternalOutput").ap()
            self.dbg["chin"] = nc.dram_tensor("dbg_chin", [128, 8, 1024], BF16, kind="ExternalOutput").ap()
        self.agg_src = [[nc.dram_tensor(f"agg_src{j}_{z}", [128, 1032], F32) for z in range(2)] for j in range(2)]
        self.agg_dst = [[nc.dram_tensor(f"agg_dst{j}_{z}", [4 * 128, 1032], F32) for z in range(2)] for j in range(2)]
        self.halo_src = nc.dram_tensor("halo_src", [128, 2 * 8 * 64], F32)
        self.halo_dst = nc.dram_tensor("halo_dst", [4 * 128, 2 * 8 * 64], F32)

    def view(self, off_bytes, shape, dt):
        esz = 4 if dt == F32 else 2
        n = int(np.prod(shape[1:]))
        assert off_bytes % 4 == 0
        assert off_bytes + n * esz <= self.SCR_BYTES, (off_bytes, n * esz, self.SCR_BYTES)
        a = self.scr[:, off_bytes // 4: off_bytes // 4 + (n * esz + 3) // 4]
        if dt != F32:
            a = a.bitcast(dt)
        if len(shape) == 2:
            return a
        names = " ".join(f"d{i}" for i in range(1, len(shape)))
        kw = {f"d{i}": shape[i] for i in range(1, len(shape))}
        return a.rearrange(f"p ({names}) -> p {names}", **kw)

    single_pool = list(range(8))
    pair_pool = [0, 2, 4, 6]

    def bank(self):
        self.bank_rr = (self.bank_rr + 1) % len(self.single_pool)
        return self.single_pool[self.bank_rr]

    def pair(self):
        self.pair_rr = (self.pair_rr + 1) % len(self.pair_pool)
        return self.pair_pool[self.pair_rr]

    def psf(self, b, n=1):
        return self.ps[:, b * 512:(b + n) * 512]

    def psb(self, b):
        return self.ps[:, b * 512:(b + 1) * 512].bitcast(BF16)

    def pkeys(self, b, n=1):
        return [("ps", b + i) for i in range(n)]

    def mm(self, out, lhsT, rhs, start, stop, reads, writes):
        return self.S.op("pe", lambda e: e.matmul(out, lhsT, rhs, start=start, stop=stop), reads, writes)

    def tr(self, out, in_, ident, reads, writes):
        return self.S.op("pe", lambda e: e.transpose(out, in_, ident), reads, writes)

    def act(self, out, in_, func, reads, writes, bias=None, scale=None):
        kw = {}
        if bias is not None:
            kw["bias"] = bias
        if scale is not None:
            kw["scale"] = scale
        return self.S.op("act", lambda e: e.activation(out, in_, func, **kw), reads, writes)

    def tt(self, eng, out, in0, in1, op, reads, writes):
        return self.S.op(eng, lambda e: e.tensor_tensor(out, in0, in1, op), reads, writes)

    def ts(self, eng, out, in0, s1, s2, op0, op1, reads, writes):
        if op1 is None:
            return self.S.op(eng, lambda e: e.tensor_scalar(out, in0, s1, None, op0), reads, writes)
        return self.S.op(eng, lambda e: e.tensor_scalar(out, in0, s1, s2, op0, op1), reads, writes)

    def stt(self, eng, out, in0, scalar, in1, op0, op1, reads, writes):
        return self.S.op(eng, lambda e: e.scalar_tensor_tensor(out, in0, scalar, in1, op0, op1), reads, writes)

    def cp(self, eng, out, in_, reads, writes):
        if eng == "act":
            return self.S.op("act", lambda e: e.copy(out, in_), reads, writes)
        return self.S.op(eng, lambda e: e.tensor_copy(out, in_), reads, writes)

    def dma(self, q, out, in_, reads, writes, chan):
        return self.S.op(q, lambda e: e.dma_start(out=out, in_=in_), reads, writes, is_dma=True, chan=chan)

    NSLOT = 3

    def build_stream(self):
        w = self.w
        st = []
        sub = 0
        for l in range(DEPTH):
            for s in range(2):
                if sub >= self.n_sub:
                    break
                sub += 1
                if s == 0 and (self.skip_mixers or (l % 3) in self.skip_kinds):
                    continue
                mod = lambda j: (("mod", l, j), w["mod_w"][l, :, j * 1024:(j + 1) * 1024])
                st += [mod(3 * s), mod(3 * s + 1)]
                core = []
                if s == 0:
                    kind, j = l % 3, l // 3
                    if kind == 0:
                        core += [(("gin", j, i), w["gla_w_in"][j, :, i * 1024:(i + 1) * 1024]) for i in range(3)]
                        core += [(("go", j), w["gla_w_o"][j])]
                    elif kind == 1:
                        core += [(("cpw1", i), w["conf_w_pw1"][0, :, i * 1024:(i + 1) * 1024]) for i in range(2)]
                        core += [(("cpw2",), w["conf_w_pw2"][0])]
                    else:
                        core += [(("sin", i), w["sc_w_in"][0, :, i * 1024:(i + 1) * 1024]) for i in (1, 2, 0)]
                        core += [(("sout",), w["sc_w_out"][0])]
                else:
                    for b in range(4):
                        core += [(("w1", l, b), w["ff_w1"][l, :, b * 1024:(b + 1) * 1024]),
                                 (("w2", l, b), w["ff_w2"][l, b * 1024:(b + 1) * 1024, :])]
                if s == 0 and l % 3 == 0:
                    st += [mod(3 * s + 2)] + core
                else:
                    st += [core[0], mod(3 * s + 2)] + core[1:]
        self.ring_tags = [t for t, _ in st]
        self.ring_pending = [a for _, a in st]

    def ring_take(self, tag):
        idx = self.ring_next_use
        assert self.ring_tags[idx] == tag, (self.ring_tags[idx], tag)
        self.ring_next_use += 1
        if idx == 0:
            self.ring_issue_upto(self.NSLOT - 1)
        assert idx < self.ring_next_load
        slot = idx % self.NSLOT
        return self.ring[:, slot], ("ring", slot), idx

    def ring_release(self, idx):
        self.ring_released.add(idx)
        while (self.ring_next_load < len(self.ring_pending)
               and (self.ring_next_load - self.NSLOT) in self.ring_released):
            self.ring_issue_upto(self.ring_next_load)

    def ring_issue_upto(self, idx):
        while self.ring_next_load <= idx and self.ring_next_load < len(self.ring_pending):
            i = self.ring_next_load
            slot = i % self.NSLOT
            src = self.ring_pending[i].rearrange("(kc p) n -> p kc n", p=128)
            for q in range(4):
                self.dma("pool", self.ring[:, slot, 2 * q:2 * q + 2, :], src[:, 2 * q:2 * q + 2, :],
                         reads=[], writes=[("ring", slot)], chan=("ring", slot))
            self.ring_next_load += 1

    def run(self):
        nc = self.nc
        self.declare()
        with contextlib.ExitStack() as st:
            sb = lambda name, shape, dt: st.enter_context(nc.sbuf_tensor(name, list(shape), dt))
            self.x = sb("x", [128, NT, D], F32)
            self.hT = sb("hT", [128, 8, 1024], BF16)
            self.ring = sb("ring", [128, self.NSLOT, 8, 1024], BF16)
            self.rows = sb("rows", [128, 4, D], F32)
            self.brow = sb("brow", [128, 1, D], F32)
            self.gsm = sb("gsm", [128, 256], F32)
            self.SCR_BYTES = 64 * 1024
            self.scr = sb("scr", [128, self.SCR_BYTES // 4], F32)
            self.cst = sb("cst", [128, 9, 128], F32)
            self.cstb = sb("cstb", [128, 4, 128], BF16)
            self.cT = sb("cT", [128, 2, 8], F32)
            self.sT = sb("sT", [128, 2, 8], F32)
            self.sTrep = sb("sTrep", [128, 2, 8, 128], BF16)
            self.cm = sb("cm", [128, 16], F32)
            self.pv = sb("pv", [128, 320], F32)
            self.hb = sb("hb", [128, 2, D], BF16)
            self.tmpf = sb("tmpf", [128, 2, D], F32)
            self.stat = sb("stat", [128, 2, 16], F32)
            self.bst = sb("bst", [128, 2, 12], F32)
            self.ps = st.enter_context(nc.psum_tensor("ps", [128, 8 * 512], F32))

            self.program()

            cnt = self.S.finalize()
            import os
            if os.environ.get("KDEBUG"):
                print("SIGNAL COUNTS", max(cnt.values()), len(cnt), "n_ops", len(self.S.ops), "chan max", max(self.S.chan_count.values()), "n_chan", len(self.S.chan_count))
            sems = {}
            for k in cnt:
                sems[k] = st.enter_context(nc.semaphore(f"s_{k[1]}_{k[2]}"))
            for i, chan in enumerate(self.S.chan_count):
                sems[("dma", chan)] = st.enter_context(nc.semaphore(f"d_{i}"))
            block = st.enter_context(nc.Block())
            S = self.S

            @block.tensor
            def _(e):
                S.emit("pe", e, sems)

            @block.scalar
            def _(e):
                S.emit("act", e, sems)

            @block.vector
            def _(e):
                S.emit("dve", e, sems)

            @block.gpsimd
            def _(e):
                S.emit("pool", e, sems)
                for (kind, name), ap in []:
                    pass

            @block.sync
            def _(e):
                S.emit("sp", e, sems)
                S.final_waits(e, sems)
                for k, v in cnt.items():
                    e.wait_ge(sems[k], v)
        return nc

    def program(self):
        import os
        if os.environ.get("DMA_PROBE"):
            w = self.w
            nblk = int(os.environ["DMA_PROBE"])
            self.ring_tags = [("p", i) for i in range(nblk)]
            self.ring_pending = [w["ff_w1"][i % 4, :, (i // 4 % 4) * 1024:(i // 4 % 4 + 1) * 1024] for i in range(nblk)]
            for i in range(nblk):
                wv, wk, wi = self.ring_take(("p", i))
                b = self.bank()
                self.mm(self.psf(b)[:, 0:128], wv[:, 7, 0:128], wv[:, 7, 896:1024], True, True, [wk], self.pkeys(b))
                self.ring_release(wi)
            self.cp("act", self.x[:, 0, 0:128], self.psf(b)[:, 0:128], self.pkeys(b), [("x", 0)])
            self.epilogue()
            return
        self.build_stream()
        self.prologue()
        sub = 0
        for l in range(DEPTH):
            for s in range(2):
                if sub >= self.n_sub:
                    break
                if s == 0 and (self.skip_mixers or (l % 3) in self.skip_kinds):
                    sub += 1
                    continue
                self.S.epoch = sub
                self.sublayer(l, s)
                sub += 1
        self.epilogue()

    def prologue(self):
        for t in range(NT):
            self.dma("sp", self.x[:, t, :], self.x_in[t * 128:(t + 1) * 128, :], [], [("x", t)], chan=("x", t))
        self.dma("sp", self.cT[:], self.cvecT, [], [("cT",)], chan="misc0")
        self.dma("sp", self.cm[:], self.cmask, [], [("cm",)], chan="misc1")
        self.dma("sp", self.cst[:], self.consts, [], [("cst",)], chan="misc2")
        self.dma("sp", self.pv[:], self.pvec, [], [("pv",)], chan="misc3")
        self.cp("dve", self.cstb[:, 0:3, :], self.cst[:, 0:3, :], [("cst",)], [("cstb",)])
        self.act(self.sT[:], self.cT[:], AF.Silu, [("cT",)], [("sT",)])
        self.cp("dve", self.sTrep[:].rearrange("p s k m -> p (s k) m"),
                self.sT[:].rearrange("p s k -> p (s k)").unsqueeze(2).to_broadcast([128, 16, 128]),
                [("sT",)], [("sTrep",)])

    def epilogue(self):
        for t in range(NT):
            self.dma("sp", self.y_out[t * 128:(t + 1) * 128, :], self.x[:, t, :], [("x", t)], [("yout", t)],
                     chan=("yo", t % 2))

    def mod_rows(self, l, j, dsts, plus_one):
        wv, wkey, widx = self.ring_take(("mod", l, j))
        bslot = self.tmp_next("brow", 1)
        self.dma("sp", self.brow[:, bslot, :], self.w["mod_b"][l, j * 1024:(j + 1) * 1024].partition_broadcast(128),
                 [], [("brow", bslot)], chan=("brow", bslot))
        for s in range(2):
            b = self.pair()
            for half in range(2):
                for kc in range(8):
                    self.mm(self.psf(b + half), self.sTrep[:, s, kc, :], wv[:, kc, half * 512:(half + 1) * 512],
                            kc == 0, kc == 7, [("sTrep",), wkey], self.pkeys(b + half))
            if plus_one:
                self.stt("dve", self.rows[:, dsts[s], :], self.psf(b, 2), 1.0, self.brow[:, bslot, :], ALU.add, ALU.add,
                         self.pkeys(b, 2) + [("brow", bslot)], [("rows", dsts[s])])
            else:
                self.tt("dve", self.rows[:, dsts[s], :], self.psf(b, 2), self.brow[:, bslot, :], ALU.add,
                        self.pkeys(b, 2) + [("brow", bslot)], [("rows", dsts[s])])
        self.ring_release(widx)

    def tmp_next(self, name, n):
        v = self.tmp_rr.get(name, 0)
        self.tmp_rr[name] = (v + 1) % n
        return v

    def row_load(self, dst_idx, src_row_ap):
        self.dma("sp", self.rows[:, dst_idx, :], src_row_ap.partition_broadcast(128), [], [("rows", dst_idx)],
                 chan=("rows", dst_idx))

    def sublayer(self, l, s):
        self.mod_rows(l, 3 * s + 0, (0, 2), False)
        self.mod_rows(l, 3 * s + 1, (1, 3), True)
        for t in range(NT):
            st_ = 0 if t < 4 else 1
            tb = self.tmp_next("tmpf", 2)
            hbk = self.tmp_next("hb", 2)
            self.tt("dve", self.tmpf[:, tb, :], self.x[:, t, :], self.rows[:, 2 * st_ + 1, :], ALU.mult,
                    [("x", t), ("rows", 2 * st_ + 1)], [("tmpf", tb)])
            self.tt("dve", self.hb[:, hbk, :], self.tmpf[:, tb, :], self.rows[:, 2 * st_, :], ALU.add,
                    [("tmpf", tb), ("rows", 2 * st_)], [("hb", hbk)])
            self.transpose_tile(self.hb[:, hbk, :], ("hb", hbk), self.hT, "hT", t)
        if self.debug and "hT" in self.dbg:
            self.dma("sp", self.dbg.pop("hT"), self.hT[:], [("hT", t) for t in range(NT)], [("dbg", 0)], chan="dbg0")
            self.dma("sp", self.dbg.pop("rows"), self.rows[:], [("rows", i) for i in range(4)], [("dbg", 1)], chan="dbg1")
        self.gate_rows = lambda: self.mod_rows(l, 3 * s + 2, (0, 1), False)
        self.row_load(2, self.w["ln_g"][l, s, :])
        self.row_load(3, self.w["ln_b"][l, s, :])
        self.S.barrier(lambda e: e.memset(self.gsm[:, 255:256], 0.0))
        if s == 0:
            kind = l % 3
            if kind == 0:
                src = self.gla_core(l // 3)
            elif kind == 1:
                src = self.conf_core()
            else:
                src = self.sconv_core()
            early = getattr(self, "early_post", set())
            for t in range(NT):
                if t in early:
                    continue
                ap, keys = src(t)
                self.post_tile(t, ap, keys)
                if t % 4 == 3:
                    self.post_group(range(t - 3, t + 1))
            self.early_post = set()
        else:
            self.mlp_core(l)

    def transpose_tile(self, src, src_key, dstT, dst_name, t):
        b = self.bank()
        pv = self.psb(b)
        for kc in range(8):
            self.tr(pv[:, kc * 128:(kc + 1) * 128], src[:, kc * 128:(kc + 1) * 128], self.cstb[:, 0, :],
                    [src_key, ("cstb",)], self.pkeys(b))
        self.cp("act", dstT[:, :, t * 128:(t + 1) * 128], pv.rearrange("p (k m) -> p k m", k=8),
                self.pkeys(b), [(dst_name, t)])

    def post_tile(self, t, ap, keys):
        mv = self.stat[:, 0, :].rearrange("p (t c) -> p t c", c=2)
        st_ = 0 if t < 4 else 1
        tb = self.tmp_next("tmpf", 2)
        xt = self.x[:, t, :]
        self.tt("dve", self.tmpf[:, tb, :], ap, self.rows[:, st_, :], ALU.mult, keys + [("rows", st_)], [("tmpf", tb)])
        self.stt("dve", xt, xt, float(ALPHA), self.tmpf[:, tb, :], ALU.mult, ALU.add, [("x", t), ("tmpf", tb)], [("x", t)])
        sums = self.bst[:].rearrange("p a b -> p (a b)")[:, 0:16].rearrange("p (t c) -> p t c", c=2)
        junk = self.tmpf[:, tb, :]
        self.S.op("act", lambda e, o=junk, i=xt, a=sums[:, t, 0:1]: e.activation(o, i, AF.Identity, accum_out=a),
                  [("x", t)], [("tmpf", tb), ("sums", t, 0)])
        self.S.op("act", lambda e, o=junk, i=xt, a=sums[:, t, 1:2]: e.activation(o, i, AF.Square, accum_out=a),
                  [("x", t)], [("tmpf", tb), ("sums", t, 1)])

    def post_group(self, tiles):
        tiles = list(tiles)
        t0, t1 = tiles[0], tiles[-1] + 1
        g = t0 // 4
        mv = self.stat[:, 0, :].rearrange("p (t c) -> p t c", c=2)
        aux = self.stat[:, 1, :].rearrange("p (c t) -> p c t", c=2)
        allmv = [("mv", t) for t in tiles]
        sums = self.bst[:].rearrange("p a b -> p (a b)")[:, 0:16].rearrange("p (t c) -> p t c", c=2)
        msq = self.bst[:].rearrange("p a b -> p (a b)")[:, 16:24]
        allsums = [("sums", t, c) for t in tiles for c in range(2)]
        self.ts("dve", mv[:, t0:t1, 0], sums[:, t0:t1, 0], 1.0 / D, None, ALU.mult, None, allsums, allmv)
        self.tt("dve", msq[:, t0:t1], mv[:, t0:t1, 0], mv[:, t0:t1, 0], ALU.mult, allmv, [("msq", g)])
        self.stt("dve", mv[:, t0:t1, 1], sums[:, t0:t1, 1], 1.0 / D, msq[:, t0:t1], ALU.mult, ALU.subtract,
                 allsums + [("msq", g)], allmv)
        self.act(aux[:, 0, t0:t1], mv[:, t0:t1, 1], AF.Sqrt, allmv, [("aux", 0, g)], bias=float(LN_EPS))
        self.S.op("dve", lambda e: e.reciprocal(aux[:, 0, t0:t1], aux[:, 0, t0:t1]), [("aux", 0, g)], [("aux", 0, g)])
        self.stt("dve", aux[:, 1, t0:t1], mv[:, t0:t1, 0], -1.0, aux[:, 0, t0:t1], ALU.mult, ALU.mult, allmv + [("aux", 0, g)], [("aux", 1, g)])
        for t in tiles:
            xt = self.x[:, t, :]
            self.act(xt, xt, AF.Identity, [("x", t), ("aux", 0, g), ("aux", 1, g)], [("x", t)],
                     bias=aux[:, 1, t:t + 1], scale=aux[:, 0, t:t + 1])
            self.tt("dve", xt, xt, self.rows[:, 2, :], ALU.mult, [("x", t), ("rows", 2)], [("x", t)])
            self.tt("dve", xt, xt, self.rows[:, 3, :], ALU.add, [("x", t), ("rows", 3)], [("x", t)])

    def mlp_core(self, l):
        acc = self.view(0, [128, NT, D], F32)
        uT = self.view(32 * 1024, [128, 2, 8, 512], BF16)
        rt = self.view(48 * 1024, [128, 4, 512], F32)
        W1, W2 = {}, {}

        def F(b, half):
            w1, k1, i1 = W1[b]
            for sub in range(8):
                pb = self.bank()
                for kc in range(8):
                    self.mm(self.psf(pb), w1[:, kc, sub * 128:(sub + 1) * 128], self.hT[:, kc, half * 512:(half + 1) * 512],
                            kc == 0, kc == 7, [k1] + [("hT", half * 4 + i) for i in range(4)], self.pkeys(pb))
                ri = self.tmp_next("rt", 4)
                self.act(rt[:, ri, :], self.psf(pb), AF.Relu, self.pkeys(pb), [("rt", ri)])
                if sub % 2 == 0:
                    self.tt("dve", uT[:, half, sub, :], rt[:, ri, :], rt[:, ri, :], ALU.mult, [("rt", ri)], [("uT", half, sub)])
                else:
                    self.act(uT[:, half, sub, :], rt[:, ri, :], AF.Square, [("rt", ri)], [("uT", half, sub)])
            if half == 1:
                self.ring_release(i1)

        def S_(b, half):
            w2, k2, i2 = W2[b]
            for tt_ in range(4):
                t = half * 4 + tt_
                for nh in range(2):
                    pb = self.bank()
                    for sub in range(8):
                        self.mm(self.psf(pb), uT[:, half, sub, tt_ * 128:(tt_ + 1) * 128], w2[:, sub, nh * 512:(nh + 1) * 512],
                                sub == 0, sub == 7, [k2, ("uT", half, sub)], self.pkeys(pb))
                    dst = acc[:, t, nh * 512:(nh + 1) * 512]
                    if b == 0:
                        self.cp("act", dst, self.psf(pb), self.pkeys(pb), [("acc", t, nh)])
                    else:
                        self.tt("dve", dst, dst, self.psf(pb), ALU.add, self.pkeys(pb) + [("acc", t, nh)], [("acc", t, nh)])
                if b == 3:
                    self.post_tile(t, acc[:, t, :], [("acc", t, 0), ("acc", t, 1)])
            if half == 1:
                self.ring_release(i2)
            if b == 3:
                self.post_group(range(half * 4, half * 4 + 4))

        W1[0] = self.ring_take(("w1", l, 0))
        self.gate_rows()
        F(0, 0)
        F(0, 1)
        for b in range(4):
            W2[b] = self.ring_take(("w2", l, b))
            S_(b, 0)
            if b < 3:
                W1[b + 1] = self.ring_take(("w1", l, b + 1))
                F(b + 1, 0)
            S_(b, 1)
            if b < 3:
                F(b + 1, 1)

    def gla_core(self, j):
        K1 = 1024
        qdec = self.view(0, [128, 4, 2, 4, 128], BF16)
        sm = self.view(8 * K1, [128, 4, 2, 4, 128], BF16)
        vtok = self.view(16 * K1, [128, 4, 1024], BF16)
        kend = self.view(24 * K1, [128, 4, 2, 512], BF16)
        Sbst = self.view(32 * K1, [128, 8, 4, 256], BF16)
        qk_tok = self.view(48 * K1, [128, 1024], F32)
        S_f = self.view(48 * K1, [128, 4, 256], F32)
        Lt = self.view(52 * K1, [128, 1024], F32)
        S_b = self.view(52 * K1, [128, 4, 256], F32)
        kdec = self.view(56 * K1, [128, 2, 4, 128], BF16)
        Sfb = self.view(56 * K1, [128, 2, 4, 256], BF16)
        zrT = self.view(60 * K1, [128, 1024], BF16)
        wga = self.view(62 * K1, [128, 8, 64], BF16)
        wgb = self.view(63 * K1, [128, 512], BF16)
        gsm = self.gsm
        dec = gsm[:, 0:64].rearrange("p (t c) -> p t c", t=4)
        Ptot = gsm[:, 64:72]
        Prc = gsm[:, 72:88].rearrange("p (s c) -> p s c", s=2)
        Dm = gsm[:, 88:92]
        ssq = gsm[:, 96:104].rearrange("p (s c) -> p s c", s=2)
        gA, gB, gC = ("gA",), ("gB",), ("gC",)
        self.single_pool, self.pair_pool = [6, 7], [0, 2, 4]
        self.single_pool, self.pair_pool = list(range(8)), [0, 2, 4, 6]
        self.gate_rows()
        self.single_pool, self.pair_pool = [6, 7], [0, 2, 4]
        w0, k0, i0 = self.ring_take(("gin", j, 0))
        w1, k1, i1 = self.ring_take(("gin", j, 1))
        w2, k2, i2 = self.ring_take(("gin", j, 2))
        wsrc = self.w
        self.memset("pool", wga, 0.0, [("wga",)])
        self.memset("pool", zrT[0:64, :], 1.0, [("zrT",)])
        for z in range(2):
            self.dma("pool", wga[:, :, z * 32:z * 32 + 16], wsrc["gla_w_ga"][j, z].rearrange("(kc p) r -> p kc r", p=128),
                     [], [("wga",)], chan="wga")
            self.dma("pool", wgb[z * 32:z * 32 + 16, :], wsrc["gla_w_gb"][j, z], [], [("wgb",)], chan="wgb")
            self.dma("pool", wgb[z * 32 + 16:z * 32 + 17, :], wsrc["gla_b_g"][j, z:z + 1, :], [], [("wgb",)], chan="wgb")
        self.dma("sp", self.brow[:, 0, :], wsrc["gla_gn_g"][j, :].partition_broadcast(128), [], [("brow", 0)], chan=("brow", 0))
        for half in range(2):
            b = self.bank()
            for kc in range(8):
                self.mm(self.psf(b)[0:64, :], wga[:, kc, :], self.hT[:, kc, half * 512:(half + 1) * 512], kc == 0, kc == 7,
                        [("wga",)] + [("hT", half * 4 + i) for i in range(4)], self.pkeys(b))
            for z in range(2):
                self.cp("act", zrT[z * 32:z * 32 + 16, half * 512:(half + 1) * 512], self.psf(b)[z * 32:z * 32 + 16, :],
                        self.pkeys(b), [("zrT",)])

        def proj(t, w, wk, nh):
            pass

        def phase1(tt, t):
            tok = slice(t * 128, (t + 1) * 128)
            pq = self.pair()
            for nh in range(2):
                for kc in range(8):
                    self.mm(self.psf(pq + nh), self.hT[:, kc, tok], w0[:, kc, nh * 512:(nh + 1) * 512], kc == 0, kc == 7,
                            [k0, ("hT", t)], self.pkeys(pq + nh))
            self.cp("act", qk_tok, self.psf(pq, 2), self.pkeys(pq, 2), [gA])
            pv_ = self.pair()
            for nh in range(2):
                for kc in range(8):
                    self.mm(self.psf(pv_ + nh), self.hT[:, kc, tok], w1[:, kc, nh * 512:(nh + 1) * 512], kc == 0, kc == 7,
                            [k1, ("hT", t)], self.pkeys(pv_ + nh))
            self.cp("act", vtok[:, tt, :], self.psf(pv_, 2), self.pkeys(pv_, 2), [("vtok", tt)])
            pz = self.pair()
            for z in range(2):
                self.mm(self.psf(pz + z), zrT[z * 32:z * 32 + 17, tok], wgb[z * 32:z * 32 + 17, :], True, True,
                        [("zrT",), ("wgb",)], self.pkeys(pz + z))
            self.act(Lt, self.psf(pz, 2), AF.Exp, self.pkeys(pz, 2), [gB], scale=-1.0)
            self.act(Lt, Lt, AF.Ln, [gB], [gB], bias=1.0)
            pc = self.pair()
            for z in range(2):
                for h in range(4):
                    c0 = (z * 4 + h) * 128
                    self.mm(self.psf(pc, 2)[:, c0:c0 + 128], Lt[:, z * 512 + h * 128:z * 512 + (h + 1) * 128], self.cst[:, 3 + z, :],
                            True, True, [gB, ("cst",)], self.pkeys(pc + z))
            pss = self.pair()
            for z in range(2):
                self.mm(self.psf(pss + z), self.cst[:, 5 + z, :], Lt[:, z * 512:(z + 1) * 512], True, True,
                        [gB, ("cst",)], self.pkeys(pss + z))
            ptot = self.bank()
            for z in range(2):
                for h in range(4):
                    c0 = (z * 4 + h) * 2
                    self.mm(self.psf(ptot)[:, c0:c0 + 2], Lt[:, z * 512 + h * 128:z * 512 + (h + 1) * 128], self.cst[:, 7, 0:2],
                            True, True, [gB, ("cst",)], self.pkeys(ptot))
            self.act(dec[:, tt, :], self.psf(ptot)[:, 0:16], AF.Exp, self.pkeys(ptot), [("dec", tt)])
            pt = self.pair()
            for idx in range(8):
                self.tr(self.psf(pt, 2)[:, idx * 128:(idx + 1) * 128], qk_tok[:, idx * 128:(idx + 1) * 128], self.cst[:, 0, :],
                        [gA, ("cst",)], self.pkeys(pt + idx // 4))
            e1 = self.tmp_next("tmpf", 2)
            E1 = self.tmpf[:, e1, :]
            self.act(E1, self.psf(pc, 2), AF.Exp, self.pkeys(pc, 2), [("tmpf", e1)])
            self.stt("dve", qdec[:, tt].rearrange("p d h i -> p d (h i)"), E1.rearrange("p (d x) -> p d x", d=2), float(DK ** -0.5),
                     self.psf(pt)[:, 0:512].unsqueeze(1).to_broadcast([128, 2, 512]), ALU.mult, ALU.mult,
                     [("tmpf", e1)] + self.pkeys(pt), [("qdec", tt)])
            e2 = self.tmp_next("tmpf", 2)
            E2 = self.tmpf[:, e2, :]
            self.act(E2, self.psf(pc, 2), AF.Exp, self.pkeys(pc, 2), [("tmpf", e2)], scale=-1.0)
            self.tt("dve", kdec.rearrange("p d h i -> p d (h i)"), E2.rearrange("p (d x) -> p d x", d=2),
                    self.psf(pt + 1)[:, 0:512].unsqueeze(1).to_broadcast([128, 2, 512]), ALU.mult,
                    [("tmpf", e2)] + self.pkeys(pt + 1), [gC])
            e3 = self.tmp_next("tmpf", 2)
            E3 = self.tmpf[:, e3, :]
            self.act(E3, self.psf(pss, 2), AF.Exp, self.pkeys(pss, 2), [("tmpf", e3)])
            self.tt("dve", kend[:, tt], E3.rearrange("p (d x) -> p d x", d=2),
                    qk_tok[:, 512:1024].unsqueeze(1).to_broadcast([128, 2, 512]), ALU.mult,
                    [("tmpf", e3), gA], [("kend", tt)])
            psc = self.pair()
            for z in range(2):
                for h in range(4):
                    c0 = (z * 4 + h) * 128
                    self.mm(self.psf(psc, 2)[:, c0:c0 + 128], kdec[:, z, h, :], qdec[:, tt, z, h, :], True, True,
                            [gC, ("qdec", tt)], self.pkeys(psc + z))
            self.tt("dve", sm[:, tt], self.psf(psc, 2).rearrange("p (d h i) -> p d h i", d=2, h=4),
                    self.cst[:, 1:3, :].unsqueeze(2).to_broadcast([128, 2, 4, 128]), ALU.mult,
                    self.pkeys(psc, 2) + [("cst",)], [("sm", tt)])

        def kv(tt, cc, z):
            pk = self.pair()
            for h in range(4):
                self.mm(self.psf(pk, 2)[:, h * 256:(h + 1) * 256], kend[:, tt, z, h * 128:(h + 1) * 128],
                        vtok[:, tt, h * 256:(h + 1) * 256], True, True, [("kend", tt), ("vtok", tt)], self.pkeys(pk + h // 2))
            return pk

        def step(S, skey, tt, cc, z):
            pk = kv(tt, cc, z)
            for h in range(4):
                ci = (z * 4 + h) * 2
                self.stt("dve", S[:, h, :], S[:, h, :], dec[:, tt, ci:ci + 1], self.psf(pk, 2)[:, h * 256:(h + 1) * 256],
                         ALU.mult, ALU.add, [skey, ("dec", tt)] + self.pkeys(pk + h // 2), [skey])

        def dec4(tt, cc, z):
            return dec[:, tt, :].rearrange("p (d h c) -> p d h c", d=2, h=4)[:, z, :, 0]

        def phase2(tiles, seq):
            T = len(tiles)
            C = T
            sample = seq is None
            Sf2 = S_f.rearrange("p h e -> p (h e)")
            Sb2 = S_b.rearrange("p h e -> p (h e)")
            if sample:
                self.memset("dve", Sf2, 0.0, [gA])
                self.memset("dve", Sb2, 0.0, [gB])
                self.memset("dve", Ptot, 1.0, [("Ptot",)])
                for c in range(C):
                    step(S_f, gA, c, 0, 0)
                    self.tt("dve", Ptot[:, 0:4], Ptot[:, 0:4], dec4(c, 0, 0), ALU.mult, [("Ptot",), ("dec", c)], [("Ptot",)])
                for c in reversed(range(C)):
                    step(S_b, gB, c, 0, 1)
                    self.tt("dve", Ptot[:, 4:8], Ptot[:, 4:8], dec4(c, 0, 1), ALU.mult, [("Ptot",), ("dec", c)], [("Ptot",)])
                dsts_ = []
                for z, S2z, skz in ((0, Sf2, gA), (1, Sb2, gB)):
                    src_ = self.agg_src[j][z].ap()
                    self.dma("sp", src_[:, 0:1024], S2z, [skz], [("aggs", z, 0)], chan=("ag0", z, 0))
                    self.dma("sp", src_[:, 1024:1032], Ptot, [("Ptot",)], [("aggs", z, 1)], chan=("ag0", z, 1))
                    self.S.op("pool", lambda e, z=z: e.collective_compute("AllGather", ALU.bypass, replica_groups=[[0, 1, 2, 3], [4, 5, 6, 7]],
                                                                          ins=[self.agg_src[j][z].ap()], outs=[self.agg_dst[j][z].ap()]),
                              [("aggs", z, 0), ("aggs", z, 1)], [("aggd", z)], is_dma=True, chan=("gcc", j, z), inc=1)
                    dsts_.append(self.agg_dst[j][z].ap().rearrange("(r p) c -> r p c", p=128))
                mid_exchange()
                st0 = self.state0
                self.dma("sp", S_f, st0[j, 0].rearrange("h d e -> d h e"), [], [gA], chan=("st0", 0))
                self.dma("sp", S_b, st0[j, 1].rearrange("h d e -> d h e"), [], [gB], chan=("st0", 1))
                for z, S2, S3, skey, order in ((0, Sf2, S_f, gA, range(4)), (1, Sb2, S_b, gB, reversed(range(4)))):
                    for i in order:
                        mcol = self.cm[:, z * 4 + i:z * 4 + i + 1]
                        ab = self.tmp_next("tmpf", 2)
                        ps_ = self.tmp_next("Prc", 2)
                        self.dma("sp", self.tmpf[:, ab, :], dsts_[z][i, :, 0:1024], [("aggd", z)], [("tmpf", ab)], chan=("agl", ab))
                        self.dma("sp", Prc[:, ps_, :], dsts_[z][i, :, 1024:1032], [("aggd", z)], [("Prc", ps_)], chan=("prl", ps_))
                        self.ts("dve", Dm, Prc[:, ps_, z * 4:(z + 1) * 4], 1.0, mcol, ALU.subtract, ALU.mult, [("Prc", ps_), ("cm",)], [("Dm",)])
                        self.ts("dve", Dm, Dm, 1.0, None, ALU.add, None, [("Dm",)], [("Dm",)])
                        for h in range(4):
                            self.ts("dve", S3[:, h, :], S3[:, h, :], Dm[:, h:h + 1], None, ALU.mult, None, [skey, ("Dm",)], [skey])
                        self.stt("dve", S2, self.tmpf[:, ab, :], mcol, S2, ALU.mult, ALU.add, [("tmpf", ab), ("cm",), skey], [skey])
            else:
                self.memset("dve", Sf2, 0.0, [gA])
                self.memset("dve", Sb2, 0.0, [gB])
            for c in reversed(range(C)):
                self.cp("act", Sbst[:, c].rearrange("p h e -> p (h e)"), Sb2, [gB], [("Sbst", c)])
                step(S_b, gB, c, 0, 1)
            for tt, t in enumerate(tiles):
                tok = slice(t * 128, (t + 1) * 128)
                self.cp("act", Sfb[:, 0].rearrange("p h e -> p (h e)"), Sf2, [gA], [gC])
                step(S_f, gA, tt, 0, 0)
                po = self.pair()
                for h in range(4):
                    ov = self.psf(po, 2)[:, h * 256:(h + 1) * 256]
                    vh = vtok[:, tt, h * 256:(h + 1) * 256]
                    wk_ = self.pkeys(po + h // 2)
                    self.mm(ov, sm[:, tt, 0, h, :], vh, True, False, [("sm", tt), ("vtok", tt)], wk_)
                    self.mm(ov, sm[:, tt, 1, h, :], vh, False, False, [("sm", tt), ("vtok", tt)], wk_)
                    self.mm(ov, qdec[:, tt, 0, h, :], Sfb[:, 0, h, :], False, False, [("qdec", tt), gC], wk_)
                    self.mm(ov, qdec[:, tt, 1, h, :], Sbst[:, tt, h, :], False, True, [("qdec", tt), ("Sbst", tt)], wk_)
                pr = self.pair()
                for nh in range(2):
                    for kc in range(8):
                        self.mm(self.psf(pr + nh), self.hT[:, kc, tok], w2[:, kc, nh * 512:(nh + 1) * 512], kc == 0, kc == 7,
                                [k2, ("hT", t)], self.pkeys(pr + nh))
                rb = self.tmp_next("tmpf", 2)
                G2 = self.tmpf[:, rb, :]
                self.act(G2, self.psf(pr, 2), AF.Silu, self.pkeys(pr, 2), [("tmpf", rb)])
                self.tt("dve", G2, G2, self.brow[:, 0, :], ALU.mult, [("tmpf", rb), ("brow", 0)], [("tmpf", rb)])
                si = self.tmp_next("ssq", 2)
                jb = self.tmp_next("tmpf", 2)
                for h in range(4):
                    self.S.op("act", lambda e, o=self.tmpf[:, jb, h * 256:(h + 1) * 256], i=self.psf(po, 2)[:, h * 256:(h + 1) * 256],
                              a=ssq[:, si, h:h + 1]: e.activation(o, i, AF.Square, accum_out=a),
                              self.pkeys(po + h // 2), [("tmpf", jb), ("ssq", si)])
                self.act(ssq[:, si, :], ssq[:, si, :], AF.Sqrt, [("ssq", si)], [("ssq", si)], bias=float(RMS_EPS), scale=1.0 / DV)
                self.S.op("dve", lambda e, o=ssq[:, si, :]: e.reciprocal(o, o), [("ssq", si)], [("ssq", si)])
                hbk = self.tmp_next("hb", 2)
                for h in range(4):
                    self.stt("dve", self.hb[:, hbk, h * 256:(h + 1) * 256], self.psf(po, 2)[:, h * 256:(h + 1) * 256], ssq[:, si, h:h + 1],
                             G2[:, h * 256:(h + 1) * 256], ALU.mult, ALU.mult,
                             self.pkeys(po + h // 2) + [("ssq", si), ("tmpf", rb)], [("hb", hbk)])
                self.transpose_tile(self.hb[:, hbk, :], ("hb", hbk), self.hT, "hT", t)
            if sample and self.debug and "gSb" in self.dbg:
                self.dma("sp", self.dbg.pop("gSb"), Sbst, [("Sbst", c) for c in range(8)], [("dbg", 9)], chan="dbg9")
            if not sample:
                self.dma("sp", self.ns_out[seq, j, 0].rearrange("h d e -> d h e"), S_f, [gA], [("nso", seq, 0)], chan=("nso", 0))
                self.dma("sp", self.ns_out[seq, j, 1].rearrange("h d e -> d h e"), S_b, [gB], [("nso", seq, 1)], chan=("nso", 1))

        segs = [([0, 1], 0), ([2, 3], 1), ([4, 5, 6, 7], None)]
        go_blk = []
        self.early_post = set()

        def wo_tile(t):
            wo, ko, io = go_blk[0]
            b = self.pair()
            for nh in range(2):
                for kc in range(8):
                    self.mm(self.psf(b + nh), self.hT[:, kc, t * 128:(t + 1) * 128], wo[:, kc, nh * 512:(nh + 1) * 512],
                            kc == 0, kc == 7, [ko, ("hT", t)], self.pkeys(b + nh))
            if t == NT - 1:
                self.ring_release(io)
            return self.psf(b, 2), self.pkeys(b, 2)

        def mid_exchange():
            if not go_blk:
                return
            for t in range(4):
                ap, keys = wo_tile(t)
                self.post_tile(t, ap, keys)
                self.early_post.add(t)
            self.post_group(range(4))
        import os
        lvl = int(os.environ.get("GLA_STOP", "9"))
        if lvl == 1:
            segs = []
        elif lvl == 2:
            phase1(0, 0)
            segs = []
        elif lvl == 3:
            segs = segs[:1]
        elif lvl == 4:
            segs = segs[:2]
        for si_, (tiles, seq) in enumerate(segs):
            for tt, t in enumerate(tiles):
                phase1(tt, t)
            if si_ == len(segs) - 1:
                self.ring_release(i0)
                self.ring_release(i1)
                if len(segs) == 3:
                    go_blk.append(self.ring_take(("go", j)))
            phase2(tiles, seq)
        if not segs:
            self.ring_release(i0)
            self.ring_release(i1)
        self.ring_release(i2)
        if self.debug and "gyT" in self.dbg:
            self.dma("sp", self.dbg.pop("gyT"), self.hT[:], [("hT", t) for t in range(NT)], [("dbg", 8)], chan="dbg8")
        if not go_blk:
            go_blk.append(self.ring_take(("go", j)))
        self.single_pool, self.pair_pool = list(range(8)), [0, 2, 4, 6]
        return wo_tile

    def memset(self, eng, ap, val, writes):
        return self.S.op(eng, lambda e: e.memset(ap, val), [], writes)

    def conf_core(self):
        pv = self.pv
        cv = self.view(0, [128, 8, 1024], F32)
        upad = self.view(32 * 1024, [128, 2, 1324], BF16)
        dg = self.view(38 * 1024, [128, 8, 128], BF16)
        sig = self.view(44 * 1024, [128, 2, 512], F32)
        sq = self.view(48 * 1024, [128, 2, 512], F32)
        mrow = self.view(52 * 1024, [128, 2, 512], F32)
        rrow = self.view(56 * 1024, [128, 2, 512], F32)
        vtmp = self.view(60 * 1024, [128, 512], F32)
        wa, ka, ia = self.ring_take(("cpw1", 0))
        self.gate_rows()
        wg, kg, ig = self.ring_take(("cpw1", 1))
        for ub in range(2):
            self.memset("pool", upad[:, ub, :], 0.0, [("upad", ub)])
        hkeys = lambda half: [("hT", half * 4 + i) for i in range(4)]

        def uview(ub, half, off, L):
            if half == 0:
                return upad[:, ub, 0:572].rearrange("p (s w) -> p s w", s=2)[:, :, off:off + L]
            return upad[:, ub, 572:1324].rearrange("p (s w) -> p s w", s=8)[:, :, off:off + L]

        def glu(j):
            ub = j % 2
            for half in range(2):
                pa = self.bank()
                for kc in range(8):
                    self.mm(self.psf(pa), wa[:, kc, j * 128:(j + 1) * 128], self.hT[:, kc, half * 512:(half + 1) * 512],
                            kc == 0, kc == 7, [ka] + hkeys(half), self.pkeys(pa))
                pg = self.bank()
                for kc in range(8):
                    self.mm(self.psf(pg), wg[:, kc, j * 128:(j + 1) * 128], self.hT[:, kc, half * 512:(half + 1) * 512],
                            kc == 0, kc == 7, [kg] + hkeys(half), self.pkeys(pg))
                si = self.tmp_next("sig", 2)
                self.act(sig[:, si, :], self.psf(pg), AF.Sigmoid, self.pkeys(pg) + [("pv",)], [("sig", si)], bias=pv[:, 8 + j:9 + j])
                s_ = 2 if half == 0 else 8
                self.stt("dve", uview(ub, half, 15, 512 // s_), self.psf(pa).rearrange("p (s w) -> p s w", s=s_), pv[:, j:j + 1],
                         sig[:, si, :].rearrange("p (s w) -> p s w", s=s_), ALU.add, ALU.mult,
                         self.pkeys(pa) + [("sig", si), ("pv",), ("upad", ub)], [("upad", ub, half)])

        def conv(j):
            ub = j % 2
            pcs = [self.bank(), self.bank()]
            for k in range(CONF_W):
                di = self.tmp_next("dg", 8)
                wcol = pv[:, 16 + k * 8 + j:17 + k * 8 + j]
                if k % 2 == 0:
                    self.ts("dve", dg[:, di, :], self.cstb[:, 0, :], wcol, None, ALU.mult, None, [("cstb",), ("pv",)], [("dg", di)])
                else:
                    self.act(dg[:, di, :], self.cstb[:, 0, :], AF.Identity, [("cstb",), ("pv",)], [("dg", di)], scale=wcol)
                for half in range(2):
                    self.mm(self.psf(pcs[half]), dg[:, di, :], uview(ub, half, k, 256 if half == 0 else 64), k == 0, k == CONF_W - 1,
                            [("dg", di), ("upad", ub, half), ("upad", ub)], self.pkeys(pcs[half]))
            for half in range(2):
                self.act(cv[:, j, half * 512:(half + 1) * 512], self.psf(pcs[half]), AF.Identity, self.pkeys(pcs[half]) + [("pv",)],
                         [("cv", j, half)], bias=pv[:, 264 + j:265 + j])

        for j in range(8):
            glu(j)
            if j > 0:
                conv(j - 1)
        self.ring_release(ia)
        self.ring_release(ig)
        conv(7)
        if self.debug:
            self.dma("sp", self.dbg["ccv"], cv, [("cv", j, h) for j in range(8) for h in range(2)], [("dbg", 5)], chan="dbg5")
            self.dma("sp", self.dbg["chin"], self.hT[:], [("hT", t) for t in range(NT)], [("dbg", 7)], chan="dbg7")
        ones = self.cst[:, 8, :]
        for half in range(2):
            b1 = self.bank()
            for j in range(8):
                self.mm(self.psf(b1), ones, cv[:, j, half * 512:(half + 1) * 512], j == 0, j == 7,
                        [("cst",), ("cv", j, half)], self.pkeys(b1))
            b2 = self.bank()
            for j in range(8):
                qi = self.tmp_next("sq", 2)
                self.act(sq[:, qi, :], cv[:, j, half * 512:(half + 1) * 512], AF.Square, [("cv", j, half)], [("sq", qi)])
                self.mm(self.psf(b2), ones, sq[:, qi, :], j == 0, j == 7, [("cst",), ("sq", qi)], self.pkeys(b2))
            mr, rr = mrow[:, half, :], rrow[:, half, :]
            self.act(mr, self.psf(b1), AF.Identity, self.pkeys(b1), [("mrow", half)], scale=1.0 / D)
            self.tt("dve", vtmp, mr, mr, ALU.mult, [("mrow", half)], [("vtmp",)])
            self.stt("dve", vtmp, self.psf(b2), 1.0 / D, vtmp, ALU.mult, ALU.subtract, self.pkeys(b2) + [("vtmp",)], [("vtmp",)])
            self.act(rr, vtmp, AF.Sqrt, [("vtmp",)], [("rrow", half)], bias=float(LN_EPS))
            self.S.op("dve", lambda e, o=rr: e.reciprocal(o, o), [("rrow", half)], [("rrow", half)])
            self.stt("dve", mr, mr, -1.0, rr, ALU.mult, ALU.mult, [("mrow", half), ("rrow", half)], [("mrow", half)])
            for j in range(8):
                c_ = cv[:, j, half * 512:(half + 1) * 512]
                self.tt("dve", c_, c_, rr, ALU.mult, [("cv", j, half), ("rrow", half)], [("cv", j, half)])
                self.tt("dve", c_, c_, mr, ALU.add, [("cv", j, half), ("mrow", half)], [("cv", j, half)])
                self.act(self.hT[:, j, half * 512:(half + 1) * 512], c_, AF.Silu,
                         [("cv", j, half), ("pv",)], [("hT", half * 4 + i) for i in range(4)],
                         bias=pv[:, 280 + j:281 + j], scale=pv[:, 272 + j:273 + j])
        if self.debug:
            self.dma("sp", self.dbg["chT"], self.hT[:], [("hT", t) for t in range(NT)], [("dbg", 6)], chan="dbg6")
        w2, k2, i2 = self.ring_take(("cpw2",))
        brow2 = self.view(0, [128, D], F32)
        self.dma("sp", brow2, self.w["conf_b_pw2"][0, :].partition_broadcast(128),
                 [], [("cv", j, h) for j in range(8) for h in range(2)], chan="cb2")

        def src(t):
            b = self.pair()
            for nh in range(2):
                for kc in range(8):
                    self.mm(self.psf(b + nh), self.hT[:, kc, t * 128:(t + 1) * 128], w2[:, kc, nh * 512:(nh + 1) * 512],
                            kc == 0, kc == 7, [k2, ("hT", t)], self.pkeys(b + nh))
            if t == NT - 1:
                self.ring_release(i2)
            tb = self.tmp_next("tmpf", 2)
            self.tt("dve", self.tmpf[:, tb, :], self.psf(b, 2), brow2, ALU.add,
                    self.pkeys(b, 2) + [("cv", 0, 0)], [("tmpf", tb)])
            return self.tmpf[:, tb, :], [("tmpf", tb)]
        return src

    def sconv_core(self):
        pv = self.pv
        Pp = self.view(0, [128, 8, 2, 258], F32)
        Ps = self.view(16512, [128, 8, 640], F32)
        vT = self.view(36992, [128, 8, 1024], BF16)
        hst = self.view(53376, [128, 2, 8, 64], F32)
        hrc = self.view(57472, [128, 2, 8, 64], F32)
        ctmp = self.view(61568, [128, 512], F32)
        utmp = self.view(63616, [128, 480], F32)
        utmp = self.tmpf
        hkeys = lambda half: [("hT", half * 4 + i) for i in range(4)]
        wc, kc_, ic = self.ring_take(("sin", 1))
        self.gate_rows()
        wu, ku, iu = self.ring_take(("sin", 2))
        self.memset("pool", Pp, 0.0, [("Pp", j) for j in range(8)])
        self.memset("pool", Ps[:, :, 0:64], 0.0, [("halo", 0)])
        self.memset("pool", Ps[:, :, 576:640], 0.0, [("halo", 1)])
        for j in range(8):
            for half in range(2):
                pc = self.bank()
                for kc in range(8):
                    self.mm(self.psf(pc), wc[:, kc, j * 128:(j + 1) * 128], self.hT[:, kc, half * 512:(half + 1) * 512],
                            kc == 0, kc == 7, [kc_] + hkeys(half), self.pkeys(pc))
                pu = self.bank()
                for kc in range(8):
                    self.mm(self.psf(pu), wu[:, kc, j * 128:(j + 1) * 128], self.hT[:, kc, half * 512:(half + 1) * 512],
                            kc == 0, kc == 7, [ku] + hkeys(half), self.pkeys(pu))
                tb = self.tmp_next("tmpf", 2)
                self.cp("act", utmp[:, tb, 0:512], self.psf(pu), self.pkeys(pu), [("tmpf", tb)])
                if half == 0:
                    self.tt("dve", Pp[:, j, :, 1:257], self.psf(pc).rearrange("p (s w) -> p s w", s=2),
                            utmp[:, tb, 0:512].rearrange("p (s w) -> p s w", s=2), ALU.mult,
                            self.pkeys(pc) + [("tmpf", tb)], [("Pp", j)])
                else:
                    self.tt("dve", Ps[:, j, 64:576], self.psf(pc), utmp[:, tb, 0:512], ALU.mult,
                            self.pkeys(pc) + [("tmpf", tb)], [("Ps", j)])
        self.ring_release(ic)
        self.ring_release(iu)
        wb, kb, ib = self.ring_take(("sin", 0))
        allPs = [("Ps", j) for j in range(8)]
        self.cp("pool", hst[:, 0], Ps[:, :, 64:128], allPs, [("hst",)])
        self.cp("pool", hst[:, 1], Ps[:, :, 512:576], allPs, [("hst",)])
        self.dma("sp", self.halo_src.ap(), hst.rearrange("p a j w -> p (a j w)"), [("hst",)], [("halo_src",)], chan="hs")
        self.S.op("pool", lambda e: e.collective_compute("AllGather", ALU.bypass, replica_groups=[[0, 1, 2, 3], [4, 5, 6, 7]],
                                                         ins=[self.halo_src.ap()], outs=[self.halo_dst.ap()]),
                  [("halo_src",)], [("halo_dst",)], is_dma=True, chan="hcc", inc=1)
        def recv_halo():
            hd = self.halo_dst.ap().rearrange("(r p) (a j w) -> r p a j w", p=128, a=2, j=8)
            for i in range(4):
                for side, a_idx, mcol, dst in ((0, 1, 8 + i, Ps[:, :, 0:64]), (1, 0, 12 + i, Ps[:, :, 576:640])):
                    ri = self.tmp_next("hrc", 2)
                    self.dma("sp", hrc[:, ri], hd[i, :, a_idx], [("halo_dst",)], [("hrc", ri)], chan=("hrc", ri))
                    self.stt("dve", dst, hrc[:, ri], self.cm[:, mcol:mcol + 1], dst, ALU.mult, ALU.add,
                             [("hrc", ri), ("cm",), ("halo", side)], [("halo", side)])
        for half in range(2):
            if half == 1:
                recv_halo()
            for j in range(8):
                pb = self.bank()
                for kc in range(8):
                    self.mm(self.psf(pb), wb[:, kc, j * 128:(j + 1) * 128], self.hT[:, kc, half * 512:(half + 1) * 512],
                            kc == 0, kc == 7, [kb] + hkeys(half), self.pkeys(pb))
                w_ = lambda k: pv[:, 288 + k * 8 + j:289 + k * 8 + j]
                if half == 0:
                    cview = ctmp.rearrange("p (s w) -> p s w", s=2)
                    srcs = [Pp[:, j, :, k:k + 256] for k in range(3)]
                    rk = [("Pp", j), ("pv",)]
                    pbv = self.psf(pb).rearrange("p (s w) -> p s w", s=2)
                    outv = vT[:, j, 0:512].rearrange("p (s w) -> p s w", s=2)
                else:
                    cview = ctmp
                    srcs = [Ps[:, j, 64 * k:64 * k + 512] for k in range(3)]
                    rk = [("Ps", j), ("halo", 0), ("halo", 1), ("pv",)]
                    pbv = self.psf(pb)
                    outv = vT[:, j, 512:1024]
                self.ts("dve", cview, srcs[0], w_(0), None, ALU.mult, None, rk, [("ctmp",)])
                self.stt("dve", cview, srcs[1], w_(1), cview, ALU.mult, ALU.add, rk + [("ctmp",)], [("ctmp",)])
                self.stt("dve", cview, srcs[2], w_(2), cview, ALU.mult, ALU.add, rk + [("ctmp",)], [("ctmp",)])
                self.tt("dve", outv, pbv, cview, ALU.mult, self.pkeys(pb) + [("ctmp",)], [("vT", j, half)])
        self.ring_release(ib)
        wo, ko, io = self.ring_take(("sout",))

        def src(t):
            b = self.pair()
            for nh in range(2):
                for kc in range(8):
                    self.mm(self.psf(b + nh), vT[:, kc, t * 128:(t + 1) * 128], wo[:, kc, nh * 512:(nh + 1) * 512],
                            kc == 0, kc == 7, [ko, ("vT", kc, t // 4)], self.pkeys(b + nh))
            if t == NT - 1:
                self.ring_release(io)
            return self.psf(b, 2), self.pkeys(b, 2)
        return src


WEIGHT_SHAPES = {
    "mod_w": (4, 1024, 6144), "mod_b": (4, 6144), "ln_g": (4, 2, 1024), "ln_b": (4, 2, 1024),
    "ff_w1": (4, 1024, 4096), "ff_w2": (4, 4096, 1024),
    "gla_w_in": (2, 1024, 3072), "gla_w_ga": (2, 2, 1024, 16), "gla_w_gb": (2, 2, 16, 512),
    "gla_b_g": (2, 2, 512), "gla_gn_g": (2, 1024), "gla_w_o": (2, 1024, 1024),
    "conf_w_pw1": (1, 1024, 2048), "conf_w_pw2": (1, 1024, 1024), "conf_b_pw2": (1, 1024),
    "sc_w_in": (1, 1024, 3072), "sc_w_out": (1, 1024, 1024),
}


def make_consts():
    j = np.arange(128)[:, None]
    i = np.arange(128)[None, :]
    same = (j // 128) == (i // 128)
    c = np.zeros((128, 9, 128), np.float32)
    c[:, 8, :] = 1.0
    c[:, 0, :] = np.eye(128, dtype=np.float32)
    c[:, 1, :] = (same & (j <= i)).astype(np.float32)
    c[:, 2, :] = (same & (j >= i)).astype(np.float32)
    c[:, 3, :] = (same & (j <= i)).astype(np.float32) * (-1.0 / 16.0)
    c[:, 4, :] = (same & (j >= i)).astype(np.float32) * (-1.0 / 16.0)
    c[:, 5, :] = (same & (j > i)).astype(np.float32) * (-1.0 / 16.0)
    c[:, 6, :] = (same & (j < i)).astype(np.float32) * (-1.0 / 16.0)
    c[:, 7, 0] = -1.0 / 16.0
    c[:, 7, 1] = 0.0
    return c


def make_in_maps(inp):
    f = lambda a: np.ascontiguousarray(np.asarray(a, dtype=np.float32))
    xp = f(inp["x_prompt"])
    xs = f(inp["x_sample"])
    c = f(inp["c"])
    cctx = f(inp["c_ctx"])
    st = f(inp["state_gla"])
    consts = make_consts()
    fm = lambda v: np.ascontiguousarray(v.reshape(-1, 128).T)
    pvec = np.zeros((128, 320), np.float32)
    pvec[:, 0:16] = fm(f(inp["conf_b_pw1"])[0])
    wdw = f(inp["conf_w_dw"])[0]
    pvec[:, 16:16 + 248] = np.concatenate([fm(wdw[k]) for k in range(CONF_W)], axis=1)
    pvec[:, 264:272] = fm(f(inp["conf_b_dw"])[0])
    pvec[:, 272:280] = fm(f(inp["conf_ln_g"])[0])
    pvec[:, 280:288] = fm(f(inp["conf_ln_b"])[0])
    wsc = f(inp["sc_w_conv"])[0]
    pvec[:, 288:312] = np.concatenate([fm(wsc[k]) for k in range(3)], axis=1)
    shared = {name: f(inp[name]) for name in WEIGHT_SHAPES}
    maps = []
    for r in range(8):
        b, p = r // 4, r % 4
        x_in = np.concatenate([xp[2 * r], xp[2 * r + 1], xs[b, 512 * p:512 * (p + 1)]], axis=0)
        cv = np.stack([cctx, c[b]], axis=0)
        cvecT = np.ascontiguousarray(cv.reshape(2, 8, 128).transpose(2, 0, 1))
        cm = np.zeros((128, 16), np.float32)
        for i in range(4):
            cm[:, i] = 1.0 if i < p else 0.0
            cm[:, 4 + i] = 1.0 if i > p else 0.0
            cm[:, 8 + i] = 1.0 if i == p - 1 else 0.0
            cm[:, 12 + i] = 1.0 if i == p + 1 else 0.0
        m = dict(shared)
        m.update({"x_in": np.ascontiguousarray(x_in), "cvecT": cvecT, "cmask": cm, "consts": consts,
                  "state0": np.ascontiguousarray(st[b]), "pvec": pvec})
        maps.append(m)
    return maps


_NC_CACHE = {}


def run(inp, n_sub=8, skip_mixers=False, trace=False, debug=False, skip_kinds=()):
    key = (n_sub, skip_mixers, debug, tuple(skip_kinds))
    if key not in _NC_CACHE:
        _NC_CACHE[key] = Builder(n_sub, skip_mixers, debug, skip_kinds).run()
    nc = _NC_CACHE[key]
    maps = make_in_maps(inp)
    res = run_bass_kernel_spmd(nc, maps, core_ids=list(range(8)), **({"trace": True} if trace else {}))
    yp = np.zeros((16, 256, D), np.float32)
    ys = np.zeros((2, 2048, D), np.float32)
    ns = np.zeros((16, 2, 2, NH, DK, DV), np.float32)
    for r in range(8):
        b, p = r // 4, r % 4
        y = res.results[r]["y_out"]
        yp[2 * r] = y[0:256]
        yp[2 * r + 1] = y[256:512]
        ys[b, 512 * p:512 * (p + 1)] = y[512:1024]
        ns[2 * r:2 * r + 2] = res.results[r]["ns_out"]
    return (yp, ys, ns), res


def kernel(**inputs):
    outs, _ = run(inputs)
    return outs
```
